# Optimizing a Trainium2 kernel written in Bass

```python
import math
import jax, jax.numpy as jnp
from jax import lax
import numpy as np

D_MODEL = 2048
BATCH = 16
SEQ = 256
DEPTH = 2
DEC_BATCH = 4
DEC_SEQ = 4096
PAST_LEN = 256

GRID_W = 64
N_EVEN = (DEPTH + 1) // 2
N_ODD = DEPTH // 2
NORM_EPS = 1e-6
H_A = 4
DK_A = 128
DV_A = 256
GATE_RANK = 16
GATE_TEMP = 16.0
H_B = 4
DK_B = 128
DV_B = 256
CHUNK = 64
H_C = 16
Q_RANK = 512
KV_RANK = 512
NOPE_DIM = 128
ROPE_DIM = 64
V_DIM = 128
ROPE_BASE = 10000.0
Q_BLOCK = 128
MLA_SCALE = (NOPE_DIM + ROPE_DIM) ** -0.5
MLA_IN = Q_RANK + KV_RANK + ROPE_DIM
D_FF = 5632
CONV_W = 3
AB_SPLITS = (H_A * DK_A, H_A * DK_A, H_A * DV_A, H_A * DV_A, 2 * GATE_RANK, H_B * DK_B, H_B * DK_B, H_B * DV_B, H_B * DV_B)
AB_IN = 2 * H_A * DK_A + 2 * H_A * DV_A + 2 * GATE_RANK + 2 * H_B * DK_B + 2 * H_B * DV_B
AB_MIX = H_A * DV_A + H_B * DV_B

kernel_name = 'bidir_gla_retnet_mla_convffn_ctxprefix_step'


def _split(z, sizes):
    idx = []
    s = 0
    for n in sizes[:-1]:
        s += n
        idx.append(s)
    return jnp.split(z, idx, axis=-1)


def rms_norm(x, g):
    x32 = x.astype(jnp.float32)
    y = x32 * lax.rsqrt(jnp.mean(x32 * x32, axis=-1, keepdims=True) + NORM_EPS)
    return (y * g.astype(jnp.float32)).astype(x.dtype)


def head_norm(o, g):
    H, dv = o.shape[-2], o.shape[-1]
    mu = jnp.mean(o, axis=-1, keepdims=True)
    var = jnp.mean(jnp.square(o - mu), axis=-1, keepdims=True)
    return (o - mu) * lax.rsqrt(var + NORM_EPS) * g.astype(jnp.float32).reshape(H, dv)


def modulation(cond, w, b):
    m = jax.nn.silu(cond) @ w + b
    return jnp.split(m[:, None, :], 6, axis=-1)


def _chunks(a, n):
    B, T, H, d = a.shape
    return jnp.moveaxis(a.astype(jnp.float32).reshape(B, n, CHUNK, H, d), 1, 0)


def gla_scan(q, k, v, log_a, s0):
    B, T, H, _ = q.shape
    dv = v.shape[-1]
    n = T // CHUNK
    causal = jnp.tril(jnp.ones((CHUNK, CHUNK), dtype=bool))

    def step(s, inp):
        qc, kc, vc, gc = inp
        b = jnp.cumsum(gc, axis=1)
        b_end = b[:, -1]
        q_dec = qc * jnp.exp(b)
        k_inv = kc * jnp.exp(-b)
        att = jnp.where(causal, jnp.einsum('bihd,bjhd->bhij', q_dec, k_inv), 0.0)
        o = jnp.einsum('bhij,bjhe->bihe', att, vc) + jnp.einsum('bihd,bhde->bihe', q_dec, s)
        k_end = kc * jnp.exp(b_end[:, None] - b)
        s = jnp.exp(b_end)[..., None] * s + jnp.einsum('bjhd,bjhe->bhde', k_end, vc)
        return s, o

    s_fin, o = lax.scan(step, s0.astype(jnp.float32), (_chunks(q, n), _chunks(k, n), _chunks(v, n), _chunks(log_a, n)))
    return jnp.moveaxis(o, 0, 1).reshape(B, T, H, dv), s_fin


def retention_scan(q, k, v, log_gamma, s0):
    B, T, H, _ = q.shape
    dv = v.shape[-1]
    n = T // CHUNK
    causal = jnp.tril(jnp.ones((CHUNK, CHUNK), dtype=bool))
    idx = jnp.arange(CHUNK, dtype=jnp.float32)
    lg = log_gamma.astype(jnp.float32)[:, None]
    diff = jnp.maximum(idx[:, None] - idx[None, :], 0.0)
    decay = jnp.where(causal, jnp.exp(lg[:, :, None] * diff), 0.0)
    q_dec = jnp.exp(lg * (idx + 1.0)).T[None, :, :, None]
    k_dec = jnp.exp(lg * (CHUNK - 1.0 - idx)).T[None, :, :, None]
    c_dec = jnp.exp(lg[:, 0] * CHUNK)[None, :, None, None]

    def step(s, inp):
        qc, kc, vc = inp
        att = jnp.einsum('bihd,bjhd->bhij', qc, kc) * decay
        o = jnp.einsum('bhij,bjhe->bihe', att, vc) + jnp.einsum('bihd,bhde->bihe', qc, s) * q_dec
        s = c_dec * s + jnp.einsum('bjhd,bjhe->bhde', kc * k_dec, vc)
        return s, o

    s_fin, o = lax.scan(step, s0.astype(jnp.float32), (_chunks(q, n), _chunks(k, n), _chunks(v, n)))
    return jnp.moveaxis(o, 0, 1).reshape(B, T, H, dv), s_fin


def _flip(a):
    return jnp.flip(a, axis=1)


def mixer_even(h, w_in, gate_w2, gate_b, ret_decay, gla_ng, ret_ng, w_out, s_gla0, s_ret0):
    B, T, _ = h.shape
    gq, gk, gv, gg, glr, rq, rk, rv, rg = _split(h @ w_in, AB_SPLITS)
    gq = gq.reshape(B, T, H_A, DK_A) * (DK_A ** -0.5)
    gk = gk.reshape(B, T, H_A, DK_A)
    gv = gv.reshape(B, T, H_A, DV_A)
    la_f = (jax.nn.log_sigmoid((glr[..., :GATE_RANK] @ gate_w2[0] + gate_b[0]).astype(jnp.float32)) / GATE_TEMP).reshape(B, T, H_A, DK_A)
    la_b = (jax.nn.log_sigmoid((glr[..., GATE_RANK:] @ gate_w2[1] + gate_b[1]).astype(jnp.float32)) / GATE_TEMP).reshape(B, T, H_A, DK_A)
    oa_f, sa_f = gla_scan(gq, gk, gv, la_f, s_gla0[:, 0])
    oa_b, sa_b = gla_scan(_flip(gq), _flip(gk), _flip(gv), _flip(la_b), s_gla0[:, 1])
    o_gla = oa_f + _flip(oa_b)
    rq = rq.reshape(B, T, H_B, DK_B)
    rk = rk.reshape(B, T, H_B, DK_B) * (DK_B ** -0.5)
    rv = rv.reshape(B, T, H_B, DV_B)
    lgam = jax.nn.log_sigmoid(ret_decay.astype(jnp.float32))
    ob_f, sb_f = retention_scan(rq, rk, rv, lgam[0], s_ret0[:, 0])
    ob_b, sb_b = retention_scan(_flip(rq), _flip(rk), _flip(rv), lgam[1], s_ret0[:, 1])
    o_ret = ob_f + _flip(ob_b)
    y_gla = jax.nn.silu(gg) * head_norm(o_gla, gla_ng).reshape(B, T, H_A * DV_A).astype(h.dtype)
    y_ret = jax.nn.silu(rg) * head_norm(o_ret, ret_ng).reshape(B, T, H_B * DV_B).astype(h.dtype)
    y = jnp.concatenate([y_gla, y_ret], axis=-1) @ w_out
    return y, jnp.stack([sa_f, sa_b], axis=1), jnp.stack([sb_f, sb_b], axis=1)


def axial_rope(x, rows, cols):
    half = ROPE_DIM // 2
    quarter = half // 2
    inv = ROPE_BASE ** (-jnp.arange(quarter, dtype=jnp.float32) * 2.0 / half)

    def rot(xa, pos):
        ang = pos[:, None] * inv[None, :]
        cos = jnp.cos(ang)[None, :, None, :]
        sin = jnp.sin(ang)[None, :, None, :]
        x1, x2 = xa[..., :quarter], xa[..., quarter:]
        return jnp.concatenate([x1 * cos - x2 * sin, x1 * sin + x2 * cos], axis=-1)

    x32 = x.astype(jnp.float32)
    return jnp.concatenate([rot(x32[..., :half], rows), rot(x32[..., half:], cols)], axis=-1).astype(x.dtype)


def mla_project(h, w_in, q_ng, w_uq, kv_ng):
    B, T, _ = h.shape
    cq, ckv, kpe = _split(h @ w_in, (Q_RANK, KV_RANK, ROPE_DIM))
    q = (rms_norm(cq, q_ng) @ w_uq).reshape(B, T, H_C, NOPE_DIM + ROPE_DIM)
    return q[..., :NOPE_DIM], q[..., NOPE_DIM:], rms_norm(ckv, kv_ng), kpe


def mla_expand(ckv, w_ukv):
    B, T, _ = ckv.shape
    kv = (ckv @ w_ukv).reshape(B, T, H_C, NOPE_DIM + V_DIM)
    return kv[..., :NOPE_DIM], kv[..., NOPE_DIM:]


def mla_attend(qn, qp, kn, kp, v):
    B, Tq, H, _ = qn.shape
    nb = Tq // Q_BLOCK
    qn_b = jnp.moveaxis(qn.reshape(B, nb, Q_BLOCK, H, NOPE_DIM), 1, 0)
    qp_b = jnp.moveaxis(qp.reshape(B, nb, Q_BLOCK, H, ROPE_DIM), 1, 0)

    def one_block(args):
        a, r = args
        s = jnp.einsum('bqhd,bkhd->bhqk', a, kn) + jnp.einsum('bqhr,bkr->bhqk', r, kp)
        p = jax.nn.softmax(s.astype(jnp.float32) * MLA_SCALE, axis=-1).astype(v.dtype)
        return jnp.einsum('bhqk,bkhd->bqhd', p, v)

    o = lax.map(one_block, (qn_b, qp_b))
    return jnp.moveaxis(o, 0, 1).reshape(B, Tq, H * V_DIM)


def mla_context(h, w_in, q_ng, w_uq, kv_ng, w_ukv, w_out):
    qn, qp, ckv, kpe = mla_project(h, w_in, q_ng, w_uq, kv_ng)
    kn, v = mla_expand(ckv, w_ukv)
    return mla_attend(qn, qp, kn, kpe, v) @ w_out, ckv, kpe


def mla_latent(h, w_in, q_ng, w_uq, kv_ng, w_ukv, w_out, ctx_ckv, ctx_kpe, rows, cols):
    qn, qp, ckv, kpe = mla_project(h, w_in, q_ng, w_uq, kv_ng)
    qp = axial_rope(qp, rows, cols)
    kpe = axial_rope(kpe[:, :, None, :], rows, cols)[:, :, 0, :]
    kn, v = mla_expand(jnp.concatenate([ctx_ckv.astype(ckv.dtype), ckv], axis=1), w_ukv)
    kp = jnp.concatenate([ctx_kpe.astype(kpe.dtype), kpe], axis=1)
    return mla_attend(qn, qp, kn, kp, v) @ w_out


def conv_ffn(h, w_in, conv_w, w_out):
    a, b = jnp.split(h @ w_in, 2, axis=-1)
    T = a.shape[1]
    ap = jnp.pad(a, ((0, 0), (1, 1), (0, 0)))
    a = conv_w[0] * ap[:, :T] + conv_w[1] * ap[:, 1:T + 1] + conv_w[2] * ap[:, 2:T + 2]
    return (jax.nn.silu(a) * b) @ w_out


def _normal(k, shape, scale):
    return scale * jax.random.normal(k, shape, jnp.float32)


def setup_inputs(seed: int = 0) -> dict:
    key = jax.random.key(seed)
    ks = jax.random.split(key, 32)
    D = D_MODEL
    ret_base = jnp.log(2.0 ** (5.0 + jnp.arange(H_B, dtype=jnp.float32)) - 1.0)
    return {
        'x_prompt': _normal(ks[0], (BATCH, SEQ, D), 1.0),
        'x_sample': _normal(ks[1], (DEC_BATCH, DEC_SEQ, D), 1.0),
        'state_gla': _normal(ks[2], (DEC_BATCH, N_EVEN, 2, H_A, DK_A, DV_A), 0.1),
        'state_ret': _normal(ks[3], (DEC_BATCH, N_EVEN, 2, H_B, DK_B, DV_B), 0.1),
        'cache_ckv': _normal(ks[4], (DEC_BATCH, N_ODD, PAST_LEN, KV_RANK), 1.0),
        'cache_kpe': _normal(ks[5], (DEC_BATCH, N_ODD, PAST_LEN, ROPE_DIM), 1.0),
        'c': _normal(ks[6], (DEC_BATCH, D), 1.0),
        'c_ctx': _normal(ks[7], (D,), 1.0),
        'mod_w': _normal(ks[8], (DEPTH, D, 6 * D), 0.5 * D ** -0.5),
        'mod_b': _normal(ks[9], (DEPTH, 6 * D), 0.02),
        'norm1_g': 1.0 + _normal(ks[10], (DEPTH, D), 0.05),
        'norm2_g': 1.0 + _normal(ks[11], (DEPTH, D), 0.05),
        'ab_w_in': _normal(ks[12], (N_EVEN, D, AB_IN), D ** -0.5),
        'gla_gate_w2': _normal(ks[13], (N_EVEN, 2, GATE_RANK, H_A * DK_A), GATE_RANK ** -0.5),
        'gla_gate_b': _normal(ks[14], (N_EVEN, 2, H_A * DK_A), 0.1),
        'ret_decay': ret_base + _normal(ks[15], (N_EVEN, 2, H_B), 0.1),
        'gla_norm_g': 1.0 + _normal(ks[16], (N_EVEN, H_A * DV_A), 0.05),
        'ret_norm_g': 1.0 + _normal(ks[17], (N_EVEN, H_B * DV_B), 0.05),
        'ab_w_out': _normal(ks[18], (N_EVEN, AB_MIX, D), AB_MIX ** -0.5),
        'mla_w_in': _normal(ks[19], (N_ODD, D, MLA_IN), D ** -0.5),
        'mla_q_norm_g': 1.0 + _normal(ks[20], (N_ODD, Q_RANK), 0.05),
        'mla_w_uq': _normal(ks[21], (N_ODD, Q_RANK, H_C * (NOPE_DIM + ROPE_DIM)), Q_RANK ** -0.5),
        'mla_kv_norm_g': 1.0 + _normal(ks[22], (N_ODD, KV_RANK), 0.05),
        'mla_w_ukv': _normal(ks[23], (N_ODD, KV_RANK, H_C * (NOPE_DIM + V_DIM)), KV_RANK ** -0.5),
        'mla_w_out': _normal(ks[24], (N_ODD, H_C * V_DIM, D), (H_C * V_DIM) ** -0.5),
        'ffn_w_in': _normal(ks[25], (DEPTH, D, 2 * D_FF), D ** -0.5),
        'ffn_conv': _normal(ks[26], (DEPTH, CONV_W, D_FF), CONV_W ** -0.5),
        'ffn_w_out': _normal(ks[27], (DEPTH, D_FF, D), D_FF ** -0.5),
        'final_norm_g': 1.0 + _normal(ks[28], (D,), 0.05),
    }


def reference(x_prompt, x_sample, state_gla, state_ret, cache_ckv, cache_kpe, c, c_ctx, mod_w, mod_b, norm1_g, norm2_g, ab_w_in, gla_gate_w2, gla_gate_b, ret_decay, gla_norm_g, ret_norm_g, ab_w_out, mla_w_in, mla_q_norm_g, mla_w_uq, mla_kv_norm_g, mla_w_ukv, mla_w_out, ffn_w_in, ffn_conv, ffn_w_out, final_norm_g):
    xc = x_prompt
    xl = x_sample
    bc = xc.shape[0]
    t_lat = xl.shape[1]
    ROWS = t_lat // GRID_W
    rows = jnp.repeat(jnp.arange(ROWS), GRID_W).astype(jnp.float32)
    cols = jnp.tile(jnp.arange(GRID_W), ROWS).astype(jnp.float32)
    gla_states, ret_states, ckv_list, kpe_list = [], [], [], []
    for l in range(DEPTH):
        mc = modulation(c_ctx[None, :], mod_w[l], mod_b[l])
        ml = modulation(c, mod_w[l], mod_b[l])
        hc = rms_norm(xc, norm1_g[l]) * (1.0 + mc[1]) + mc[0]
        hl = rms_norm(xl, norm1_g[l]) * (1.0 + ml[1]) + ml[0]
        if l % 2 == 0:
            e = l // 2
            zg = jnp.zeros((bc, 2, H_A, DK_A, DV_A), jnp.float32)
            zr = jnp.zeros((bc, 2, H_B, DK_B, DV_B), jnp.float32)
            oc, sg, sr = mixer_even(hc, ab_w_in[e], gla_gate_w2[e], gla_gate_b[e], ret_decay[e], gla_norm_g[e], ret_norm_g[e], ab_w_out[e], zg, zr)
            ol, _, _ = mixer_even(hl, ab_w_in[e], gla_gate_w2[e], gla_gate_b[e], ret_decay[e], gla_norm_g[e], ret_norm_g[e], ab_w_out[e], state_gla[:, e], state_ret[:, e])
            gla_states.append(sg)
            ret_states.append(sr)
        else:
            i = l // 2
            oc, ckv, kpe = mla_context(hc, mla_w_in[i], mla_q_norm_g[i], mla_w_uq[i], mla_kv_norm_g[i], mla_w_ukv[i], mla_w_out[i])
            ol = mla_latent(hl, mla_w_in[i], mla_q_norm_g[i], mla_w_uq[i], mla_kv_norm_g[i], mla_w_ukv[i], mla_w_out[i], cache_ckv[:, i], cache_kpe[:, i], rows, cols)
            ckv_list.append(ckv)
            kpe_list.append(kpe)
        xc = xc + mc[2] * oc
        xl = xl + ml[2] * ol
        hc = rms_norm(xc, norm2_g[l]) * (1.0 + mc[4]) + mc[3]
        hl = rms_norm(xl, norm2_g[l]) * (1.0 + ml[4]) + ml[3]
        xc = xc + mc[5] * conv_ffn(hc, ffn_w_in[l], ffn_conv[l], ffn_w_out[l])
        xl = xl + ml[5] * conv_ffn(hl, ffn_w_in[l], ffn_conv[l], ffn_w_out[l])
    y_prompt = rms_norm(xc, final_norm_g)
    y_sample = rms_norm(xl, final_norm_g)
    new_state_gla = jnp.stack(gla_states, axis=1)
    new_state_ret = jnp.stack(ret_states, axis=1)
    new_cache_ckv = jnp.stack(ckv_list, axis=1)
    new_cache_kpe = jnp.stack(kpe_list, axis=1)
    return (y_prompt, y_sample, new_state_gla, new_state_ret, new_cache_ckv, new_cache_kpe)
```

```python
import math
from contextlib import ExitStack
import numpy as np
import ml_dtypes
import concourse.bass as bass
import concourse.mybir as mybir
from concourse.bass_utils import run_bass_kernel_spmd

F32 = mybir.dt.float32
BF16 = mybir.dt.bfloat16
AF = mybir.ActivationFunctionType
ALU = mybir.AluOpType

ENGS = ("pe", "act", "dve", "pool", "sp")
SEM_WRAP = 3500


class Op:
    __slots__ = ("eng", "fn", "reads", "writes", "dma", "idx", "waits", "sig", "extra")

    def __init__(self, eng, fn, reads, writes, dma):
        self.eng = eng
        self.fn = fn
        self.reads = reads
        self.writes = writes
        self.dma = dma
        self.waits = []
        self.sig = None
        self.extra = ()


def _stream(o):
    return ("d", o.dma) if o.dma is not None else ("e", o.eng)


class Prog:
    def __init__(self, nc, stack):
        self.nc = nc
        self.stack = stack
        self.ops = []
        self.sems = {}
        self.sigcount = {}
        self.nops_total = 0

    def op(self, eng, fn, reads=(), writes=(), dma=None):
        if dma is not None:
            km = self.__dict__.setdefault("dmakeys", {})
            if dma not in km:
                km[dma] = "q%d" % len(km)
            dma = km[dma]
        o = Op(eng, fn, tuple(reads), tuple(writes), dma)
        o.idx = len(self.ops)
        self.ops.append(o)
        return o

    def _resolve(self):
        ops = self.ops
        last_w, readers = {}, {}
        deps_of = []
        for o in ops:
            deps = set(o.extra)
            for r in o.reads:
                w = last_w.get(r)
                if w is not None:
                    deps.add(w.idx)
            for w_ in o.writes:
                w = last_w.get(w_)
                if w is not None:
                    deps.add(w.idx)
                for rd in readers.get(w_, ()):
                    deps.add(rd.idx)
            deps.discard(o.idx)
            deps_of.append(deps)
            for r in o.reads:
                readers.setdefault(r, []).append(o)
            for w_ in o.writes:
                last_w[w_] = o
                readers[w_] = []
        pos, cnt = {}, {}
        for o in ops:
            s = _stream(o)
            cnt[s] = cnt.get(s, 0) + 1
            pos[o.idx] = cnt[s]
        waited = {e: {} for e in ENGS}
        need_sig = set()
        for o in ops:
            e = o.eng
            best = {}
            for d in deps_of[o.idx]:
                p = ops[d]
                s = _stream(p)
                if s == ("e", e):
                    if e == "pe":
                        continue
                    raw = any(b in p.writes for b in o.reads) or any(b in p.writes for b in o.writes)
                    if not raw and d not in o.extra:
                        continue
                if pos[d] > best.get(s, 0):
                    best[s] = pos[d]
            for s, v in best.items():
                if waited[e].get(s, 0) >= v:
                    continue
                waited[e][s] = v
                o.waits.append((s, v))
                need_sig.add((s, v))
        sigval = {}
        for o in ops:
            s = _stream(o)
            if (s, pos[o.idx]) in need_sig or o.dma is not None:
                self.sigcount[s] = self.sigcount.get(s, 0) + 1
                sigval[(s, pos[o.idx])] = self.sigcount[s]
                o.sig = (s, self.sigcount[s])
        for o in ops:
            o.waits = [(s, sigval[(s, v)]) for (s, v) in o.waits]

    def _semval(self, s, v):
        k = (v - 1) // SEM_WRAP
        val = (v - 1) % SEM_WRAP + 1
        lst = self.sems.setdefault(s, [])
        while len(lst) <= k:
            nm = ("s_%s_%s_%d" % (s[0], s[1], len(lst))).replace(" ", "")
            lst.append(self.stack.enter_context(self.nc.semaphore(nm)))
        return lst[k], val * (16 if s[0] == "d" else 1)

    def flush(self):
        nc = self.nc
        if not self.ops:
            return
        last = {}
        for o in self.ops:
            last[_stream(o)] = o.idx
        j = self.op("sp", lambda e: e.nop())
        j.extra = tuple(last.values())
        self._resolve()
        by_eng = {e: [o for o in self.ops if o.eng == e] for e in ENGS}
        semval = self._semval

        def run(engobj, ops):
            for o in ops:
                for (s, v) in o.waits:
                    sem, val = semval(s, v)
                    engobj.wait_ge(sem, val)
                ins = o.fn(engobj)
                if o.sig is not None:
                    s, v = o.sig
                    sem, _ = semval(s, v)
                    ins.then_inc(sem, 16 if s[0] == "d" else 1)

        with nc.Block() as block:
            if by_eng["pe"]:
                @block.tensor
                def _(e):
                    run(e, by_eng["pe"])
            if by_eng["act"]:
                @block.scalar
                def _(e):
                    run(e, by_eng["act"])
            if by_eng["dve"]:
                @block.vector
                def _(e):
                    run(e, by_eng["dve"])
            if by_eng["pool"]:
                @block.gpsimd
                def _(e):
                    run(e, by_eng["pool"])
            if by_eng["sp"]:
                @block.sync
                def _(e):
                    run(e, by_eng["sp"])
        self.nops_total += len(self.ops)
        self.ops = []
        self.dmakeys = {}
        nc.all_engine_barrier()


D = 2048
KC = 16
DFF = 5632
FC = 44
EPS = 1e-6
AB_IN = 6176
C_GQ, C_GK, C_GV, C_GG, C_GLR, C_RQ, C_RK, C_RV, C_RG = 0, 512, 1024, 2048, 3072, 3104, 3616, 4128, 5152


def split(L, w):
    n = (L + w - 1) // w
    ww = (L + n - 1) // n
    return [(s, min(L, s + ww)) for s in range(0, L, ww)]


def build(cfg):
    TS, TP, PAST, NPR = cfg["TS"], cfg["TP"], cfg["PAST"], 2
    LQ = cfg.get("LQ", TS)
    dump = cfg.get("dump", None)
    stop_after = cfg.get("stop", "Z")
    T0 = TS + NPR * TP
    CH = cfg.get("CH", 128)
    NCH = T0 // CH
    NCT = 512 // CH
    EXT = 1 if LQ < TS else 0
    LQE = LQ + EXT
    nc = bass.Bass("TRN2", target_bir_lowering=False)
    I = {}

    def inp(name, shape, dt=F32):
        I[name] = nc.dram_tensor(name, list(shape), dt, kind="ExternalInput").ap()
        return I[name]

    def outp(name, shape):
        return nc.dram_tensor(name, list(shape), F32, kind="ExternalOutput").ap()

    def scr(name, shape, dt=F32):
        return nc.dram_tensor(name, list(shape), dt).ap()

    xs = inp("xs", [TS, D]); xp = inp("xp", [NPR * TP, D])
    sg_in = inp("sg", [2, 4, 128, 256]); sr_in = inp("sr", [2, 4, 128, 256])
    cckv = inp("cckv", [PAST, 512]); ckpe = inp("ckpe", [PAST, 64])
    cond = inp("cond", [2, D])
    mod_w = inp("mod_w", [2, D, 6 * D]); mod_b = inp("mod_b", [2, 6 * D])
    n1g = inp("norm1_g", [2, D]); n2g = inp("norm2_g", [2, D])
    w_ab = inp("ab_w_in", [D, AB_IN]); gw2 = inp("gla_gate_w2", [2, 16, 512]); gb = inp("gla_gate_b", [2, 512])
    rdec = inp("ret_decay", [8]); gng = inp("gla_norm_g", [1024]); rng_ = inp("ret_norm_g", [1024])
    w_abo = inp("ab_w_out", [D, D])
    w_mi = inp("mla_w_in", [D, 1088]); qng = inp("mla_q_norm_g", [512]); w_uq = inp("mla_w_uq", [512, 3072])
    kvng = inp("mla_kv_norm_g", [512]); w_ukv = inp("mla_w_ukv", [512, 4096]); w_mo = inp("mla_w_out", [D, D])
    w_fi = inp("ffn_w_in", [2, D, 2 * DFF]); fconv = inp("ffn_conv", [2, 3, DFF]); w_fo = inp("ffn_w_out", [2, DFF, D])
    fng = inp("final_norm_g", [D])
    c_id = inp("c_ident", [128, 128]); c_tri = inp("c_tri", [4, 128, 128]); c_mask = inp("c_mask", [2, CH, 8 * CH])
    c_pos = inp("c_pos", [128, 4]); c_rope = inp("c_rope", [2, 64, TS])

    y_s = outp("y_s", [LQ, D]); y_p = outp("y_p", [NPR * TP, D])
    o_sg = outp("ns_gla", [NPR, 2, 4, 128, 256]); o_sr = outp("ns_ret", [NPR, 2, 4, 128, 256])
    o_ckv = outp("n_ckv", [NPR * TP, 512]); o_kpe = outp("n_kpe", [NPR * TP, 64])
    dbg = outp("dbg", [D, T0]) if dump else None

    XA = scr("XA", [D, T0]); XB = scr("XB", [D, T0])
    Wab = scr("Wab", [D, AB_IN], BF16); Wabo = scr("Wabo", [D, D], BF16)
    Wmi = scr("Wmi", [D, 1088], BF16); Wuq = scr("Wuq", [512, 3072], BF16); Wukv = scr("Wukv", [512, 4096], BF16)
    Wmo = scr("Wmo", [D, D], BF16)
    Wfi = scr("Wfi", [2, D, 2 * DFF], BF16); Wfo = scr("Wfo", [2, DFF, D], BF16)
    modD = scr("modD", [2, 2, 6 * D])
    QDT = scr("QDT", [2, 4, 128, T0], BF16); KIT = scr("KIT", [2, 4, 128, T0], BF16)
    QRT = scr("QRT", [4, 128, T0], BF16); KRT = scr("KRT", [4, 128, T0], BF16)
    KE = scr("KE", [2, T0, 512], BF16); KR = scr("KR", [T0, 512], BF16)
    VG = scr("VG", [T0, 1024], BF16); VR = scr("VR", [2, T0, 1024], BF16); GG = scr("GG", [T0, 2048], BF16)
    SDD = scr("SDD", [2, 128, NCH, 4])
    OF = scr("OF", [T0, 2048]); OT = scr("OT", [T0, 2048])

    st = ExitStack()
    P = Prog(nc, st)
    cnt = [0]

    def uid(p):
        cnt[0] += 1
        return "%s%d" % (p, cnt[0])

    def DMA(eng, out, in_, reads, writes, key, slow=False):
        if slow:
            P.op(eng, lambda e: e.dma_start(out=out, in_=in_, allow_slow_non_contiguous=True), reads, writes, dma=key)
        else:
            P.op(eng, lambda e: e.dma_start(out=out, in_=in_), reads, writes, dma=key)

    def ACT(out, in_, func, reads, writes, scale=1.0, bias=0.0, accum=None):
        if accum is None:
            P.op("act", lambda e: e.activation(out=out, in_=in_, func=func, bias=bias, scale=scale), reads, writes)
        else:
            P.op("act", lambda e: e.activation(out=out, in_=in_, func=func, bias=bias, scale=scale, accum_out=accum), reads, writes)

    def TSC(eng, out, in0, s1, s2, op0, op1, reads, writes):
        if s2 is None:
            P.op(eng, lambda e: e.tensor_scalar(out=out, in0=in0, scalar1=s1, scalar2=None, op0=op0), reads, writes)
        else:
            P.op(eng, lambda e: e.tensor_scalar(out=out, in0=in0, scalar1=s1, scalar2=s2, op0=op0, op1=op1), reads, writes)

    def TT(eng, out, in0, in1, op, reads, writes):
        P.op(eng, lambda e: e.tensor_tensor(out=out, in0=in0, in1=in1, op=op), reads, writes)

    def STT(eng, out, in0, scalar, in1, op0, op1, reads, writes):
        P.op(eng, lambda e: e.scalar_tensor_tensor(out=out, in0=in0, scalar=scalar, in1=in1, op0=op0, op1=op1), reads, writes)

    def CP(eng, out, in_, reads, writes):
        P.op(eng, lambda e: e.tensor_copy(out=out, in_=in_), reads, writes)

    def MM(out, lhsT, rhs, start, stop, reads, writes):
        P.op("pe", lambda e: e.matmul(out, lhsT, rhs, start=start, stop=stop), reads, writes)

    def TR(out, in_, ident, reads, writes):
        P.op("pe", lambda e: e.transpose(out, in_, ident), reads, writes)

    rr = [0]

    def ve():
        rr[0] ^= 1
        return "dve" if rr[0] else "pool"

    def xsrc_rows(t0, n):
        if t0 < TS:
            return xs[t0:t0 + n, :]
        return xp[t0 - TS:t0 - TS + n, :]

    def row_of(t0):
        return 0 if t0 < TS else 1

    seqs = [(0, TS, "s")] + [(TS + j * TP, TP, "p%d" % j) for j in range(NPR)]

    class Phase:
        def __init__(self):
            self.s = ExitStack()
            self.ps_n = 0

        def sb(self, name, shape, dt=F32):
            return self.s.enter_context(nc.sbuf_tensor(uid(name), list(shape), dt))

        def ps(self, name, shape=(128, 512), dt=F32):
            return self.s.enter_context(nc.psum_tensor(uid(name), list(shape), dt))

        def end(self):
            P.flush()
            self.s.close()

    def consts(ph, bf_ident=False):
        c = {}
        c["id"] = ph.sb("id", [128, 128])
        DMA("sp", c["id"][:], c_id[:, :], [], ["c_id"], "c_id")
        c["ones"] = ph.sb("ones", [128, 128], BF16)
        P.op("pool", lambda e: e.memset(c["ones"][:], 1.0), [], ["c_ones"])
        c["eps"] = ph.sb("eps", [128, 1])
        P.op("pool", lambda e: e.memset(c["eps"][:], EPS), [], ["c_eps"])
        if bf_ident:
            c["idb"] = ph.sb("idb", [128, 128], BF16)
            CP("dve", c["idb"][:], c["id"][:], ["c_id"], ["c_idb"])
        return c

    CASTS = {"ab": (Wab, w_ab, D, AB_IN), "abo": (Wabo, w_abo, D, D), "mi": (Wmi, w_mi, D, 1088), "uq": (Wuq, w_uq, 512, 3072),
             "ukv": (Wukv, w_ukv, 512, 4096), "mo": (Wmo, w_mo, D, D),
             "fi0": (Wfi[0], w_fi[0], D, 2 * DFF), "fo0": (Wfo[0], w_fo[0], DFF, D),
             "fi1": (Wfi[1], w_fi[1], D, 2 * DFF), "fo1": (Wfo[1], w_fo[1], DFF, D)}
    kcast = [0]

    def cast_list(names):
        out = []
        for nm in names:
            dst, src, R, C_ = CASTS[nm]
            rs = max(128, (1 << 21) // C_ // 128 * 128)
            for r0 in range(0, R, rs):
                out.append((dst[r0:min(R, r0 + rs), :], src[r0:min(R, r0 + rs), :]))
        return out

    def emit_casts(lst, n):
        for _ in range(min(n, len(lst))):
            dst, src = lst.pop(0)
            k = kcast[0]; kcast[0] += 1
            DMA("pool", dst, src, [], ["wcast%d" % k], "wc%d" % (k % 8))

    def gen_M(ph, c):
        cT = ph.sb("cT", [128, 2, KC]); csT = ph.sb("csT", [128, 2, KC])
        for r in range(2):
            DMA("sp", cT[:, r, :], cond[r].rearrange("(c p) -> p c", p=128), ["cT"], ["cT"], "cT", slow=True)
        ACT(csT[:], cT[:], AF.Silu, ["cT"], ["csT"])
        wm = [ph.sb("wm", [128, KC, 512]) for _ in range(2)]
        mb = [ph.sb("mb", [2, 512]) for _ in range(2)]
        mrow = [ph.sb("mrow", [2, 512]) for _ in range(2)]
        pm = [ph.ps("pm") for _ in range(2)]
        for l in range(2):
            for j in range(24):
                b = j % 2
                DMA("sp", wm[b][:], mod_w[l][:, j * 512:(j + 1) * 512].rearrange("(k p) c -> p k c", p=128), [], ["wm%d" % b], "wm%d" % b)
                DMA("act", mb[b][:], mod_b[l, j * 512:(j + 1) * 512].partition_broadcast(2), [], ["mb%d" % b], "mb%d" % b)
                for kc in range(KC):
                    MM(pm[b][0:2, :], csT[:, :, kc], wm[b][:, kc, :], kc == 0, kc == KC - 1, ["csT", "wm%d" % b], ["pm%d" % b])
                TT("dve", mrow[b][:], pm[b][0:2, :], mb[b][:], ALU.add, ["pm%d" % b, "mb%d" % b], ["mrow%d" % b])
                DMA("sp", modD[l][:, j * 512:(j + 1) * 512], mrow[b][:], ["mrow%d" % b], ["modD"], "mrow%d" % b)
                yield

    def load_mod(ph, l, which):
        m = {}
        for j in which:
            t = ph.sb("modT", [128, 2, KC])
            for r in range(2):
                DMA("sp", t[:, r, :], modD[l][r, j * D:(j + 1) * D].rearrange("(c p) -> p c", p=128), ["modD", "modT%d" % j], ["modT%d" % j], "modT%d" % j, slow=True)
            m[j] = t
        return m

    def load_vecT(ph, src1d, n, key):
        t = ph.sb(key, [128, n])
        DMA("sp", t[:], src1d.rearrange("(c p) -> p c", p=128), [], [key], key, slow=True)
        return t

    def make_AB(ph, l, gsrc, js, jb, key):
        m = load_mod(ph, l, (js, jb))
        g = load_vecT(ph, gsrc, KC, key + "g")
        A = ph.sb("A", [128, 2, KC])
        for r in range(2):
            STT("dve", A[:, r, :], m[js][:, r, :], 1.0, g[:], ALU.add, ALU.mult, ["modT%d" % js, key + "g"], [key + "A"])
        return A, m[jb]

    def fm_rstd(ph, c, xt, kcn, N, dn, rk, keyp, bufs):
        sq, psn, rstd = bufs
        for kc in range(kcn):
            b = kc % 2
            ACT(sq[b][:, 0:N], xt[:, kc, 0:N], AF.Square, rk, [keyp + "sq%d" % b])
            MM(psn[:, 0:N], c["ones"][:], sq[b][:, 0:N], kc == 0, kc == kcn - 1, [keyp + "sq%d" % b, "c_ones"], [keyp + "psn"])
        ACT(rstd[:, 0:N], psn[:, 0:N], AF.Sqrt, [keyp + "psn", "c_eps"], [keyp + "rstd"], scale=1.0 / dn, bias=c["eps"][:, 0:1])
        P.op("dve", lambda e: e.reciprocal(out=rstd[:, 0:N], in_=rstd[:, 0:N]), [keyp + "rstd"], [keyp + "rstd"])

    def fm_norm_mod(ph, c, xt, N, A, B, r, hT, rk, keyp, bufs, tmp, hkey=None):
        fm_rstd(ph, c, xt, KC, N, D, rk, keyp, bufs)
        rstd = bufs[2]
        for kc in range(KC):
            b = kc % 2
            TT(ve() if False else "dve", tmp[b][:, 0:N], xt[:, kc, 0:N], rstd[:, 0:N], ALU.mult, rk + [keyp + "rstd"], [keyp + "tmp%d" % b])
            ACT(hT[:, kc, 0:N], tmp[b][:, 0:N], AF.Identity, [keyp + "tmp%d" % b, keyp + "A"], [hkey or (keyp + "hT")],
                scale=A[:, r, kc:kc + 1], bias=B[:, r, kc:kc + 1])

    def gen_T(ph, c):
        xt = [ph.sb("xt", [128, D]) for _ in range(2)]
        xT = [ph.sb("xT", [128, KC, 128]) for _ in range(2)]
        pt = [ph.ps("pt") for _ in range(4)]
        for i in range(T0 // 128):
            b = i % 2
            DMA("sp", xt[b][:], xsrc_rows(i * 128, 128), [], ["xt%d" % b], "xt%d" % b)
            for q in range(4):
                for k4 in range(4):
                    kc = q * 4 + k4
                    TR(pt[q][:, k4 * 128:(k4 + 1) * 128], xt[b][:, kc * 128:(kc + 1) * 128], c["id"][:], ["xt%d" % b, "c_id"], ["pt%d" % q])
                if q % 2 == 0:
                    CP("dve", xT[b][:, q * 4:(q + 1) * 4, :], pt[q][:].rearrange("p (k t) -> p k t", k=4), ["pt%d" % q], ["xT%d" % b])
                else:
                    ACT(xT[b][:, q * 4:(q + 1) * 4, :], pt[q][:].rearrange("p (k t) -> p k t", k=4), AF.Copy, ["pt%d" % q], ["xT%d" % b])
            DMA("act", XA.rearrange("(k p) t -> p k t", p=128)[:, :, i * 128:(i + 1) * 128], xT[b][:], ["xT%d" % b], ["XA"], "stx%d" % b)
            yield

    def phase_WMT():
        ph = Phase()
        c = consts(ph)
        cl = cast_list(["ab"])
        emit_casts(cl, len(cl))
        gens = [gen_M(ph, c), gen_T(ph, c)]
        while gens:
            for g_ in list(gens):
                try:
                    next(g_)
                except StopIteration:
                    gens.remove(g_)
        ph.end()

    def phase_A():
        ph = Phase()
        c = consts(ph)
        A, B = make_AB(ph, 0, n1g[0], 1, 0, "A_")
        tri = ph.sb("tri", [128, 4, 128])
        DMA("sp", tri[:], c_tri.rearrange("k p t -> p k t"), [], ["tri"], "tri")
        pos = ph.sb("pos", [128, 4])
        DMA("sp", pos[:], c_pos[:, :], [], ["pos"], "pos")
        rd = ph.sb("rd", [128, 8]); lg = ph.sb("lg", [128, 8]); nlg = ph.sb("nlg", [128, 8])
        DMA("sp", rd[:], rdec.partition_broadcast(128), [], ["rd"], "rd")
        ACT(lg[:], rd[:], AF.Exp, ["rd"], ["lg"], scale=-1.0)
        ACT(lg[:], lg[:], AF.Ln, ["lg"], ["lg"], bias=1.0)
        TSC("dve", nlg[:], lg[:], -1.0, None, ALU.mult, None, ["lg"], ["nlg"])
        vsc = ph.sb("vsc", [128, 8])
        for d_ in range(2):
            for h in range(4):
                ACT(vsc[:, d_ * 4 + h:d_ * 4 + h + 1], pos[:, d_:d_ + 1], AF.Exp, ["pos", "lg"], ["vsc"], scale=lg[:, d_ * 4 + h:d_ * 4 + h + 1])
        w2f = ph.sb("w2f", [17, 2, 512]); w2b = ph.sb("w2b", [17, 2, 512], BF16)
        DMA("sp", w2f[0:16, :, :], gw2.rearrange("d k c -> k d c"), [], ["w2f"], "w2f")
        DMA("sp", w2f[16:17, :, :], gb.rearrange("(o d) c -> o d c", o=1), ["w2f"], ["w2f"], "w2f")
        CP("dve", w2b[:], w2f[:], ["w2f"], ["w2b"])
        xt = [ph.sb("xt", [128, KC, 512]) for _ in range(1)]
        sq = [ph.sb("sq", [128, 512], BF16) for _ in range(2)]
        rstd = ph.sb("rstd", [128, 512])
        tmp = [ph.sb("tmp", [128, 512]) for _ in range(2)]
        hT = ph.sb("hT", [128, KC, 512], BF16)
        wsl = [ph.sb("wsl", [128, KC, 512], BF16) for _ in range(2)]
        glrT = [ph.sb("glrT", [17, 512], BF16) for _ in range(2)]
        for d_ in range(2):
            P.op("pool", lambda e, d_=d_: e.memset(glrT[d_][:], 1.0), [], ["glrT%d" % d_])
        ex = ph.sb("ex", [128, 512]); la = [ph.sb("la", [128, 512]) for _ in range(2)]
        EB = [ph.sb("EB", [128, 4, 512]) for _ in range(2)]; EI = [ph.sb("EI", [128, 4, 512]) for _ in range(2)]
        EE = [ph.sb("EE", [128, 4, 512]) for _ in range(2)]
        sd = ph.sb("sd", [128, 2, NCT, 4])
        stf = [ph.sb("stf", [128, 512], BF16) for _ in range(4)]
        stt = [ph.sb("stt", [128, 2048], BF16) for _ in range(2)]
        psn = ph.ps("psn"); pg = ph.ps("pg"); pa = [ph.ps("pa") for _ in range(3)]; pb_ = [ph.ps("pb") for _ in range(2)]
        Wv = Wab.rearrange("(k p) c -> p k c", p=128)
        nslab = [0]

        def wslab(c0, cw):
            b = nslab[0] % 2
            nslab[0] += 1
            DMA("sp", wsl[b][:, :, 0:cw], Wv[:, :, c0:c0 + cw], ["wcast"], ["wsl%d" % b], "wsl%d" % b)
            return wsl[b], "wsl%d" % b

        sfi = [0]; sti = [0]; pai = [0]
        clA = cast_list(["abo", "fi0", "fo0"])
        perA = (len(clA) + (T0 // 512) - 1) // (T0 // 512)

        for (t0, t1) in [(t, t + 512) for t in range(0, T0, 512)]:
            N = 512
            r = row_of(t0)
            emit_casts(clA, perA)
            DMA("act", xt[0][:], XA.rearrange("(k p) t -> p k t", p=128)[:, :, t0:t1], ["XA"], ["A_xt"], "A_xt")
            fm_norm_mod(ph, c, xt[0], N, A, B, r, hT, ["A_xt"], "A_", (sq, psn, rstd), tmp)
            w, wk = wslab(C_GLR, 32)
            for d_ in range(2):
                for kc in range(KC):
                    MM(pg[0:16, :], w[:, kc, d_ * 16:(d_ + 1) * 16], hT[:, kc, :], kc == 0, kc == KC - 1, [wk, "A_hT"], ["pg"])
                CP("dve", glrT[d_][0:16, :], pg[0:16, :], ["pg"], ["glrT%d" % d_])
            for s in range(4):
                for d_ in range(2):
                    p_ = pa[pai[0] % 3]; pk = "pa%d" % (pai[0] % 3); pai[0] += 1
                    MM(p_[:], glrT[d_][:, s * 128:(s + 1) * 128], w2b[:, d_, :], True, True, ["glrT%d" % d_, "w2b"], [pk])
                    ACT(ex[:], p_[:], AF.Exp, [pk], ["ex"], scale=-1.0)
                    ACT(la[d_][:], ex[:], AF.Ln, ["ex"], ["la%d" % d_], bias=1.0)
                    p2 = pa[pai[0] % 3]; pk2 = "pa%d" % (pai[0] % 3); pai[0] += 1
                    for h in range(4):
                        MM(p2[:, h * 128:(h + 1) * 128], la[d_][:, h * 128:(h + 1) * 128], tri[:, d_, :], True, True, ["la%d" % d_, "tri"], [pk2])
                    ACT(EB[d_][:, :, s * 128:(s + 1) * 128], p2[:].rearrange("p (h t) -> p h t", h=4), AF.Exp, [pk2], ["EB%d" % d_])
                    ACT(EI[d_][:, :, s * 128:(s + 1) * 128], p2[:].rearrange("p (h t) -> p h t", h=4), AF.Exp, [pk2], ["EI%d" % d_], scale=-1.0)
                    p3 = pa[pai[0] % 3]; pk3 = "pa%d" % (pai[0] % 3); pai[0] += 1
                    MM(p3[:], tri[:, 2 + d_, :], la[d_][:], True, True, ["la%d" % d_, "tri"], [pk3])
                    ACT(EE[d_][:, s, :], p3[:], AF.Exp, [pk3], ["EE%d" % d_])
            for d_ in range(2):
                off = CH - 1 if d_ == 0 else 0
                CP("pool", sd[:, d_, :, :], EB[d_][:].rearrange("p h (c j) -> p c h j", j=CH)[:, :, :, off], ["EB%d" % d_], ["sd"])
                DMA("act", SDD[d_][:, t0 // CH:t0 // CH + NCT, :], sd[:, d_, :, :], ["sd"], ["SDD"], "sd%d" % d_)
            for (c0, isq, gla) in [(C_GQ, True, True), (C_GK, False, True), (C_RQ, True, False), (C_RK, False, False)]:
                w, wk = wslab(c0, 512)
                for h in range(4):
                    p_ = pb_[h % 2]; pk = "pb%d" % (h % 2)
                    for kc in range(KC):
                        MM(p_[:], w[:, kc, h * 128:(h + 1) * 128], hT[:, kc, :], kc == 0, kc == KC - 1, [wk, "A_hT"], [pk])
                    if gla:
                        for d_ in range(2):
                            sb_ = stf[sfi[0] % 4]; sk = "stf%d" % (sfi[0] % 4); sfi[0] += 1
                            if isq:
                                STT("dve", sb_[:], p_[:], 128 ** -0.5, EB[d_][:, h, :], ALU.mult, ALU.mult, [pk, "EB%d" % d_], [sk])
                                DMA("act", QDT[d_, h][:, t0:t1], sb_[:], [sk], ["QDT"], sk)
                            else:
                                TT("dve", sb_[:], p_[:], EI[d_][:, h, :], ALU.mult, [pk, "EI%d" % d_], [sk])
                                DMA("act", KIT[d_, h][:, t0:t1], sb_[:], [sk], ["KIT"], sk)
                    else:
                        sb_ = stf[sfi[0] % 4]; sk = "stf%d" % (sfi[0] % 4); sfi[0] += 1
                        if isq:
                            ACT(sb_[:], p_[:], AF.Copy, [pk], [sk])
                            DMA("act", QRT[h][:, t0:t1], sb_[:], [sk], ["QRT"], sk)
                        else:
                            ACT(sb_[:], p_[:], AF.Copy, [pk], [sk], scale=128 ** -0.5)
                            DMA("act", KRT[h][:, t0:t1], sb_[:], [sk], ["KRT"], sk)
            for (c0, cw, kind) in [(C_GK, 512, "ke"), (C_RK, 512, "kr"), (C_GV, 512, "gv0"), (C_GV + 512, 512, "gv1"),
                                   (C_RV, 512, "rv0"), (C_RV + 512, 512, "rv1"),
                                   (C_GG, 512, "g0"), (C_GG + 512, 512, "g1"), (C_RG, 512, "g2"), (C_RG + 512, 512, "g3")]:
                w, wk = wslab(c0, cw)
                for s in range(4):
                    p_ = pa[pai[0] % 3]; pk = "pa%d" % (pai[0] % 3); pai[0] += 1
                    for kc in range(KC):
                        MM(p_[:], hT[:, kc, s * 128:(s + 1) * 128], w[:, kc, :], kc == 0, kc == KC - 1, [wk, "A_hT"], [pk])
                    rows = slice(t0 + s * 128, t0 + (s + 1) * 128)
                    sb_ = stt[sti[0] % 2]; sk = "stt%d" % (sti[0] % 2); sti[0] += 1
                    if kind == "ke":
                        for d_ in range(2):
                            TT("dve", sb_[:, d_ * 512:(d_ + 1) * 512], p_[:], EE[d_][:, s, :], ALU.mult, [pk, "EE%d" % d_], [sk])
                        DMA("act", KE[0][rows, :], sb_[:, 0:512], [sk], ["KE"], sk)
                        DMA("act", KE[1][rows, :], sb_[:, 512:1024], [sk], ["KE"], sk + "b")
                    elif kind == "kr":
                        ACT(sb_[:, 0:512], p_[:], AF.Copy, [pk], [sk], scale=128 ** -0.5)
                        DMA("act", KR[rows, :], sb_[:, 0:512], [sk], ["KR"], sk)
                    elif kind[:2] == "gv":
                        j = int(kind[2])
                        ACT(sb_[:, 0:512], p_[:], AF.Copy, [pk], [sk])
                        DMA("act", VG[rows, j * 512:(j + 1) * 512], sb_[:, 0:512], [sk], ["VG"], sk)
                    elif kind[:2] == "rv":
                        j = int(kind[2])
                        for d_ in range(2):
                            for hh in range(2):
                                h = j * 2 + hh
                                TSC("dve", sb_[:, d_ * 512 + hh * 256:d_ * 512 + (hh + 1) * 256], p_[:, hh * 256:(hh + 1) * 256],
                                    vsc[:, d_ * 4 + h:d_ * 4 + h + 1], None, ALU.mult, None, [pk, "vsc"], [sk])
                        DMA("act", VR[0][rows, j * 512:(j + 1) * 512], sb_[:, 0:512], [sk], ["VR"], sk)
                        DMA("act", VR[1][rows, j * 512:(j + 1) * 512], sb_[:, 512:1024], [sk], ["VR"], sk + "b")
                    else:
                        j = int(kind[1])
                        ACT(sb_[:, 0:512], p_[:], AF.Copy, [pk], [sk])
                        DMA("act", GG[rows, j * 512:(j + 1) * 512], sb_[:, 0:512], [sk], ["GG"], sk)
        ph.end()


    def phase_scan(d_):
        ph = Phase()
        msk = ph.sb("msk", [CH, 8 * CH])
        DMA("sp", msk[:], c_mask[d_], [], ["msk"], "msk")
        pos = ph.sb("pos", [128, 4])
        DMA("sp", pos[:], c_pos[:, :], [], ["pos"], "pos")
        rd = ph.sb("rd", [128, 8]); lg = ph.sb("lg", [128, 8])
        DMA("sp", rd[:], rdec.partition_broadcast(128), [], ["rd"], "rd")
        ACT(lg[:], rd[:], AF.Exp, ["rd"], ["lg"], scale=-1.0)
        ACT(lg[:], lg[:], AF.Ln, ["lg"], ["lg"], bias=1.0)
        nlg = ph.sb("nlg", [128, 8])
        TSC("dve", nlg[:], lg[:], -1.0, None, ALU.mult, None, ["lg"], ["nlg"])
        pcol = ph.sb("pcol", [128, 4]); eret = ph.sb("eret", [128, 4]); c64 = ph.sb("c64", [128, 1])
        P.op("pool", lambda e: e.memset(c64[:], float(CH)), [], ["c64"])
        for h in range(4):
            ACT(pcol[:, h:h + 1], pos[:, d_:d_ + 1], AF.Exp, ["pos", "nlg"], ["pcol"], scale=nlg[:, d_ * 4 + h:d_ * 4 + h + 1])
            ACT(eret[:, h:h + 1], c64[:], AF.Exp, ["c64", "nlg"], ["eret"], scale=nlg[:, d_ * 4 + h:d_ * 4 + h + 1])
        S = ph.sb("S", [128, 8, 256]); Sb = ph.sb("Sb", [128, 8, 256], BF16)
        tS = [ph.sb("tS", [128, 256]) for _ in range(2)]
        qg = [ph.sb("qg", [128, 4, 512], BF16) for _ in range(2)]; kg = [ph.sb("kg", [128, 4, 512], BF16) for _ in range(2)]
        qr = [ph.sb("qr", [128, 4, 512], BF16) for _ in range(2)]; kr = [ph.sb("kr", [128, 4, 512], BF16) for _ in range(2)]
        ke = [ph.sb("ke", [CH, NCT, 512], BF16) for _ in range(2)]; krr = [ph.sb("krr", [CH, NCT, 512], BF16) for _ in range(2)]
        vg = [ph.sb("vg", [CH, NCT, 1024], BF16) for _ in range(2)]; vr = [ph.sb("vr", [CH, NCT, 1024], BF16) for _ in range(2)]
        sdt = [ph.sb("sdt", [128, NCT, 4]) for _ in range(2)]
        ofc = [ph.sb("ofc", [CH, 2048]) for _ in range(2)]; oc = [ph.sb("oc", [CH, 2048]) for _ in range(2)]
        att = [ph.sb("att", [CH, 8 * CH], BF16) for _ in range(2)]
        pat = ph.ps("pat", (128, 8 * CH)); po = [ph.ps("po", (128, 1024)) for _ in range(2)]; pkv = [ph.ps("pkv") for _ in range(2)]
        ti = 0; cidx = 0
        SK = ["S%d" % h for h in range(8)]
        for (s0, L, kind) in seqs:
            if kind == "s":
                DMA("sp", S[:, 0:4, :], sg_in[d_].rearrange("h p e -> p h e"), [], SK[0:4], "S0")
                DMA("sp", S[:, 4:8, :], sr_in[d_].rearrange("h p e -> p h e"), [], SK[4:8], "S0b")
            else:
                P.op("pool", lambda e: e.memset(S[:], 0.0), [], SK)
            CP("pool", Sb[:], S[:], SK, ["Sb0", "Sb1"])
            tl = [(t, min(t + 512, s0 + L)) for t in range(s0, s0 + L, 512)]
            if d_ == 1:
                tl = tl[::-1]
            for (t0, t1) in tl:
                b = ti % 2; ti += 1
                n = t1 - t0; ncw = n // CH
                B_ = "%d" % b
                DMA("sp", qg[b][:, :, 0:n], QDT[d_].rearrange("h p t -> p h t")[:, :, t0:t1], ["QDT"], ["qg" + B_], "qg" + B_)
                DMA("sp", kg[b][:, :, 0:n], KIT[d_].rearrange("h p t -> p h t")[:, :, t0:t1], ["KIT"], ["kg" + B_], "kg" + B_)
                DMA("sp", qr[b][:, :, 0:n], QRT.rearrange("h p t -> p h t")[:, :, t0:t1], ["QRT"], ["qr" + B_], "qr" + B_)
                DMA("sp", kr[b][:, :, 0:n], KRT.rearrange("h p t -> p h t")[:, :, t0:t1], ["KRT"], ["kr" + B_], "kr" + B_)
                DMA("act", ke[b][:, 0:ncw, :], KE[d_][t0:t1, :].rearrange("(c j) f -> j c f", j=CH), ["KE"], ["ke" + B_], "ke" + B_)
                DMA("act", krr[b][:, 0:ncw, :], KR[t0:t1, :].rearrange("(c j) f -> j c f", j=CH), ["KR"], ["krr" + B_], "krr" + B_)
                DMA("act", vg[b][:, 0:ncw, :], VG[t0:t1, :].rearrange("(c j) f -> j c f", j=CH), ["VG"], ["vg" + B_], "vg" + B_)
                DMA("act", vr[b][:, 0:ncw, :], VR[d_][t0:t1, :].rearrange("(c j) f -> j c f", j=CH), ["VR"], ["vr" + B_], "vr" + B_)
                DMA("sp", sdt[b][:, 0:ncw, :], SDD[d_][:, t0 // CH:t0 // CH + ncw, :], ["SDD"], ["sdt" + B_], "sdt" + B_)
                cl = list(range(ncw))
                if d_ == 1:
                    cl = cl[::-1]
                for ci in cl:
                    cb = cidx % 2; cidx += 1
                    CB = "%d" % cb
                    cs = slice(ci * CH, (ci + 1) * CH)
                    rows = slice(t0 + ci * CH, t0 + (ci + 1) * CH)
                    if d_ == 1:
                        DMA("sp", ofc[cb][:], OF[rows, :], ["OF"], ["ofc" + CB], "ofc" + CB)
                    for h in range(8):
                        K_ = kg[b][:, h, cs] if h < 4 else kr[b][:, h - 4, cs]
                        Q_ = qg[b][:, h, cs] if h < 4 else qr[b][:, h - 4, cs]
                        MM(pat[0:CH, h * CH:(h + 1) * CH], K_, Q_, True, True, ["kg" + B_, "kr" + B_, "qg" + B_, "qr" + B_], ["pat"])
                    TT("dve", att[cb][:], pat[0:CH, :], msk[:], ALU.mult, ["pat", "msk"], ["att" + CB])
                    for g in range(2):
                        for hh in range(4):
                            h = g * 4 + hh
                            V_ = (vg[b] if g == 0 else vr[b])[:, ci, hh * 256:(hh + 1) * 256]
                            Q_ = qg[b][:, hh, cs] if g == 0 else qr[b][:, hh, cs]
                            MM(po[g][0:CH, hh * 256:(hh + 1) * 256], att[cb][:, h * CH:(h + 1) * CH], V_, True, False,
                               ["att" + CB, "vg" + B_, "vr" + B_], ["po%d" % g])
                            MM(po[g][0:CH, hh * 256:(hh + 1) * 256], Q_, Sb[:, h, :], False, True, ["qg" + B_, "qr" + B_, "Sb%d" % g], ["po%d" % g])
                    if d_ == 0:
                        ACT(oc[cb][:, 0:1024], po[0][0:CH, :], AF.Copy, ["po0"], ["oc" + CB])
                    else:
                        TT("dve", oc[cb][:, 0:1024], po[0][0:CH, :], ofc[cb][:, 0:1024], ALU.add, ["po0", "ofc" + CB], ["oc" + CB])
                    for hh in range(4):
                        o_ = oc[cb][:, 1024 + hh * 256:1024 + (hh + 1) * 256]
                        if d_ == 0:
                            TSC("dve", o_, po[1][0:CH, hh * 256:(hh + 1) * 256], pcol[0:CH, hh:hh + 1], None, ALU.mult, None, ["po1", "pcol"], ["oc" + CB])
                        else:
                            STT("dve", o_, po[1][0:CH, hh * 256:(hh + 1) * 256], pcol[0:CH, hh:hh + 1], ofc[cb][:, 1024 + hh * 256:1024 + (hh + 1) * 256],
                                ALU.mult, ALU.add, ["po1", "pcol", "ofc" + CB], ["oc" + CB])
                    DMA("act", (OF if d_ == 0 else OT)[rows, :], oc[cb][:], ["oc" + CB], ["OF" if d_ == 0 else "OT"], "oc" + CB)
                    for g in range(2):
                        for hh in range(4):
                            h = g * 4 + hh
                            Ke_ = (ke[b] if g == 0 else krr[b])[:, ci, hh * 128:(hh + 1) * 128]
                            V_ = (vg[b] if g == 0 else vr[b])[:, ci, hh * 256:(hh + 1) * 256]
                            MM(pkv[hh // 2][:, (hh % 2) * 256:(hh % 2 + 1) * 256], Ke_, V_, True, True, ["ke" + B_, "krr" + B_, "vg" + B_, "vr" + B_], ["pkv%d" % (hh // 2)])
                        for hh in range(4):
                            h = g * 4 + hh
                            tb = h % 2
                            e1 = sdt[b][:, ci, hh:hh + 1] if g == 0 else eret[:, hh:hh + 1]
                            e2 = 1.0 if g == 0 else eret[:, hh:hh + 1]
                            ACT(tS[tb][:], S[:, h, :], AF.Copy, ["S%d" % h, "sdt" + B_, "eret"], ["tS%d" % tb], scale=e1)
                            STT("dve", S[:, h, :], pkv[hh // 2][:, (hh % 2) * 256:(hh % 2 + 1) * 256], e2, tS[tb][:], ALU.mult, ALU.add,
                                ["pkv%d" % (hh // 2), "tS%d" % tb, "eret"], ["S%d" % h])
                        ACT(Sb[:, g * 4:(g + 1) * 4, :], S[:, g * 4:(g + 1) * 4, :], AF.Copy, ["S%d" % (g * 4 + q_) for q_ in range(4)], ["Sb%d" % g])
            if kind != "s":
                j = int(kind[1])
                DMA("sp", o_sg[j, d_].rearrange("h p e -> p h e"), S[:, 0:4, :], SK[0:4], ["o_sg"], "So")
                DMA("sp", o_sr[j, d_].rearrange("h p e -> p h e"), S[:, 4:8, :], SK[4:8], ["o_sr"], "So2")
        ph.end()

    def phase_D():
        ph = Phase()
        c = consts(ph, bf_ident=True)
        m = load_mod(ph, 0, (2,))
        ngb = ph.sb("ngb", [128, 2048])
        DMA("sp", ngb[:, 0:1024], gng.partition_broadcast(128), [], ["ngb"], "ngb")
        DMA("sp", ngb[:, 1024:2048], rng_.partition_broadcast(128), ["ngb"], ["ngb"], "ngb")
        ot = [ph.sb("ot", [128, 2048]) for _ in range(2)]; gt = [ph.sb("gt", [128, 2048], BF16) for _ in range(2)]
        on_ = [ph.sb("on", [128, 2048]) for _ in range(2)]; sgm_ = [ph.sb("sgm", [128, 2048]) for _ in range(2)]
        y_ = [ph.sb("y", [128, 2048], BF16) for _ in range(2)]
        st8_ = [ph.sb("st8", [128, 4, 8]) for _ in range(2)]
        yT = ph.sb("yT", [128, KC, 512], BF16); x0 = ph.sb("x0", [128, KC, 512])
        wsl = [ph.sb("wsl", [128, KC, 512], BF16) for _ in range(2)]
        ptb = [ph.ps("ptb", (128, 512), BF16) for _ in range(2)]; pq = [ph.ps("pq") for _ in range(2)]
        Wv = Wabo.rearrange("(k p) c -> p k c", p=128)
        XAv = XA.rearrange("(k p) t -> p k t", p=128); XBv = XB.rearrange("(k p) t -> p k t", p=128)
        si = 0; wi = 0
        for t0 in range(0, T0, 512):
            r = row_of(t0)
            DMA("act", x0[:], XAv[:, :, t0:t0 + 512], ["XA"], ["x0"], "x0")
            for s in range(4):
                b = si % 2; si += 1
                B_ = "%d" % b
                on = on_[b]; sgm = sgm_[b]; y = y_[b]; st8 = st8_[b]
                rows = slice(t0 + s * 128, t0 + (s + 1) * 128)
                DMA("sp", ot[b][:], OT[rows, :], ["OT"], ["ot" + B_], "ot" + B_)
                DMA("sp", gt[b][:], GG[rows, :], ["GG"], ["gt" + B_], "gt" + B_)
                otv = ot[b][:].rearrange("p (h e) -> p h e", h=8)
                P.op("dve", lambda e, otv=otv, st8=st8: e.tensor_reduce(out=st8[:, 0, :], in_=otv, axis=mybir.AxisListType.X, op=ALU.add), ["ot" + B_], ["st8" + B_])
                TT("pool", on[:], ot[b][:], ot[b][:], ALU.mult, ["ot" + B_], ["on" + B_])
                onv = on[:].rearrange("p (h e) -> p h e", h=8)
                P.op("dve", lambda e, onv=onv, st8=st8: e.tensor_reduce(out=st8[:, 1, :], in_=onv, axis=mybir.AxisListType.X, op=ALU.add), ["on" + B_], ["st8" + B_])
                TSC("dve", st8[:, 0, :], st8[:, 0, :], 1.0 / 256, None, ALU.mult, None, ["st8" + B_], ["st8" + B_])
                TT("dve", st8[:, 2, :], st8[:, 0, :], st8[:, 0, :], ALU.mult, ["st8" + B_], ["st8" + B_])
                STT("dve", st8[:, 1, :], st8[:, 1, :], 1.0 / 256, st8[:, 2, :], ALU.mult, ALU.subtract, ["st8" + B_], ["st8" + B_])
                ACT(st8[:, 1, :], st8[:, 1, :], AF.Sqrt, ["st8" + B_, "c_eps"], ["st8" + B_], bias=c["eps"][:, 0:1])
                P.op("dve", lambda e, st8=st8: e.reciprocal(out=st8[:, 1, :], in_=st8[:, 1, :]), ["st8" + B_], ["st8" + B_])
                STT("dve", st8[:, 2, :], st8[:, 0, :], -1.0, st8[:, 1, :], ALU.mult, ALU.mult, ["st8" + B_], ["st8" + B_])
                for h in range(8):
                    ACT(on[:, h * 256:(h + 1) * 256], ot[b][:, h * 256:(h + 1) * 256], AF.Identity, ["ot" + B_, "st8" + B_], ["on" + B_],
                        scale=st8[:, 1, h:h + 1], bias=st8[:, 2, h:h + 1])
                ACT(sgm[:], gt[b][:], AF.Silu, ["gt" + B_], ["sgm" + B_])
                TT("pool", sgm[:], sgm[:], ngb[:], ALU.mult, ["sgm" + B_, "ngb"], ["sgm" + B_])
                TT("dve", y[:], on[:], sgm[:], ALU.mult, ["on" + B_, "sgm" + B_], ["y" + B_])
                for q in range(4):
                    pb = q % 2
                    for k4 in range(4):
                        kc = q * 4 + k4
                        TR(ptb[pb][:, k4 * 128:(k4 + 1) * 128], y[:, kc * 128:(kc + 1) * 128], c["idb"][:], ["y" + B_, "c_idb"], ["ptb%d" % pb])
                    CP("dve", yT[:, q * 4:(q + 1) * 4, s * 128:(s + 1) * 128], ptb[pb][:].rearrange("p (k t) -> p k t", k=4), ["ptb%d" % pb], ["yT"])
            for f4 in range(4):
                wb = wi % 2; wi += 1
                DMA("sp", wsl[wb][:], Wv[:, :, f4 * 512:(f4 + 1) * 512], ["wcast"], ["wsl%d" % wb], "wsl%d" % wb)
                for f1 in range(4):
                    fc = f4 * 4 + f1
                    pb = fc % 2
                    for kc in range(KC):
                        MM(pq[pb][:], wsl[wb][:, kc, f1 * 128:(f1 + 1) * 128], yT[:, kc, :], kc == 0, kc == KC - 1, ["wsl%d" % wb, "yT"], ["pq%d" % pb])
                    STT("dve", x0[:, fc, :], pq[pb][:], m[2][:, r, fc:fc + 1], x0[:, fc, :], ALU.mult, ALU.add, ["pq%d" % pb, "modT2", "x0"], ["x0"])
            DMA("act", XBv[:, :, t0:t0 + 512], x0[:], ["x0"], ["XB"], "x0s")
        ph.end()

    def phase_E(l, Xin, Xout, ranges, TT_):
        ph = Phase()
        c = consts(ph)
        A, B = make_AB(ph, l, n2g[l], 4, 3, "E_")
        m = load_mod(ph, l, (5,))
        cw = ph.sb("cw", [128, 3, FC])
        for j in range(3):
            DMA("sp", cw[:, j, :], fconv[l][j].rearrange("(c p) -> p c", p=128), ["cw"], ["cw"], "cw", slow=True)
        xt = ph.sb("xt", [128, KC, 512])
        P.op("pool", lambda e: e.memset(xt[:], 1.0), [], ["E_xt"])
        sq = [ph.sb("sq", [128, 512], BF16) for _ in range(2)]; rstd = ph.sb("rstd", [128, 512])
        tmp = [ph.sb("tmp", [128, 512]) for _ in range(2)]
        hT = ph.sb("hT", [128, KC, 512], BF16); actT = ph.sb("actT", [128, FC, 512], BF16)
        wa = [ph.sb("wa", [128, KC, 256], BF16) for _ in range(2)]; wb_ = [ph.sb("wb", [128, KC, 256], BF16) for _ in range(2)]
        wo = [ph.sb("wo", [128, FC, 256], BF16) for _ in range(2)]
        ac = [ph.sb("ac", [128, 512]) for _ in range(2)]; sa = [ph.sb("sa", [128, 512]) for _ in range(2)]
        psn = ph.ps("psn"); pa = [ph.ps("pa") for _ in range(2)]; pb = [ph.ps("pb") for _ in range(2)]; pq = [ph.ps("pq") for _ in range(2)]
        Wi = Wfi[l].rearrange("(k p) c -> p k c", p=128); Wo_ = Wfo[l].rearrange("(k p) c -> p k c", p=128)
        Xi = Xin.rearrange("(k p) t -> p k t", p=128); Xo = Xout.rearrange("(k p) t -> p k t", p=128)
        wi = 0; woi = 0; ci_ = 0
        clE = cast_list(["mi", "uq", "ukv", "mo", "fi1", "fo1"]) if l == 0 else []
        ntile = sum(len(split(L, 510)) for (_, L, _, _) in ranges)
        perE = (len(clE) + ntile - 1) // ntile
        for (s0, L, r, ext) in ranges:
            for (s, e_) in split(L, 510):
                n = e_ - s; NW = n + 2
                emit_casts(clE, perE)
                lo = max(s - 1, 0); hi = min(e_ + 1, L + ext)
                off = lo - (s - 1)
                DMA("act", xt[:, :, off:off + hi - lo], Xi[:, :, s0 + lo:s0 + hi], ["Xin"], ["E_xt"], "E_xt")
                fm_norm_mod(ph, c, xt, NW, A, B, r, hT, ["E_xt"], "E_", (sq, psn, rstd), tmp)
                if s == 0:
                    P.op("dve", lambda e: e.memset(hT[:, :, 0:1], 0.0), ["E_hT"], ["E_hT"])
                if e_ == L and not ext:
                    P.op("dve", lambda e, NW=NW: e.memset(hT[:, :, NW - 1:NW], 0.0), ["E_hT"], ["E_hT"])
                for c2 in range(FC // 2):
                    w_ = wi % 2; wi += 1
                    W_ = "%d" % w_
                    DMA("sp", wa[w_][:], Wi[:, :, c2 * 256:(c2 + 1) * 256], ["wcast"], ["wa" + W_], "wa" + W_)
                    DMA("sp", wb_[w_][:], Wi[:, :, DFF + c2 * 256:DFF + (c2 + 1) * 256], ["wcast"], ["wb" + W_], "wb" + W_)
                    for cc in range(2):
                        ch = c2 * 2 + cc
                        p_ = ci_ % 2; ci_ += 1
                        P_ = "%d" % p_
                        for kc in range(KC):
                            MM(pa[p_][:, 0:NW], wa[w_][:, kc, cc * 128:(cc + 1) * 128], hT[:, kc, 0:NW], kc == 0, kc == KC - 1, ["wa" + W_, "E_hT"], ["pa" + P_])
                        for kc in range(KC):
                            MM(pb[p_][:, 0:NW], wb_[w_][:, kc, cc * 128:(cc + 1) * 128], hT[:, kc, 0:NW], kc == 0, kc == KC - 1, ["wb" + W_, "E_hT"], ["pb" + P_])
                        TSC("dve", ac[p_][:, 0:n], pa[p_][:, 1:n + 1], cw[:, 1, ch:ch + 1], None, ALU.mult, None, ["pa" + P_, "cw"], ["ac" + P_])
                        STT("dve", ac[p_][:, 0:n], pa[p_][:, 0:n], cw[:, 0, ch:ch + 1], ac[p_][:, 0:n], ALU.mult, ALU.add, ["pa" + P_, "cw", "ac" + P_], ["ac" + P_])
                        STT("dve", ac[p_][:, 0:n], pa[p_][:, 2:n + 2], cw[:, 2, ch:ch + 1], ac[p_][:, 0:n], ALU.mult, ALU.add, ["pa" + P_, "cw", "ac" + P_], ["ac" + P_])
                        ACT(sa[p_][:, 0:n], ac[p_][:, 0:n], AF.Silu, ["ac" + P_], ["sa" + P_])
                        TT("dve", actT[:, ch, 0:n], sa[p_][:, 0:n], pb[p_][:, 1:n + 1], ALU.mult, ["sa" + P_, "pb" + P_], ["actT"])
                for f2 in range(8):
                    w_ = woi % 2; woi += 1
                    W_ = "%d" % w_
                    for k0 in range(0, FC, 11):
                        DMA("sp", wo[w_][:, k0:k0 + 11, :], Wo_[:, k0:k0 + 11, f2 * 256:(f2 + 1) * 256], ["wcast", "wo" + W_], ["wo" + W_], "wo" + W_)
                    for f1 in range(2):
                        fc = f2 * 2 + f1
                        q_ = fc % 2
                        for kc in range(FC):
                            MM(pq[q_][:, 0:n], wo[w_][:, kc, f1 * 128:(f1 + 1) * 128], actT[:, kc, 0:n], kc == 0, kc == FC - 1, ["wo" + W_, "actT"], ["pq%d" % q_])
                        STT("dve", xt[:, fc, 1:n + 1], pq[q_][:, 0:n], m[5][:, r, fc:fc + 1], xt[:, fc, 1:n + 1], ALU.mult, ALU.add,
                            ["pq%d" % q_, "modT5", "E_xt"], ["E_xt"])
                DMA("act", Xo[:, :, s0 + s:s0 + e_], xt[:, :, 1:n + 1], ["E_xt"], ["Xout"], "E_xs")
        ph.end()


    TK = PAST + T0
    QNT = scr("QNT", [16, 128, T0], BF16); QPT = scr("QPT", [16, 64, T0], BF16)
    KNT = scr("KNT", [16, 128, TK], BF16); KPT = scr("KPT", [64, TK], BF16); VV = scr("VV", [16, TK, 128], BF16)
    AOT = scr("AOT", [D, T0], BF16)
    MLA_SCALE = 192 ** -0.5

    def phase_F():
        ph = Phase()
        c = consts(ph)
        A, B = make_AB(ph, 1, n1g[1], 1, 0, "F_")
        qg_ = load_vecT(ph, qng, 4, "qng"); kg_ = load_vecT(ph, kvng, 4, "kvng")
        NT = 256
        wmi = ph.sb("wmi", [128, KC, 1088], BF16); wmr = ph.sb("wmr", [128, KC, 64], BF16)
        wuq = ph.sb("wuq", [128, 4, 3072], BF16); wur = ph.sb("wur", [128, 4, 16, 64], BF16)
        wkv = ph.sb("wkv", [128, 4, 4096], BF16)
        for k0 in range(0, KC, 4):
            DMA("sp", wmi[:, k0:k0 + 4, :], Wmi.rearrange("(k p) c -> p k c", p=128)[:, k0:k0 + 4, :], ["wcast", "wmi"], ["wmi"], "wmi")
        DMA("sp", wuq[:], Wuq.rearrange("(k p) c -> p k c", p=128), ["wcast"], ["wuq"], "wuq")
        DMA("sp", wkv[:], Wukv.rearrange("(k p) c -> p k c", p=128), ["wcast"], ["wkv"], "wkv")
        wuqv = wuq[:].rearrange("p k (h c) -> p k h c", c=192)
        for (d0, s0_, sign) in [(0, 16, -1.0), (16, 0, 1.0), (32, 48, -1.0), (48, 32, 1.0)]:
            TSC("dve", wmr[:, :, d0:d0 + 16], wmi[:, :, 1024 + s0_:1024 + s0_ + 16], sign, None, ALU.mult, None, ["wmi"], ["wmr"])
            for k in range(4):
                TSC("dve", wur[:, k, :, d0:d0 + 16], wuqv[:, k, :, 128 + s0_:128 + s0_ + 16], sign, None, ALU.mult, None, ["wuq"], ["wur"])
        xt_ = [ph.sb("xt", [128, KC, NT]) for _ in range(2)]; sq = [ph.sb("sq", [128, 512], BF16) for _ in range(2)]; rstd = ph.sb("rstd", [128, 512])
        tmp = [ph.sb("tmp", [128, 512]) for _ in range(2)]; hT_ = [ph.sb("hT", [128, KC, NT], BF16) for _ in range(2)]
        cq = ph.sb("cq", [128, 4, NT]); ckv = ph.sb("ckv", [128, 4, NT])
        cqn = ph.sb("cqn", [128, 4, NT], BF16); ckn = ph.sb("ckn", [128, 4, NT], BF16)
        rope = ph.sb("rope", [64, 2, NT]); t1 = ph.sb("t1", [64, NT]); t2 = ph.sb("t2", [64, NT])
        kraw = ph.sb("kraw", [64, NT])
        stb = [ph.sb("stb", [128, 512], BF16) for _ in range(4)]
        tok = [ph.sb("tok", [128, 512]) for _ in range(2)]
        cin = ph.sb("cin", [128, 512]); kin = ph.sb("kin", [128, 64])
        psn = ph.ps("psn"); pp = [ph.ps("pp") for _ in range(3)]; pr_ = [ph.ps("pr") for _ in range(2)]; ptr = [ph.ps("ptr") for _ in range(2)]
        ppi = [0]; sbi = [0]; pri = [0]

        def nextp():
            i = ppi[0] % 3; ppi[0] += 1
            return pp[i], "pp%d" % i

        def nexts():
            i = sbi[0] % 4; sbi[0] += 1
            return stb[i], "stb%d" % i

        def expand_kv(N, kbase):
            for h in range(16):
                p_, pk = nextp()
                for kc in range(4):
                    MM(p_[:, 0:N], wkv[:, kc, h * 256:h * 256 + 128], ckn[:, kc, 0:N], kc == 0, kc == 3, ["wkv", "ckn"], [pk])
                sb_, sk = nexts()
                if h % 2 == 0:
                    ACT(sb_[:, 0:N], p_[:, 0:N], AF.Copy, [pk], [sk])
                else:
                    CP("dve", sb_[:, 0:N], p_[:, 0:N], [pk], [sk])
                DMA("act", KNT[h][:, kbase:kbase + N], sb_[:, 0:N], [sk], ["KNT"], sk)
            wv = wkv[:].rearrange("p k (h c) -> p k h c", c=256)
            for s in range(N // 128):
                for h4 in range(4):
                    p_, pk = nextp()
                    for kc in range(4):
                        MM(p_[:].rearrange("p (h c) -> p h c", c=128), ckn[:, kc, s * 128:(s + 1) * 128], wv[:, kc, h4 * 4:(h4 + 1) * 4, 128:256],
                           kc == 0, kc == 3, ["wkv", "ckn"], [pk])
                    sb_, sk = nexts()
                    CP("dve", sb_[:], p_[:], [pk], [sk])
                    DMA("act", VV[h4 * 4:(h4 + 1) * 4, kbase + s * 128:kbase + (s + 1) * 128, :].rearrange("h t e -> t h e"),
                        sb_[:].rearrange("p (h e) -> p h e", e=128), [sk], ["VV"], sk)

        for s in range(PAST // 128):
            DMA("sp", cin[:], cckv[s * 128:(s + 1) * 128, :], [], ["cin"], "cin")
            DMA("sp", kin[:], ckpe[s * 128:(s + 1) * 128, :], [], ["kin"], "kin")
            p_, pk = nextp()
            for k4 in range(4):
                TR(p_[:, k4 * 128:(k4 + 1) * 128], cin[:, k4 * 128:(k4 + 1) * 128], c["id"][:], ["cin", "c_id"], [pk])
            CP("dve", ckn[:, :, s * 128:(s + 1) * 128], p_[:].rearrange("p (k t) -> p k t", k=4), [pk], ["ckn"])
            p2, pk2 = nextp()
            TR(p2[0:64, 0:128], kin[:, :], c["id"][:], ["kin", "c_id"], [pk2])
            sb_, sk = nexts()
            CP("dve", sb_[0:64, 0:128], p2[0:64, 0:128], [pk2], [sk])
            DMA("act", KPT[:, s * 128:(s + 1) * 128], sb_[0:64, 0:128], [sk], ["KPT"], sk)
        expand_kv(PAST, 0)
        XAv = XA.rearrange("(k p) t -> p k t", p=128)
        for t0 in range(0, T0, NT):
            N = NT; t1_ = t0 + N
            r = row_of(t0); lat = t0 < TS
            tb_ = (t0 // NT) % 2
            xt = xt_[tb_]; hT = hT_[tb_]; HK = "F_hT%d" % tb_; XK = "F_xt%d" % tb_
            DMA("act", xt[:], XAv[:, :, t0:t1_], ["XA"], [XK], XK)
            if lat:
                DMA("sp", rope[:], c_rope[:, :, t0:t1_].rearrange("a d t -> d a t"), [], ["rope"], "rope")
            fm_norm_mod(ph, c, xt, N, A, B, r, hT, [XK], "F_", (sq, psn, rstd), tmp, hkey=HK)
            for j in range(8):
                p_, pk = nextp()
                for kc in range(KC):
                    MM(p_[:, 0:N], wmi[:, kc, j * 128:(j + 1) * 128], hT[:, kc, 0:N], kc == 0, kc == KC - 1, ["wmi", HK], [pk])
                dst = (cq if j < 4 else ckv)[:, j % 4, 0:N]
                ACT(dst, p_[:, 0:N], AF.Copy, [pk], ["cq" if j < 4 else "ckv"])

            def roped(praw, prot, pkr, pkt, out_bf, outkey):
                if lat:
                    TT("dve", t1[:, 0:N], praw, rope[:, 0, 0:N], ALU.mult, [pkr, "rope"], ["t1"])
                    TT("dve", t2[:, 0:N], prot, rope[:, 1, 0:N], ALU.mult, [pkt, "rope"], ["t2"])
                    TT("pool", out_bf, t1[:, 0:N], t2[:, 0:N], ALU.add, ["t1", "t2"], [outkey])
                else:
                    CP("dve", out_bf, praw, [pkr], [outkey])

            a_ = pri[0] % 2; pri[0] += 1
            for kc in range(KC):
                MM(pr_[a_][0:64, 0:N], wmi[:, kc, 1024:1088], hT[:, kc, 0:N], kc == 0, kc == KC - 1, ["wmi", HK], ["pr%d" % a_])
            for kc in range(KC):
                MM(pr_[a_][0:64, 256:256 + N], wmr[:, kc, :], hT[:, kc, 0:N], kc == 0, kc == KC - 1, ["wmr", HK], ["pr%d" % a_])
            sb_, sk = nexts()
            roped(pr_[a_][0:64, 0:N], pr_[a_][0:64, 256:256 + N], "pr%d" % a_, "pr%d" % a_, sb_[0:64, 0:N], sk)
            DMA("act", KPT[:, PAST + t0:PAST + t1_], sb_[0:64, 0:N], [sk], ["KPT"], sk)
            if not lat:
                CP("dve", kraw[:, 0:N], pr_[a_][0:64, 0:N], ["pr%d" % a_], ["kraw"])
                for s in range(N // 128):
                    b = s % 2
                    TR(ptr[b][:, 0:64], kraw[:, s * 128:(s + 1) * 128], c["id"][0:64, 0:64], ["kraw", "c_id"], ["ptr%d" % b])
                    CP("dve", tok[b][:, 0:64], ptr[b][:, 0:64], ["ptr%d" % b], ["tok%d" % b])
                    DMA("act", o_kpe[t0 - TS + s * 128:t0 - TS + (s + 1) * 128, :], tok[b][:, 0:64], ["tok%d" % b], ["o_kpe"], "tok%d" % b)
            fm_rstd(ph, c, ckv, 4, N, 512, ["ckv"], "F_", (sq, psn, rstd))
            for kc in range(4):
                STT("dve", ckv[:, kc, 0:N], ckv[:, kc, 0:N], kg_[:, kc:kc + 1], rstd[:, 0:N], ALU.mult, ALU.mult, ["ckv", "kvng", "F_rstd"], ["ckv"])
            CP("pool", ckn[:, :, 0:N], ckv[:, :, 0:N], ["ckv"], ["ckn"])
            if not lat:
                for s in range(N // 128):
                    b = s % 2
                    for k4 in range(4):
                        TR(ptr[b][:, k4 * 128:(k4 + 1) * 128], ckv[:, k4, s * 128:(s + 1) * 128], c["id"][:], ["ckv", "c_id"], ["ptr%d" % b])
                    CP("dve", tok[b][:], ptr[b][:], ["ptr%d" % b], ["tok%d" % b])
                    DMA("act", o_ckv[t0 - TS + s * 128:t0 - TS + (s + 1) * 128, :], tok[b][:], ["tok%d" % b], ["o_ckv"], "tok%d" % b)
            expand_kv(N, PAST + t0)
            if lat and t0 >= LQE:
                continue
            fm_rstd(ph, c, cq, 4, N, 512, ["cq"], "F_", (sq, psn, rstd))
            for kc in range(4):
                STT("dve", cqn[:, kc, 0:N], cq[:, kc, 0:N], qg_[:, kc:kc + 1], rstd[:, 0:N], ALU.mult, ALU.mult, ["cq", "qng", "F_rstd"], ["cqn"])
            for h in range(16):
                p_, pk = nextp()
                for kc in range(4):
                    MM(p_[:, 0:N], wuq[:, kc, h * 192:h * 192 + 128], cqn[:, kc, 0:N], kc == 0, kc == 3, ["wuq", "cqn"], [pk])
                sb_, sk = nexts()
                ACT(sb_[:, 0:N], p_[:, 0:N], AF.Copy, [pk], [sk])
                DMA("act", QNT[h][:, t0:t1_], sb_[:, 0:N], [sk], ["QNT"], sk)
                a_ = pri[0] % 2; pri[0] += 1
                for kc in range(4):
                    MM(pr_[a_][0:64, 0:N], wuq[:, kc, h * 192 + 128:h * 192 + 192], cqn[:, kc, 0:N], kc == 0, kc == 3, ["wuq", "cqn"], ["pr%d" % a_])
                for kc in range(4):
                    MM(pr_[a_][0:64, 256:256 + N], wur[:, kc, h, :], cqn[:, kc, 0:N], kc == 0, kc == 3, ["wur", "cqn"], ["pr%d" % a_])
                sb_, sk = nexts()
                roped(pr_[a_][0:64, 0:N], pr_[a_][0:64, 256:256 + N], "pr%d" % a_, "pr%d" % a_, sb_[0:64, 0:N], sk)
                DMA("act", QPT[h][:, t0:t1_], sb_[0:64, 0:N], [sk], ["QPT"], sk)
        ph.end()

    def phase_G():
        ph = Phase()
        c = consts(ph)
        NKM = PAST + TS
        kn = [ph.sb("kn", [128, NKM], BF16) for _ in range(2)]
        vv = [ph.sb("vv", [128, NKM // 128, 128], BF16) for _ in range(2)]
        kp = ph.sb("kp", [64, NKM], BF16)
        qn = [ph.sb("qn", [128, 512], BF16) for _ in range(2)]; qp = [ph.sb("qp", [64, 512], BF16) for _ in range(2)]
        pT = [ph.sb("pT", [128, 512], BF16) for _ in range(3)]
        rec = ph.sb("rec", [128, 512]); ao = [ph.sb("ao", [128, 512], BF16) for _ in range(2)]

        pss = [ph.ps("pss") for _ in range(3)]; po = [ph.ps("po") for _ in range(2)]; pd = [ph.ps("pd") for _ in range(2)]
        steps = []
        hi_ = 0; qi = 0
        for (s0, L, kind) in seqs:
            if kind == "s":
                k0, NK = 0, PAST + TS
            else:
                k0, NK = PAST + s0, L
            NKT = NK // 128
            first_seq = True
            for h in range(16):
                hb = hi_ % 2; hi_ += 1
                first_h = True
                for (s_, e_) in split(LQE if kind == "s" else L, 512):
                    qb = qi % 2; qi += 1
                    for kt in range(NKT):
                        steps.append(dict(s0=s0, k0=k0, NK=NK, NKT=NKT, h=h, hb=hb, qb=qb, s=s_, e=e_, kt=kt,
                                          ld_seq=first_seq and kt == 0, ld_h=first_h and kt == 0, ld_q=kt == 0))
                        first_seq = False; first_h = False

        def emit_S(i, st):
            sb = i % 3
            SB = "%d" % sb; HB = "%d" % st["hb"]; QB = "%d" % st["qb"]
            hb, qb, h, k0, NK, NKT, s0 = st["hb"], st["qb"], st["h"], st["k0"], st["NK"], st["NKT"], st["s0"]
            n = st["e"] - st["s"]
            if st["ld_seq"]:
                DMA("sp", kp[:, 0:NK], KPT[:, k0:k0 + NK], ["KPT"], ["kp"], "kp")
            if st["ld_h"]:
                DMA("sp", kn[hb][:, 0:NK], KNT[h][:, k0:k0 + NK], ["KNT"], ["kn" + HB], "kn" + HB)
                for kt0 in range(0, NKT, 8):
                    kt1 = min(NKT, kt0 + 8)
                    DMA("act", vv[hb][:, kt0:kt1, :], VV[h][k0 + kt0 * 128:k0 + kt1 * 128, :].rearrange("(t p) e -> p t e", p=128),
                        ["VV", "vv" + HB], ["vv" + HB], "vv" + HB)
            if st["ld_q"]:
                DMA("sp", qn[qb][:, 0:n], QNT[h][:, s0 + st["s"]:s0 + st["e"]], ["QNT"], ["qn" + QB], "qn" + QB)
                DMA("sp", qp[qb][:, 0:n], QPT[h][:, s0 + st["s"]:s0 + st["e"]], ["QPT"], ["qp" + QB], "qp" + QB)
            ks = slice(st["kt"] * 128, (st["kt"] + 1) * 128)
            MM(pss[sb][:, 0:n], kn[hb][:, ks], qn[qb][:, 0:n], True, False, ["kn" + HB, "qn" + QB], ["pss" + SB])
            MM(pss[sb][:, 0:n], kp[:, ks], qp[qb][:, 0:n], False, True, ["kp", "qp" + QB], ["pss" + SB])
            ACT(pT[sb][:, 0:n], pss[sb][:, 0:n], AF.Exp, ["pss" + SB], ["pT" + SB], scale=MLA_SCALE)

        def emit_PV(i, st):
            sb = i % 3
            SB = "%d" % sb; HB = "%d" % st["hb"]; QB = "%d" % st["qb"]
            hb, qb, h, NKT, s0, kt = st["hb"], st["qb"], st["h"], st["NKT"], st["s0"], st["kt"]
            n = st["e"] - st["s"]
            MM(po[qb][:, 0:n], vv[hb][:, kt, :], pT[sb][:, 0:n], kt == 0, kt == NKT - 1, ["vv" + HB, "pT" + SB], ["po" + QB])
            MM(pd[qb][:, 0:n], c["ones"][:], pT[sb][:, 0:n], kt == 0, kt == NKT - 1, ["c_ones", "pT" + SB], ["pd" + QB])
            if kt == NKT - 1:
                P.op("dve", lambda e, qb=qb, n=n: e.reciprocal(out=rec[:, 0:n], in_=pd[qb][:, 0:n]), ["pd" + QB], ["rec"])
                TT("dve", ao[qb][:, 0:n], po[qb][:, 0:n], rec[:, 0:n], ALU.mult, ["po" + QB, "rec"], ["ao" + QB])
                DMA("act", AOT[h * 128:(h + 1) * 128, s0 + st["s"]:s0 + st["e"]], ao[qb][:, 0:n], ["ao" + QB], ["AOT"], "ao" + QB)

        emit_S(0, steps[0])
        for i in range(len(steps)):
            if i + 1 < len(steps):
                emit_S(i + 1, steps[i + 1])
            emit_PV(i, steps[i])
        ph.end()

    def phase_H():
        ph = Phase()
        m = load_mod(ph, 1, (2,))
        aT = ph.sb("aT", [128, KC, 512], BF16); x0 = ph.sb("x0", [128, KC, 512])
        wsl = [ph.sb("wsl", [128, KC, 512], BF16) for _ in range(2)]
        pq = [ph.ps("pq") for _ in range(2)]
        Wv = Wmo.rearrange("(k p) c -> p k c", p=128)
        XAv = XA.rearrange("(k p) t -> p k t", p=128); XBv = XB.rearrange("(k p) t -> p k t", p=128)
        wi = 0
        hr = [(a_, b_) for (a_, b_) in split(LQE, 512)] + [(t, min(t + 512, T0)) for t in range(TS, T0, 512)]
        for (t0, t1h) in hr:
            r = row_of(t0); n = t1h - t0
            DMA("act", x0[:, :, 0:n], XAv[:, :, t0:t1h], ["XA"], ["x0"], "x0")
            DMA("sp", aT[:, :, 0:n], AOT.rearrange("(k p) t -> p k t", p=128)[:, :, t0:t1h], ["AOT"], ["aT"], "aT")
            for f4 in range(4):
                wb = wi % 2; wi += 1
                DMA("sp", wsl[wb][:], Wv[:, :, f4 * 512:(f4 + 1) * 512], ["wcast"], ["wsl%d" % wb], "wsl%d" % wb)
                for f1 in range(4):
                    fc = f4 * 4 + f1
                    pb = fc % 2
                    for kc in range(KC):
                        MM(pq[pb][:, 0:n], wsl[wb][:, kc, f1 * 128:(f1 + 1) * 128], aT[:, kc, 0:n], kc == 0, kc == KC - 1, ["wsl%d" % wb, "aT"], ["pq%d" % pb])
                    STT("dve", x0[:, fc, 0:n], pq[pb][:, 0:n], m[2][:, r, fc:fc + 1], x0[:, fc, 0:n], ALU.mult, ALU.add, ["pq%d" % pb, "modT2", "x0"], ["x0"])
            DMA("act", XBv[:, :, t0:t1h], x0[:, :, 0:n], ["x0"], ["XB"], "x0s")
        ph.end()

    def phase_Z():
        ph = Phase()
        c = consts(ph)
        g = load_vecT(ph, fng, KC, "fng")
        xt = ph.sb("xt", [128, KC, 512]); sq = [ph.sb("sq", [128, 512], BF16) for _ in range(2)]; rstd = ph.sb("rstd", [128, 512])
        yt = [ph.sb("yt", [128, 2048]) for _ in range(2)]
        psn = ph.ps("psn"); pt = [ph.ps("pt") for _ in range(4)]
        XAv = XA.rearrange("(k p) t -> p k t", p=128)
        yi = 0
        for t0 in list(range(0, LQ, 512)) + list(range(TS, T0, 512)):
            DMA("act", xt[:], XAv[:, :, t0:t0 + 512], ["XA"], ["Z_xt"], "Z_xt")
            fm_rstd(ph, c, xt, KC, 512, D, ["Z_xt"], "Z_", (sq, psn, rstd))
            for kc in range(KC):
                STT("dve", xt[:, kc, :], xt[:, kc, :], g[:, kc:kc + 1], rstd[:], ALU.mult, ALU.mult, ["Z_xt", "fng", "Z_rstd"], ["Z_xt"])
            for s in range(4):
                b = yi % 2; yi += 1
                for q in range(4):
                    for k4 in range(4):
                        kc = q * 4 + k4
                        TR(pt[q][:, k4 * 128:(k4 + 1) * 128], xt[:, kc, s * 128:(s + 1) * 128], c["id"][:], ["Z_xt", "c_id"], ["pt%d" % q])
                    if q % 2 == 0:
                        CP("dve", yt[b][:, q * 512:(q + 1) * 512], pt[q][:], ["pt%d" % q], ["yt%d" % b])
                    else:
                        ACT(yt[b][:, q * 512:(q + 1) * 512], pt[q][:], AF.Copy, ["pt%d" % q], ["yt%d" % b])
                tt = t0 + s * 128
                dst = y_s[tt:tt + 128, :] if tt < TS else y_p[tt - TS:tt - TS + 128, :]
                DMA("sp", dst, yt[b][:], ["yt%d" % b], ["y"], "yt%d" % b)
        ph.end()

    L0_ranges = [(s0, L, 0 if k == "s" else 1, 0) for (s0, L, k) in seqs]
    L1_ranges = [((s0, LQ, 0, EXT) if k == "s" else (s0, L, 1, 0)) for (s0, L, k) in seqs]
    phases = [("T", phase_WMT), ("A", phase_A), ("B", lambda: phase_scan(0)), ("C", lambda: phase_scan(1)),
              ("D", phase_D), ("E", lambda: phase_E(0, XB, XA, L0_ranges, T0)),
              ("F", phase_F), ("G", phase_G), ("H", phase_H), ("I", lambda: phase_E(1, XB, XA, L1_ranges, T0)), ("Z", phase_Z)]
    for nm, fn in phases:
        fn()
        if nm == stop_after:
            break
    if dump:
        ph = Phase()
        t = ph.sb("dd", [128, KC, 512])
        XD = XA if dump == "XA" else XB
        for t0 in range(0, T0, 512):
            DMA("sp", t[:], XD.rearrange("(k p) t -> p k t", p=128)[:, :, t0:t0 + 512], ["XA"], ["dd"], "dd")
            DMA("sp", dbg.rearrange("(k p) t -> p k t", p=128)[:, :, t0:t0 + 512], t[:], ["dd"], ["dbg"], "dd2")
        ph.end()
    st.close()
    return nc, P


def host_consts(TS, mir=False, CH=128):
    c = {}
    c["c_ident"] = np.eye(128, dtype=np.float32)
    j = np.arange(128)[:, None]; i = np.arange(128)[None, :]
    same = (j // CH) == (i // CH)
    tri = np.zeros((4, 128, 128), np.float32)
    tri[0] = np.where(same & (j <= i), -1 / 16.0, 0.0)
    tri[1] = np.where(same & (j >= i), -1 / 16.0, 0.0)
    tri[2] = np.where(same & (j > i), -1 / 16.0, 0.0)
    tri[3] = np.where(same & (j < i), -1 / 16.0, 0.0)
    c["c_tri"] = tri
    jj = np.arange(CH)[:, None]; ii = np.arange(CH)[None, :]
    m = np.zeros((2, CH, 8 * CH), np.float32)
    m[0] = np.tile((ii >= jj).astype(np.float32), (1, 8))
    m[1] = np.tile((ii <= jj).astype(np.float32), (1, 8))
    c["c_mask"] = m
    p = np.arange(128)
    pos = np.zeros((128, 4), np.float32)
    pos[:, 0] = (p % CH) + 1
    pos[:, 1] = CH - (p % CH)
    c["c_pos"] = pos
    t = np.arange(TS)
    if mir:
        t = TS - 1 - t
    inv = 10000.0 ** (-np.arange(16, dtype=np.float64) * 2.0 / 32)
    d = np.arange(64)
    posd = np.where((d // 32)[:, None] == 0, (t // 64)[None, :], (t % 64)[None, :]).astype(np.float64)
    ang = posd * inv[d % 16][:, None]
    c["c_rope"] = np.stack([np.cos(ang), np.sin(ang)], 0).astype(np.float32)
    return c


def prep_core(inp, cid, cfg):
    TS, TP = cfg["TS"], cfg["TP"]
    mir = bool(cfg.get("mirror", False)) and (cid % 2 == 1)
    b = cid // 2
    p0 = 2 * cid
    f = lambda a: np.ascontiguousarray(np.asarray(a, dtype=np.float32))
    m = {}
    if mir:
        m["xs"] = f(inp["x_sample"][b][::-1])
        m["xp"] = f(inp["x_prompt"][p0:p0 + 2][:, ::-1].reshape(2 * TP, D))
        m["sg"] = f(inp["state_gla"][b, 0][::-1]); m["sr"] = f(inp["state_ret"][b, 0][::-1])
    else:
        m["xs"] = f(inp["x_sample"][b])
        m["xp"] = f(inp["x_prompt"][p0:p0 + 2].reshape(2 * TP, D))
        m["sg"] = f(inp["state_gla"][b, 0]); m["sr"] = f(inp["state_ret"][b, 0])
    m["cckv"] = f(inp["cache_ckv"][b, 0]); m["ckpe"] = f(inp["cache_kpe"][b, 0])
    m["cond"] = f(np.stack([inp["c"][b], inp["c_ctx"]], 0))
    for k in ("mod_w", "mod_b", "norm1_g", "norm2_g", "ffn_w_in", "ffn_conv", "ffn_w_out", "final_norm_g"):
        m[k] = f(inp[k])
    for k in ("ab_w_in", "gla_gate_w2", "gla_gate_b", "gla_norm_g", "ret_norm_g", "ab_w_out", "mla_w_in", "mla_q_norm_g",
              "mla_w_uq", "mla_kv_norm_g", "mla_w_ukv", "mla_w_out"):
        m[k] = f(inp[k][0])
    m["ret_decay"] = f(inp["ret_decay"][0].reshape(8))
    if mir:
        w = np.array(m["ab_w_in"])
        w[:, C_GLR:C_GLR + 16] = m["ab_w_in"][:, C_GLR + 16:C_GLR + 32]
        w[:, C_GLR + 16:C_GLR + 32] = m["ab_w_in"][:, C_GLR:C_GLR + 16]
        m["ab_w_in"] = w
        m["gla_gate_w2"] = f(m["gla_gate_w2"][::-1]); m["gla_gate_b"] = f(m["gla_gate_b"][::-1])
        m["ret_decay"] = f(inp["ret_decay"][0][::-1].reshape(8))
        m["ffn_conv"] = f(m["ffn_conv"][:, ::-1, :])
    m.update(host_consts(TS, mir, cfg.get("CH", 128)))
    return m


_CACHE = {}


def kernel(**inputs):
    inputs = {k: np.asarray(v) for k, v in inputs.items()}
    TS = inputs["x_sample"].shape[1]
    TP = inputs["x_prompt"].shape[1]
    PAST = inputs["cache_ckv"].shape[2]
    NB = inputs["x_sample"].shape[0]
    BP = inputs["x_prompt"].shape[0]
    n = 8
    cfg = dict(TS=TS, TP=TP, PAST=PAST, LQ=TS // 2, mirror=True)
    key = (TS, TP, PAST)
    if key not in _CACHE:
        _CACHE[key] = build(cfg)
    nc, _ = _CACHE[key]
    in_maps = [prep_core(inputs, cid, cfg) for cid in range(n)]
    res = run_bass_kernel_spmd(nc, in_maps, core_ids=list(range(n)))
    R_ = res.results
    y_prompt = np.zeros((BP, TP, D), np.float32)
    y_sample = np.zeros((NB, TS, D), np.float32)
    ns_gla = np.zeros((BP, 1, 2, 4, 128, 256), np.float32)
    ns_ret = np.zeros((BP, 1, 2, 4, 128, 256), np.float32)
    n_ckv = np.zeros((BP, 1, TP, 512), np.float32)
    n_kpe = np.zeros((BP, 1, TP, 64), np.float32)
    for cid in range(n):
        r = R_[cid]
        b = cid // 2
        half = TS // 2
        p0 = 2 * cid
        if cid % 2 == 0:
            y_sample[b, :half] = r["y_s"][:half]
            y_prompt[p0:p0 + 2] = r["y_p"].reshape(2, TP, D)
            ns_gla[p0:p0 + 2, 0] = r["ns_gla"]
            ns_ret[p0:p0 + 2, 0] = r["ns_ret"]
            n_ckv[p0:p0 + 2, 0] = r["n_ckv"].reshape(2, TP, 512)
            n_kpe[p0:p0 + 2, 0] = r["n_kpe"].reshape(2, TP, 64)
        else:
            y_sample[b, half:] = r["y_s"][:half][::-1]
            y_prompt[p0:p0 + 2] = r["y_p"].reshape(2, TP, D)[:, ::-1]
            ns_gla[p0:p0 + 2, 0] = r["ns_gla"][:, ::-1]
            ns_ret[p0:p0 + 2, 0] = r["ns_ret"][:, ::-1]
            n_ckv[p0:p0 + 2, 0] = r["n_ckv"].reshape(2, TP, 512)[:, ::-1]
            n_kpe[p0:p0 + 2, 0] = r["n_kpe"].reshape(2, TP, 64)[:, ::-1]
    return (y_prompt, y_sample, ns_gla, ns_ret, n_ckv, n_kpe)
```

```python
import math
from contextlib import ExitStack
import numpy as np
import ml_dtypes
import concourse.bass as bass
import concourse.mybir as mybir
from concourse.bass_utils import run_bass_kernel_spmd

F32 = mybir.dt.float32
BF16 = mybir.dt.bfloat16
AF = mybir.ActivationFunctionType
ALU = mybir.AluOpType

ENGS = ("pe", "act", "dve", "pool", "sp")
SEM_WRAP = 3500


class Op:
    __slots__ = ("eng", "fn", "reads", "writes", "dma", "idx", "waits", "sig", "extra")

    def __init__(self, eng, fn, reads, writes, dma):
        self.eng = eng
        self.fn = fn
        self.reads = reads
        self.writes = writes
        self.dma = dma
        self.waits = []
        self.sig = None
        self.extra = ()


def _stream(o):
    return ("d", o.dma) if o.dma is not None else ("e", o.eng)


class Prog:
    def __init__(self, nc, stack):
        self.nc = nc
        self.stack = stack
        self.ops = []
        self.sems = {}
        self.sigcount = {}
        self.nops_total = 0

    def op(self, eng, fn, reads=(), writes=(), dma=None):
        if dma is not None:
            km = self.__dict__.setdefault("dmakeys", {})
            if dma not in km:
                pre = "g" if eng == "pool" else "q"
                km[dma] = "%s%d" % (pre, sum(1 for v in km.values() if v[0] == pre))
            dma = km[dma]
        o = Op(eng, fn, tuple(reads), tuple(writes), dma)
        o.idx = len(self.ops)
        self.ops.append(o)
        return o

    def _resolve(self):
        ops = self.ops
        last_w, readers = {}, {}
        deps_of = []
        for o in ops:
            deps = set(o.extra)
            for r in o.reads:
                w = last_w.get(r)
                if w is not None:
                    deps.add(w.idx)
            for w_ in o.writes:
                w = last_w.get(w_)
                if w is not None:
                    deps.add(w.idx)
                for rd in readers.get(w_, ()):
                    deps.add(rd.idx)
            deps.discard(o.idx)
            deps_of.append(deps)
            for r in o.reads:
                readers.setdefault(r, []).append(o)
            for w_ in o.writes:
                last_w[w_] = o
                readers[w_] = []
        pos, cnt = {}, {}
        for o in ops:
            s = _stream(o)
            cnt[s] = cnt.get(s, 0) + 1
            pos[o.idx] = cnt[s]
        waited = {e: {} for e in ENGS}
        need_sig = set()
        for o in ops:
            e = o.eng
            best = {}
            for d in deps_of[o.idx]:
                p = ops[d]
                s = _stream(p)
                if s == ("e", e):
                    if e == "pe":
                        continue
                    raw = any(b in p.writes for b in o.reads) or any(b in p.writes for b in o.writes)
                    if not raw and d not in o.extra:
                        continue
                if pos[d] > best.get(s, 0):
                    best[s] = pos[d]
            for s, v in best.items():
                if waited[e].get(s, 0) >= v:
                    continue
                waited[e][s] = v
                o.waits.append((s, v))
                need_sig.add((s, v))
        sigval = {}
        for o in ops:
            s = _stream(o)
            if (s, pos[o.idx]) in need_sig or o.dma is not None:
                self.sigcount[s] = self.sigcount.get(s, 0) + 1
                sigval[(s, pos[o.idx])] = self.sigcount[s]
                o.sig = (s, self.sigcount[s])
        for o in ops:
            o.waits = [(s, sigval[(s, v)]) for (s, v) in o.waits]

    def _semval(self, s, v):
        k = (v - 1) // SEM_WRAP
        val = (v - 1) % SEM_WRAP + 1
        lst = self.sems.setdefault(s, [])
        while len(lst) <= k:
            nm = ("s_%s_%s_%d" % (s[0], s[1], len(lst))).replace(" ", "")
            lst.append(self.stack.enter_context(self.nc.semaphore(nm)))
        return lst[k], val * (16 if s[0] == "d" else 1)

    def flush(self):
        nc = self.nc
        if not self.ops:
            return
        last = {}
        for o in self.ops:
            last[_stream(o)] = o.idx
        j = self.op("sp", lambda e: e.nop())
        j.extra = tuple(last.values())
        self._resolve()
        by_eng = {e: [o for o in self.ops if o.eng == e] for e in ENGS}
        semval = self._semval

        def run(engobj, ops):
            for o in ops:
                for (s, v) in o.waits:
                    sem, val = semval(s, v)
                    engobj.wait_ge(sem, val)
                ins = o.fn(engobj)
                if o.sig is not None:
                    s, v = o.sig
                    sem, _ = semval(s, v)
                    ins.then_inc(sem, 16 if s[0] == "d" else 1)

        with nc.Block() as block:
            if by_eng["pe"]:
                @block.tensor
                def _(e):
                    run(e, by_eng["pe"])
            if by_eng["act"]:
                @block.scalar
                def _(e):
                    run(e, by_eng["act"])
            if by_eng["dve"]:
                @block.vector
                def _(e):
                    run(e, by_eng["dve"])
            if by_eng["pool"]:
                @block.gpsimd
                def _(e):
                    run(e, by_eng["pool"])
            if by_eng["sp"]:
                @block.sync
                def _(e):
                    run(e, by_eng["sp"])
        self.nops_total += len(self.ops)
        self.ops = []
        self.dmakeys = {}
        nc.all_engine_barrier()


D = 2048
KC = 16
DFF = 5632
FC = 44
EPS = 1e-6
AB_IN = 6176
C_GQ, C_GK, C_GV, C_GG, C_GLR, C_RQ, C_RK, C_RV, C_RG = 0, 512, 1024, 2048, 3072, 3104, 3616, 4128, 5152


def split(L, w):
    n = (L + w - 1) // w
    ww = (L + n - 1) // n
    return [(s, min(L, s + ww)) for s in range(0, L, ww)]


def build(cfg):
    TS, TP, PAST, NPR = cfg["TS"], cfg["TP"], cfg["PAST"], 2
    LQ = cfg.get("LQ", TS)
    dump = cfg.get("dump", None)
    stop_after = cfg.get("stop", "Z")
    T0 = TS + NPR * TP
    CH = cfg.get("CH", 128)
    NCH = T0 // CH
    NCT = 512 // CH
    EXT = 1 if LQ < TS else 0
    LQE = LQ + EXT
    nc = bass.Bass("TRN2", target_bir_lowering=False)
    I = {}

    def inp(name, shape, dt=F32):
        I[name] = nc.dram_tensor(name, list(shape), dt, kind="ExternalInput").ap()
        return I[name]

    def outp(name, shape):
        return nc.dram_tensor(name, list(shape), F32, kind="ExternalOutput").ap()

    def scr(name, shape, dt=F32):
        return nc.dram_tensor(name, list(shape), dt).ap()

    xs = inp("xs", [TS, D]); xp = inp("xp", [NPR * TP, D])
    sg_in = inp("sg", [2, 4, 128, 256]); sr_in = inp("sr", [2, 4, 128, 256])
    cckv = inp("cckv", [PAST, 512]); ckpe = inp("ckpe", [PAST, 64])
    cond = inp("cond", [2, D])
    mod_w = inp("mod_w", [2, D, 6 * D]); mod_b = inp("mod_b", [2, 6 * D])
    n1g = inp("norm1_g", [2, D]); n2g = inp("norm2_g", [2, D])
    w_ab = inp("ab_w_in", [D, AB_IN]); gw2 = inp("gla_gate_w2", [2, 16, 512]); gb = inp("gla_gate_b", [2, 512])
    rdec = inp("ret_decay", [8]); gng = inp("gla_norm_g", [1024]); rng_ = inp("ret_norm_g", [1024])
    w_abo = inp("ab_w_out", [D, D])
    w_mi = inp("mla_w_in", [D, 1088]); qng = inp("mla_q_norm_g", [512]); w_uq = inp("mla_w_uq", [512, 3072])
    kvng = inp("mla_kv_norm_g", [512]); w_ukv = inp("mla_w_ukv", [512, 4096]); w_mo = inp("mla_w_out", [D, D])
    w_fi = inp("ffn_w_in", [2, D, 2 * DFF]); fconv = inp("ffn_conv", [2, 3, DFF]); w_fo = inp("ffn_w_out", [2, DFF, D])
    fng = inp("final_norm_g", [D])
    c_id = inp("c_ident", [128, 128]); c_tri = inp("c_tri", [4, 128, 128]); c_mask = inp("c_mask", [2, CH, 8 * CH])
    c_pos = inp("c_pos", [128, 4]); c_rope = inp("c_rope", [2, 64, TS])

    y_s = outp("y_s", [LQ, D]); y_p = outp("y_p", [NPR * TP, D])
    o_sg = outp("ns_gla", [NPR, 2, 4, 128, 256]); o_sr = outp("ns_ret", [NPR, 2, 4, 128, 256])
    o_ckv = outp("n_ckv", [NPR * TP, 512]); o_kpe = outp("n_kpe", [NPR * TP, 64])
    dbg = outp("dbg", [D, T0]) if dump else None

    XA = scr("XA", [D, T0]); XB = scr("XB", [D, T0])
    Wab = scr("Wab", [D, AB_IN], BF16); Wabo = scr("Wabo", [D, D], BF16)
    Wmi = scr("Wmi", [D, 1088], BF16); Wuq = scr("Wuq", [512, 3072], BF16); Wukv = scr("Wukv", [512, 4096], BF16)
    Wmo = scr("Wmo", [D, D], BF16)
    Wfi = scr("Wfi", [2, D, 2 * DFF], BF16); Wfo = scr("Wfo", [2, DFF, D], BF16)
    modD = scr("modD", [2, 2, 6 * D])
    QDT = scr("QDT", [2, 4, 128, T0], BF16); KIT = scr("KIT", [2, 4, 128, T0], BF16)
    QRT = scr("QRT", [4, 128, T0], BF16); KRT = scr("KRT", [4, 128, T0], BF16)
    KE = scr("KE", [2, T0, 512], BF16); KR = scr("KR", [T0, 512], BF16)
    VG = scr("VG", [T0, 1024], BF16); VR = scr("VR", [2, T0, 1024], BF16); GG = scr("GG", [T0, 2048], BF16)
    SDD = scr("SDD", [2, 128, NCH, 4])
    OF = scr("OF", [T0, 2048]); OT = scr("OT", [T0, 2048])

    st = ExitStack()
    P = Prog(nc, st)
    cnt = [0]

    def uid(p):
        cnt[0] += 1
        return "%s%d" % (p, cnt[0])

    def DMA(eng, out, in_, reads, writes, key, slow=False):
        if slow:
            P.op(eng, lambda e: e.dma_start(out=out, in_=in_, allow_slow_non_contiguous=True), reads, writes, dma=key)
        else:
            P.op(eng, lambda e: e.dma_start(out=out, in_=in_), reads, writes, dma=key)

    def ACT(out, in_, func, reads, writes, scale=1.0, bias=0.0, accum=None):
        if accum is None:
            P.op("act", lambda e: e.activation(out=out, in_=in_, func=func, bias=bias, scale=scale), reads, writes)
        else:
            P.op("act", lambda e: e.activation(out=out, in_=in_, func=func, bias=bias, scale=scale, accum_out=accum), reads, writes)

    def TSC(eng, out, in0, s1, s2, op0, op1, reads, writes):
        if s2 is None:
            P.op(eng, lambda e: e.tensor_scalar(out=out, in0=in0, scalar1=s1, scalar2=None, op0=op0), reads, writes)
        else:
            P.op(eng, lambda e: e.tensor_scalar(out=out, in0=in0, scalar1=s1, scalar2=s2, op0=op0, op1=op1), reads, writes)

    def TT(eng, out, in0, in1, op, reads, writes):
        P.op(eng, lambda e: e.tensor_tensor(out=out, in0=in0, in1=in1, op=op), reads, writes)

    def STT(eng, out, in0, scalar, in1, op0, op1, reads, writes):
        P.op(eng, lambda e: e.scalar_tensor_tensor(out=out, in0=in0, scalar=scalar, in1=in1, op0=op0, op1=op1), reads, writes)

    def CP(eng, out, in_, reads, writes):
        P.op(eng, lambda e: e.tensor_copy(out=out, in_=in_), reads, writes)

    def MM(out, lhsT, rhs, start, stop, reads, writes):
        P.op("pe", lambda e: e.matmul(out, lhsT, rhs, start=start, stop=stop), reads, writes)

    def TR(out, in_, ident, reads, writes):
        P.op("pe", lambda e: e.transpose(out, in_, ident), reads, writes)

    rr = [0]

    def ve():
        rr[0] ^= 1
        return "dve" if rr[0] else "pool"

    def xsrc_rows(t0, n):
        if t0 < TS:
            return xs[t0:t0 + n, :]
        return xp[t0 - TS:t0 - TS + n, :]

    def row_of(t0):
        return 0 if t0 < TS else 1

    seqs = [(0, TS, "s")] + [(TS + j * TP, TP, "p%d" % j) for j in range(NPR)]

    class Phase:
        def __init__(self):
            self.s = ExitStack()
            self.ps_n = 0

        def sb(self, name, shape, dt=F32):
            return self.s.enter_context(nc.sbuf_tensor(uid(name), list(shape), dt))

        def ps(self, name, shape=(128, 512), dt=F32):
            return self.s.enter_context(nc.psum_tensor(uid(name), list(shape), dt))

        def end(self):
            P.flush()
            self.s.close()

    def consts(ph, bf_ident=False):
        c = {}
        c["id"] = ph.sb("id", [128, 128])
        DMA("sp", c["id"][:], c_id[:, :], [], ["c_id"], "c_id")
        c["ones"] = ph.sb("ones", [128, 128], BF16)
        P.op("pool", lambda e: e.memset(c["ones"][:], 1.0), [], ["c_ones"])
        c["eps"] = ph.sb("eps", [128, 1])
        P.op("pool", lambda e: e.memset(c["eps"][:], EPS), [], ["c_eps"])
        if bf_ident:
            c["idb"] = ph.sb("idb", [128, 128], BF16)
            CP("dve", c["idb"][:], c["id"][:], ["c_id"], ["c_idb"])
        return c

    CASTS = {"ab": (Wab, w_ab, D, AB_IN), "abo": (Wabo, w_abo, D, D), "mi": (Wmi, w_mi, D, 1088), "uq": (Wuq, w_uq, 512, 3072),
             "ukv": (Wukv, w_ukv, 512, 4096), "mo": (Wmo, w_mo, D, D),
             "fi0": (Wfi[0], w_fi[0], D, 2 * DFF), "fo0": (Wfo[0], w_fo[0], DFF, D),
             "fi1": (Wfi[1], w_fi[1], D, 2 * DFF), "fo1": (Wfo[1], w_fo[1], DFF, D)}
    kcast = [0]

    def cast_list(names):
        out = []
        for nm in names:
            dst, src, R, C_ = CASTS[nm]
            rs = max(128, (1 << 21) // C_ // 128 * 128)
            for r0 in range(0, R, rs):
                out.append((dst[r0:min(R, r0 + rs), :], src[r0:min(R, r0 + rs), :]))
        return out

    def emit_casts(lst, n):
        for _ in range(min(n, len(lst))):
            dst, src = lst.pop(0)
            k = kcast[0]; kcast[0] += 1
            DMA("pool", dst, src, [], ["wcast%d" % k], "wc%d" % (k % 8))

    def gen_M(ph, c):
        cT = ph.sb("cT", [128, 2, KC]); csT = ph.sb("csT", [128, 2, KC])
        for r in range(2):
            DMA("sp", cT[:, r, :], cond[r].rearrange("(c p) -> p c", p=128), ["cT"], ["cT"], "cT", slow=True)
        ACT(csT[:], cT[:], AF.Silu, ["cT"], ["csT"])
        wm = [ph.sb("wm", [128, KC, 512]) for _ in range(2)]
        mb = [ph.sb("mb", [2, 512]) for _ in range(2)]
        mrow = [ph.sb("mrow", [2, 512]) for _ in range(2)]
        pm = [ph.ps("pm") for _ in range(2)]
        for l in range(2):
            for j in range(24):
                b = j % 2
                DMA("sp", wm[b][:], mod_w[l][:, j * 512:(j + 1) * 512].rearrange("(k p) c -> p k c", p=128), [], ["wm%d" % b], "wm%d" % b)
                DMA("sp", mb[b][:], mod_b[l, j * 512:(j + 1) * 512].partition_broadcast(2), [], ["mb%d" % b], "mb%d" % b)
                for kc in range(KC):
                    MM(pm[b][0:2, :], csT[:, :, kc], wm[b][:, kc, :], kc == 0, kc == KC - 1, ["csT", "wm%d" % b], ["pm%d" % b])
                TT("dve", mrow[b][:], pm[b][0:2, :], mb[b][:], ALU.add, ["pm%d" % b, "mb%d" % b], ["mrow%d" % b])
                DMA("act", modD[l][:, j * 512:(j + 1) * 512], mrow[b][:], ["mrow%d" % b], ["modD"], "mrow%d" % b)
                yield

    def load_mod(ph, l, which):
        m = {}
        for j in which:
            t = ph.sb("modT", [128, 2, KC])
            for r in range(2):
                DMA("sp", t[:, r, :], modD[l][r, j * D:(j + 1) * D].rearrange("(c p) -> p c", p=128), ["modD", "modT%d" % j], ["modT%d" % j], "modT%d" % j, slow=True)
            m[j] = t
        return m

    def load_vecT(ph, src1d, n, key):
        t = ph.sb(key, [128, n])
        DMA("sp", t[:], src1d.rearrange("(c p) -> p c", p=128), [], [key], key, slow=True)
        return t

    def make_AB(ph, l, gsrc, js, jb, key):
        m = load_mod(ph, l, (js, jb))
        g = load_vecT(ph, gsrc, KC, key + "g")
        A = ph.sb("A", [128, 2, KC])
        for r in range(2):
            STT("dve", A[:, r, :], m[js][:, r, :], 1.0, g[:], ALU.add, ALU.mult, ["modT%d" % js, key + "g"], [key + "A"])
        return A, m[jb]

    def fm_rstd(ph, c, xt, kcn, N, dn, rk, keyp, bufs):
        sq, psn, rstd = bufs
        for kc in range(kcn):
            b = kc % 2
            ACT(sq[b][:, 0:N], xt[:, kc, 0:N], AF.Square, rk, [keyp + "sq%d" % b])
            MM(psn[:, 0:N], c["ones"][:], sq[b][:, 0:N], kc == 0, kc == kcn - 1, [keyp + "sq%d" % b, "c_ones"], [keyp + "psn"])
        ACT(rstd[:, 0:N], psn[:, 0:N], AF.Sqrt, [keyp + "psn", "c_eps"], [keyp + "rstd"], scale=1.0 / dn, bias=c["eps"][:, 0:1])
        P.op("dve", lambda e: e.reciprocal(out=rstd[:, 0:N], in_=rstd[:, 0:N]), [keyp + "rstd"], [keyp + "rstd"])

    def fm_norm_mod(ph, c, xt, N, A, B, r, hT, rk, keyp, bufs, tmp, hkey=None):
        fm_rstd(ph, c, xt, KC, N, D, rk, keyp, bufs)
        rstd = bufs[2]
        for kc in range(KC):
            b = kc % 2
            TT(ve() if False else "dve", tmp[b][:, 0:N], xt[:, kc, 0:N], rstd[:, 0:N], ALU.mult, rk + [keyp + "rstd"], [keyp + "tmp%d" % b])
            ACT(hT[:, kc, 0:N], tmp[b][:, 0:N], AF.Identity, [keyp + "tmp%d" % b, keyp + "A"], [hkey or (keyp + "hT")],
                scale=A[:, r, kc:kc + 1], bias=B[:, r, kc:kc + 1])

    def gen_T(ph, c):
        xt = [ph.sb("xt", [128, D]) for _ in range(2)]
        xT = [ph.sb("xT", [128, KC, 128]) for _ in range(2)]
        pt = [ph.ps("pt") for _ in range(4)]
        for i in range(T0 // 128):
            b = i % 2
            DMA("sp", xt[b][:], xsrc_rows(i * 128, 128), [], ["xt%d" % b], "xt%d" % b)
            for q in range(4):
                for k4 in range(4):
                    kc = q * 4 + k4
                    TR(pt[q][:, k4 * 128:(k4 + 1) * 128], xt[b][:, kc * 128:(kc + 1) * 128], c["id"][:], ["xt%d" % b, "c_id"], ["pt%d" % q])
                if q % 2 == 0:
                    CP("dve", xT[b][:, q * 4:(q + 1) * 4, :], pt[q][:].rearrange("p (k t) -> p k t", k=4), ["pt%d" % q], ["xT%d" % b])
                else:
                    ACT(xT[b][:, q * 4:(q + 1) * 4, :], pt[q][:].rearrange("p (k t) -> p k t", k=4), AF.Copy, ["pt%d" % q], ["xT%d" % b])
            DMA("act", XA.rearrange("(k p) t -> p k t", p=128)[:, :, i * 128:(i + 1) * 128], xT[b][:], ["xT%d" % b], ["XA"], "stx%d" % b)
            yield

    def phase_WMT():
        ph = Phase()
        c = consts(ph)
        cl = cast_list(["ab"])
        emit_casts(cl, len(cl))
        gens = [gen_M(ph, c), gen_T(ph, c)]
        while gens:
            for g_ in list(gens):
                try:
                    next(g_)
                except StopIteration:
                    gens.remove(g_)
        ph.end()

    def phase_A():
        ph = Phase()
        c = consts(ph)
        A, B = make_AB(ph, 0, n1g[0], 1, 0, "A_")
        tri = ph.sb("tri", [128, 4, 128])
        DMA("sp", tri[:], c_tri.rearrange("k p t -> p k t"), [], ["tri"], "tri")
        pos = ph.sb("pos", [128, 4])
        DMA("sp", pos[:], c_pos[:, :], [], ["pos"], "pos")
        rd = ph.sb("rd", [128, 8]); lg = ph.sb("lg", [128, 8]); nlg = ph.sb("nlg", [128, 8])
        DMA("sp", rd[:], rdec.partition_broadcast(128), [], ["rd"], "rd")
        ACT(lg[:], rd[:], AF.Exp, ["rd"], ["lg"], scale=-1.0)
        ACT(lg[:], lg[:], AF.Ln, ["lg"], ["lg"], bias=1.0)
        TSC("dve", nlg[:], lg[:], -1.0, None, ALU.mult, None, ["lg"], ["nlg"])
        vsc = ph.sb("vsc", [128, 8])
        for d_ in range(2):
            for h in range(4):
                ACT(vsc[:, d_ * 4 + h:d_ * 4 + h + 1], pos[:, d_:d_ + 1], AF.Exp, ["pos", "lg"], ["vsc"], scale=lg[:, d_ * 4 + h:d_ * 4 + h + 1])
        w2f = ph.sb("w2f", [17, 2, 512]); w2b = ph.sb("w2b", [17, 2, 512], BF16)
        DMA("sp", w2f[0:16, :, :], gw2.rearrange("d k c -> k d c"), [], ["w2f"], "w2f")
        DMA("sp", w2f[16:17, :, :], gb.rearrange("(o d) c -> o d c", o=1), ["w2f"], ["w2f"], "w2f")
        CP("dve", w2b[:], w2f[:], ["w2f"], ["w2b"])
        xt = [ph.sb("xt", [128, KC, 512]) for _ in range(1)]
        sq = [ph.sb("sq", [128, 512], BF16) for _ in range(2)]
        rstd = ph.sb("rstd", [128, 512])
        tmp = [ph.sb("tmp", [128, 512]) for _ in range(2)]
        hT = ph.sb("hT", [128, KC, 512], BF16)
        wsl = [ph.sb("wsl", [128, KC, 512], BF16) for _ in range(4)]
        glrT = [ph.sb("glrT", [17, 512], BF16) for _ in range(2)]
        for d_ in range(2):
            P.op("pool", lambda e, d_=d_: e.memset(glrT[d_][:], 1.0), [], ["glrT%d" % d_])
        ex = ph.sb("ex", [128, 512]); la = [ph.sb("la", [128, 512]) for _ in range(2)]
        EB = [ph.sb("EB", [128, 4, 512]) for _ in range(2)]; EI = [ph.sb("EI", [128, 4, 512]) for _ in range(2)]
        EE = [ph.sb("EE", [128, 4, 512]) for _ in range(2)]
        sd = ph.sb("sd", [128, 2, NCT, 4])
        stf = [ph.sb("stf", [128, 512], BF16) for _ in range(4)]
        stt = [ph.sb("stt", [128, 2048], BF16) for _ in range(2)]
        psn = ph.ps("psn"); pg = ph.ps("pg"); pa = [ph.ps("pa") for _ in range(3)]; pb_ = [ph.ps("pb") for _ in range(2)]
        Wv = Wab.rearrange("(k p) c -> p k c", p=128)
        nslab = [0]

        def wslab(c0, cw):
            b = nslab[0] % 4
            nslab[0] += 1
            DMA("sp", wsl[b][:, :, 0:cw], Wv[:, :, c0:c0 + cw], ["wcast"], ["wsl%d" % b], "wsl%d" % b)
            return wsl[b], "wsl%d" % b

        sfi = [0]; sti = [0]; pai = [0]
        clA = cast_list(["abo", "fi0", "fo0"])
        perA = (len(clA) + (T0 // 512) - 1) // (T0 // 512)

        for (t0, t1) in [(t, t + 512) for t in range(0, T0, 512)]:
            N = 512
            r = row_of(t0)
            emit_casts(clA, perA)
            DMA("act", xt[0][:], XA.rearrange("(k p) t -> p k t", p=128)[:, :, t0:t1], ["XA"], ["A_xt"], "A_xt")
            fm_norm_mod(ph, c, xt[0], N, A, B, r, hT, ["A_xt"], "A_", (sq, psn, rstd), tmp)
            w, wk = wslab(C_GLR, 32)
            for d_ in range(2):
                for kc in range(KC):
                    MM(pg[0:16, :], w[:, kc, d_ * 16:(d_ + 1) * 16], hT[:, kc, :], kc == 0, kc == KC - 1, [wk, "A_hT"], ["pg"])
                CP("dve", glrT[d_][0:16, :], pg[0:16, :], ["pg"], ["glrT%d" % d_])
            for s in range(4):
                for d_ in range(2):
                    p_ = pa[pai[0] % 3]; pk = "pa%d" % (pai[0] % 3); pai[0] += 1
                    MM(p_[:], glrT[d_][:, s * 128:(s + 1) * 128], w2b[:, d_, :], True, True, ["glrT%d" % d_, "w2b"], [pk])
                    ACT(ex[:], p_[:], AF.Exp, [pk], ["ex"], scale=-1.0)
                    ACT(la[d_][:], ex[:], AF.Ln, ["ex"], ["la%d" % d_], bias=1.0)
                    p2 = pa[pai[0] % 3]; pk2 = "pa%d" % (pai[0] % 3); pai[0] += 1
                    for h in range(4):
                        MM(p2[:, h * 128:(h + 1) * 128], la[d_][:, h * 128:(h + 1) * 128], tri[:, d_, :], True, True, ["la%d" % d_, "tri"], [pk2])
                    ACT(EB[d_][:, :, s * 128:(s + 1) * 128], p2[:].rearrange("p (h t) -> p h t", h=4), AF.Exp, [pk2], ["EB%d" % d_])
                    ACT(EI[d_][:, :, s * 128:(s + 1) * 128], p2[:].rearrange("p (h t) -> p h t", h=4), AF.Exp, [pk2], ["EI%d" % d_], scale=-1.0)
                    p3 = pa[pai[0] % 3]; pk3 = "pa%d" % (pai[0] % 3); pai[0] += 1
                    MM(p3[:], tri[:, 2 + d_, :], la[d_][:], True, True, ["la%d" % d_, "tri"], [pk3])
                    ACT(EE[d_][:, s, :], p3[:], AF.Exp, [pk3], ["EE%d" % d_])
            for d_ in range(2):
                off = CH - 1 if d_ == 0 else 0
                CP("pool", sd[:, d_, :, :], EB[d_][:].rearrange("p h (c j) -> p c h j", j=CH)[:, :, :, off], ["EB%d" % d_], ["sd"])
                DMA("act", SDD[d_][:, t0 // CH:t0 // CH + NCT, :], sd[:, d_, :, :], ["sd"], ["SDD"], "sd%d" % d_)
            for (c0, isq, gla) in [(C_GQ, True, True), (C_GK, False, True), (C_RQ, True, False), (C_RK, False, False)]:
                w, wk = wslab(c0, 512)
                for h in range(4):
                    p_ = pb_[h % 2]; pk = "pb%d" % (h % 2)
                    for kc in range(KC):
                        MM(p_[:], w[:, kc, h * 128:(h + 1) * 128], hT[:, kc, :], kc == 0, kc == KC - 1, [wk, "A_hT"], [pk])
                    if gla:
                        for d_ in range(2):
                            sb_ = stf[sfi[0] % 4]; sk = "stf%d" % (sfi[0] % 4); sfi[0] += 1
                            if isq:
                                STT("dve", sb_[:], p_[:], 128 ** -0.5, EB[d_][:, h, :], ALU.mult, ALU.mult, [pk, "EB%d" % d_], [sk])
                                DMA("act", QDT[d_, h][:, t0:t1], sb_[:], [sk], ["QDT"], sk)
                            else:
                                TT("dve", sb_[:], p_[:], EI[d_][:, h, :], ALU.mult, [pk, "EI%d" % d_], [sk])
                                DMA("act", KIT[d_, h][:, t0:t1], sb_[:], [sk], ["KIT"], sk)
                    else:
                        sb_ = stf[sfi[0] % 4]; sk = "stf%d" % (sfi[0] % 4); sfi[0] += 1
                        if isq:
                            ACT(sb_[:], p_[:], AF.Copy, [pk], [sk])
                            DMA("act", QRT[h][:, t0:t1], sb_[:], [sk], ["QRT"], sk)
                        else:
                            ACT(sb_[:], p_[:], AF.Copy, [pk], [sk], scale=128 ** -0.5)
                            DMA("act", KRT[h][:, t0:t1], sb_[:], [sk], ["KRT"], sk)
            for (c0, cw, kind) in [(C_GK, 512, "ke"), (C_RK, 512, "kr"), (C_GV, 512, "gv0"), (C_GV + 512, 512, "gv1"),
                                   (C_RV, 512, "rv0"), (C_RV + 512, 512, "rv1"),
                                   (C_GG, 512, "g0"), (C_GG + 512, 512, "g1"), (C_RG, 512, "g2"), (C_RG + 512, 512, "g3")]:
                w, wk = wslab(c0, cw)
                for s in range(4):
                    p_ = pa[pai[0] % 3]; pk = "pa%d" % (pai[0] % 3); pai[0] += 1
                    for kc in range(KC):
                        MM(p_[:], hT[:, kc, s * 128:(s + 1) * 128], w[:, kc, :], kc == 0, kc == KC - 1, [wk, "A_hT"], [pk])
                    rows = slice(t0 + s * 128, t0 + (s + 1) * 128)
                    sb_ = stt[sti[0] % 2]; sk = "stt%d" % (sti[0] % 2); sti[0] += 1
                    if kind == "ke":
                        for d_ in range(2):
                            TT("dve", sb_[:, d_ * 512:(d_ + 1) * 512], p_[:], EE[d_][:, s, :], ALU.mult, [pk, "EE%d" % d_], [sk])
                        DMA("act", KE[0][rows, :], sb_[:, 0:512], [sk], ["KE"], sk)
                        DMA("act", KE[1][rows, :], sb_[:, 512:1024], [sk], ["KE"], sk + "b")
                    elif kind == "kr":
                        ACT(sb_[:, 0:512], p_[:], AF.Copy, [pk], [sk], scale=128 ** -0.5)
                        DMA("act", KR[rows, :], sb_[:, 0:512], [sk], ["KR"], sk)
                    elif kind[:2] == "gv":
                        j = int(kind[2])
                        ACT(sb_[:, 0:512], p_[:], AF.Copy, [pk], [sk])
                        DMA("act", VG[rows, j * 512:(j + 1) * 512], sb_[:, 0:512], [sk], ["VG"], sk)
                    elif kind[:2] == "rv":
                        j = int(kind[2])
                        for d_ in range(2):
                            for hh in range(2):
                                h = j * 2 + hh
                                TSC("dve", sb_[:, d_ * 512 + hh * 256:d_ * 512 + (hh + 1) * 256], p_[:, hh * 256:(hh + 1) * 256],
                                    vsc[:, d_ * 4 + h:d_ * 4 + h + 1], None, ALU.mult, None, [pk, "vsc"], [sk])
                        DMA("act", VR[0][rows, j * 512:(j + 1) * 512], sb_[:, 0:512], [sk], ["VR"], sk)
                        DMA("act", VR[1][rows, j * 512:(j + 1) * 512], sb_[:, 512:1024], [sk], ["VR"], sk + "b")
                    else:
                        j = int(kind[1])
                        ACT(sb_[:, 0:512], p_[:], AF.Copy, [pk], [sk])
                        DMA("act", GG[rows, j * 512:(j + 1) * 512], sb_[:, 0:512], [sk], ["GG"], sk)
        ph.end()


    def phase_scan(d_):
        ph = Phase()
        msk = ph.sb("msk", [CH, 8 * CH])
        DMA("sp", msk[:], c_mask[d_], [], ["msk"], "msk")
        pos = ph.sb("pos", [128, 4])
        DMA("sp", pos[:], c_pos[:, :], [], ["pos"], "pos")
        rd = ph.sb("rd", [128, 8]); lg = ph.sb("lg", [128, 8])
        DMA("sp", rd[:], rdec.partition_broadcast(128), [], ["rd"], "rd")
        ACT(lg[:], rd[:], AF.Exp, ["rd"], ["lg"], scale=-1.0)
        ACT(lg[:], lg[:], AF.Ln, ["lg"], ["lg"], bias=1.0)
        nlg = ph.sb("nlg", [128, 8])
        TSC("dve", nlg[:], lg[:], -1.0, None, ALU.mult, None, ["lg"], ["nlg"])
        pcol = ph.sb("pcol", [128, 4]); eret = ph.sb("eret", [128, 4]); c64 = ph.sb("c64", [128, 1])
        P.op("pool", lambda e: e.memset(c64[:], float(CH)), [], ["c64"])
        for h in range(4):
            ACT(pcol[:, h:h + 1], pos[:, d_:d_ + 1], AF.Exp, ["pos", "nlg"], ["pcol"], scale=nlg[:, d_ * 4 + h:d_ * 4 + h + 1])
            ACT(eret[:, h:h + 1], c64[:], AF.Exp, ["c64", "nlg"], ["eret"], scale=nlg[:, d_ * 4 + h:d_ * 4 + h + 1])
        S = ph.sb("S", [128, 8, 256]); Sb = ph.sb("Sb", [128, 8, 256], BF16)
        tS = [ph.sb("tS", [128, 256]) for _ in range(2)]
        qg = [ph.sb("qg", [128, 4, 512], BF16) for _ in range(2)]; kg = [ph.sb("kg", [128, 4, 512], BF16) for _ in range(2)]
        qr = [ph.sb("qr", [128, 4, 512], BF16) for _ in range(2)]; kr = [ph.sb("kr", [128, 4, 512], BF16) for _ in range(2)]
        ke = [ph.sb("ke", [CH, NCT, 512], BF16) for _ in range(2)]; krr = [ph.sb("krr", [CH, NCT, 512], BF16) for _ in range(2)]
        vg = [ph.sb("vg", [CH, NCT, 1024], BF16) for _ in range(2)]; vr = [ph.sb("vr", [CH, NCT, 1024], BF16) for _ in range(2)]
        sdt = [ph.sb("sdt", [128, NCT, 4]) for _ in range(2)]
        ofc = [ph.sb("ofc", [CH, 2048]) for _ in range(2)]; oc = [ph.sb("oc", [CH, 2048]) for _ in range(2)]
        att = [ph.sb("att", [CH, 8 * CH], BF16) for _ in range(2)]
        pat = ph.ps("pat", (128, 8 * CH)); po = [ph.ps("po", (128, 1024)) for _ in range(2)]; pkv = [ph.ps("pkv") for _ in range(2)]
        ti = 0; cidx = 0
        SK = ["S%d" % h for h in range(8)]
        for (s0, L, kind) in seqs:
            if kind == "s":
                DMA("sp", S[:, 0:4, :], sg_in[d_].rearrange("h p e -> p h e"), [], SK[0:4], "S0")
                DMA("sp", S[:, 4:8, :], sr_in[d_].rearrange("h p e -> p h e"), [], SK[4:8], "S0b")
            else:
                P.op("pool", lambda e: e.memset(S[:], 0.0), [], SK)
            CP("pool", Sb[:], S[:], SK, ["Sb0", "Sb1"])
            tl = [(t, min(t + 512, s0 + L)) for t in range(s0, s0 + L, 512)]
            if d_ == 1:
                tl = tl[::-1]
            for (t0, t1) in tl:
                b = ti % 2; ti += 1
                n = t1 - t0; ncw = n // CH
                B_ = "%d" % b
                DMA("sp", qg[b][:, :, 0:n], QDT[d_].rearrange("h p t -> p h t")[:, :, t0:t1], ["QDT"], ["qg" + B_], "qg" + B_)
                DMA("sp", kg[b][:, :, 0:n], KIT[d_].rearrange("h p t -> p h t")[:, :, t0:t1], ["KIT"], ["kg" + B_], "kg" + B_)
                DMA("sp", qr[b][:, :, 0:n], QRT.rearrange("h p t -> p h t")[:, :, t0:t1], ["QRT"], ["qr" + B_], "qr" + B_)
                DMA("sp", kr[b][:, :, 0:n], KRT.rearrange("h p t -> p h t")[:, :, t0:t1], ["KRT"], ["kr" + B_], "kr" + B_)
                DMA("act", ke[b][:, 0:ncw, :], KE[d_][t0:t1, :].rearrange("(c j) f -> j c f", j=CH), ["KE"], ["ke" + B_], "ke" + B_)
                DMA("act", krr[b][:, 0:ncw, :], KR[t0:t1, :].rearrange("(c j) f -> j c f", j=CH), ["KR"], ["krr" + B_], "krr" + B_)
                DMA("act", vg[b][:, 0:ncw, :], VG[t0:t1, :].rearrange("(c j) f -> j c f", j=CH), ["VG"], ["vg" + B_], "vg" + B_)
                DMA("act", vr[b][:, 0:ncw, :], VR[d_][t0:t1, :].rearrange("(c j) f -> j c f", j=CH), ["VR"], ["vr" + B_], "vr" + B_)
                DMA("sp", sdt[b][:, 0:ncw, :], SDD[d_][:, t0 // CH:t0 // CH + ncw, :], ["SDD"], ["sdt" + B_], "sdt" + B_)
                cl = list(range(ncw))
                if d_ == 1:
                    cl = cl[::-1]
                for ci in cl:
                    cb = cidx % 2; cidx += 1
                    CB = "%d" % cb
                    cs = slice(ci * CH, (ci + 1) * CH)
                    rows = slice(t0 + ci * CH, t0 + (ci + 1) * CH)
                    if d_ == 1:
                        DMA("sp", ofc[cb][:], OF[rows, :], ["OF"], ["ofc" + CB], "ofc" + CB)
                    for h in range(8):
                        K_ = kg[b][:, h, cs] if h < 4 else kr[b][:, h - 4, cs]
                        Q_ = qg[b][:, h, cs] if h < 4 else qr[b][:, h - 4, cs]
                        MM(pat[0:CH, h * CH:(h + 1) * CH], K_, Q_, True, True, ["kg" + B_, "kr" + B_, "qg" + B_, "qr" + B_], ["pat"])
                    TT("dve", att[cb][:], pat[0:CH, :], msk[:], ALU.mult, ["pat", "msk"], ["att" + CB])
                    for g in range(2):
                        for hh in range(4):
                            h = g * 4 + hh
                            V_ = (vg[b] if g == 0 else vr[b])[:, ci, hh * 256:(hh + 1) * 256]
                            Q_ = qg[b][:, hh, cs] if g == 0 else qr[b][:, hh, cs]
                            MM(po[g][0:CH, hh * 256:(hh + 1) * 256], att[cb][:, h * CH:(h + 1) * CH], V_, True, False,
                               ["att" + CB, "vg" + B_, "vr" + B_], ["po%d" % g])
                            MM(po[g][0:CH, hh * 256:(hh + 1) * 256], Q_, Sb[:, h, :], False, True, ["qg" + B_, "qr" + B_, "Sb%d" % g], ["po%d" % g])
                    if d_ == 0:
                        ACT(oc[cb][:, 0:1024], po[0][0:CH, :], AF.Copy, ["po0"], ["oc" + CB])
                    else:
                        TT("dve", oc[cb][:, 0:1024], po[0][0:CH, :], ofc[cb][:, 0:1024], ALU.add, ["po0", "ofc" + CB], ["oc" + CB])
                    for hh in range(4):
                        o_ = oc[cb][:, 1024 + hh * 256:1024 + (hh + 1) * 256]
                        if d_ == 0:
                            TSC("dve", o_, po[1][0:CH, hh * 256:(hh + 1) * 256], pcol[0:CH, hh:hh + 1], None, ALU.mult, None, ["po1", "pcol"], ["oc" + CB])
                        else:
                            STT("dve", o_, po[1][0:CH, hh * 256:(hh + 1) * 256], pcol[0:CH, hh:hh + 1], ofc[cb][:, 1024 + hh * 256:1024 + (hh + 1) * 256],
                                ALU.mult, ALU.add, ["po1", "pcol", "ofc" + CB], ["oc" + CB])
                    DMA("act", (OF if d_ == 0 else OT)[rows, :], oc[cb][:], ["oc" + CB], ["OF" if d_ == 0 else "OT"], "oc" + CB)
                    for g in range(2):
                        for hh in range(4):
                            h = g * 4 + hh
                            Ke_ = (ke[b] if g == 0 else krr[b])[:, ci, hh * 128:(hh + 1) * 128]
                            V_ = (vg[b] if g == 0 else vr[b])[:, ci, hh * 256:(hh + 1) * 256]
                            MM(pkv[hh // 2][:, (hh % 2) * 256:(hh % 2 + 1) * 256], Ke_, V_, True, True, ["ke" + B_, "krr" + B_, "vg" + B_, "vr" + B_], ["pkv%d" % (hh // 2)])
                        for hh in range(4):
                            h = g * 4 + hh
                            tb = h % 2
                            e1 = sdt[b][:, ci, hh:hh + 1] if g == 0 else eret[:, hh:hh + 1]
                            e2 = 1.0 if g == 0 else eret[:, hh:hh + 1]
                            ACT(tS[tb][:], S[:, h, :], AF.Copy, ["S%d" % h, "sdt" + B_, "eret"], ["tS%d" % tb], scale=e1)
                            STT("dve", S[:, h, :], pkv[hh // 2][:, (hh % 2) * 256:(hh % 2 + 1) * 256], e2, tS[tb][:], ALU.mult, ALU.add,
                                ["pkv%d" % (hh // 2), "tS%d" % tb, "eret"], ["S%d" % h])
                        ACT(Sb[:, g * 4:(g + 1) * 4, :], S[:, g * 4:(g + 1) * 4, :], AF.Copy, ["S%d" % (g * 4 + q_) for q_ in range(4)], ["Sb%d" % g])
            if kind != "s":
                j = int(kind[1])
                DMA("sp", o_sg[j, d_].rearrange("h p e -> p h e"), S[:, 0:4, :], SK[0:4], ["o_sg"], "So")
                DMA("sp", o_sr[j, d_].rearrange("h p e -> p h e"), S[:, 4:8, :], SK[4:8], ["o_sr"], "So2")
        ph.end()

    def phase_D():
        ph = Phase()
        c = consts(ph, bf_ident=True)
        m = load_mod(ph, 0, (2,))
        ngb = ph.sb("ngb", [128, 2048])
        DMA("sp", ngb[:, 0:1024], gng.partition_broadcast(128), [], ["ngb"], "ngb")
        DMA("sp", ngb[:, 1024:2048], rng_.partition_broadcast(128), ["ngb"], ["ngb"], "ngb")
        ot = [ph.sb("ot", [128, 2048]) for _ in range(2)]; gt = [ph.sb("gt", [128, 2048], BF16) for _ in range(2)]
        on_ = [ph.sb("on", [128, 2048]) for _ in range(2)]; sgm_ = [ph.sb("sgm", [128, 2048]) for _ in range(2)]
        y_ = [ph.sb("y", [128, 2048], BF16) for _ in range(2)]
        st8_ = [ph.sb("st8", [128, 4, 8]) for _ in range(2)]
        yT = ph.sb("yT", [128, KC, 512], BF16); x0 = ph.sb("x0", [128, KC, 512])
        wsl = [ph.sb("wsl", [128, KC, 512], BF16) for _ in range(2)]
        ptb = [ph.ps("ptb", (128, 512), BF16) for _ in range(2)]; pq = [ph.ps("pq") for _ in range(2)]
        Wv = Wabo.rearrange("(k p) c -> p k c", p=128)
        XAv = XA.rearrange("(k p) t -> p k t", p=128); XBv = XB.rearrange("(k p) t -> p k t", p=128)
        si = 0; wi = 0
        for t0 in range(0, T0, 512):
            r = row_of(t0)
            DMA("act", x0[:], XAv[:, :, t0:t0 + 512], ["XA"], ["x0"], "x0")
            for s in range(4):
                b = si % 2; si += 1
                B_ = "%d" % b
                on = on_[b]; sgm = sgm_[b]; y = y_[b]; st8 = st8_[b]
                rows = slice(t0 + s * 128, t0 + (s + 1) * 128)
                DMA("sp", ot[b][:], OT[rows, :], ["OT"], ["ot" + B_], "ot" + B_)
                DMA("sp", gt[b][:], GG[rows, :], ["GG"], ["gt" + B_], "gt" + B_)
                otv = ot[b][:].rearrange("p (h e) -> p h e", h=8)
                P.op("dve", lambda e, otv=otv, st8=st8: e.tensor_reduce(out=st8[:, 0, :], in_=otv, axis=mybir.AxisListType.X, op=ALU.add), ["ot" + B_], ["st8" + B_])
                TT("pool", on[:], ot[b][:], ot[b][:], ALU.mult, ["ot" + B_], ["on" + B_])
                onv = on[:].rearrange("p (h e) -> p h e", h=8)
                P.op("dve", lambda e, onv=onv, st8=st8: e.tensor_reduce(out=st8[:, 1, :], in_=onv, axis=mybir.AxisListType.X, op=ALU.add), ["on" + B_], ["st8" + B_])
                TSC("dve", st8[:, 0, :], st8[:, 0, :], 1.0 / 256, None, ALU.mult, None, ["st8" + B_], ["st8" + B_])
                TT("dve", st8[:, 2, :], st8[:, 0, :], st8[:, 0, :], ALU.mult, ["st8" + B_], ["st8" + B_])
                STT("dve", st8[:, 1, :], st8[:, 1, :], 1.0 / 256, st8[:, 2, :], ALU.mult, ALU.subtract, ["st8" + B_], ["st8" + B_])
                ACT(st8[:, 1, :], st8[:, 1, :], AF.Sqrt, ["st8" + B_, "c_eps"], ["st8" + B_], bias=c["eps"][:, 0:1])
                P.op("dve", lambda e, st8=st8: e.reciprocal(out=st8[:, 1, :], in_=st8[:, 1, :]), ["st8" + B_], ["st8" + B_])
                STT("dve", st8[:, 2, :], st8[:, 0, :], -1.0, st8[:, 1, :], ALU.mult, ALU.mult, ["st8" + B_], ["st8" + B_])
                for h in range(8):
                    ACT(on[:, h * 256:(h + 1) * 256], ot[b][:, h * 256:(h + 1) * 256], AF.Identity, ["ot" + B_, "st8" + B_], ["on" + B_],
                        scale=st8[:, 1, h:h + 1], bias=st8[:, 2, h:h + 1])
                ACT(sgm[:], gt[b][:], AF.Silu, ["gt" + B_], ["sgm" + B_])
                TT("pool", sgm[:], sgm[:], ngb[:], ALU.mult, ["sgm" + B_, "ngb"], ["sgm" + B_])
                TT("dve", y[:], on[:], sgm[:], ALU.mult, ["on" + B_, "sgm" + B_], ["y" + B_])
                for q in range(4):
                    pb = q % 2
                    for k4 in range(4):
                        kc = q * 4 + k4
                        TR(ptb[pb][:, k4 * 128:(k4 + 1) * 128], y[:, kc * 128:(kc + 1) * 128], c["idb"][:], ["y" + B_, "c_idb"], ["ptb%d" % pb])
                    CP("dve", yT[:, q * 4:(q + 1) * 4, s * 128:(s + 1) * 128], ptb[pb][:].rearrange("p (k t) -> p k t", k=4), ["ptb%d" % pb], ["yT"])
            for f4 in range(4):
                wb = wi % 2; wi += 1
                DMA("sp", wsl[wb][:], Wv[:, :, f4 * 512:(f4 + 1) * 512], ["wcast"], ["wsl%d" % wb], "wsl%d" % wb)
                for f1 in range(4):
                    fc = f4 * 4 + f1
                    pb = fc % 2
                    for kc in range(KC):
                        MM(pq[pb][:], wsl[wb][:, kc, f1 * 128:(f1 + 1) * 128], yT[:, kc, :], kc == 0, kc == KC - 1, ["wsl%d" % wb, "yT"], ["pq%d" % pb])
                    STT("dve", x0[:, fc, :], pq[pb][:], m[2][:, r, fc:fc + 1], x0[:, fc, :], ALU.mult, ALU.add, ["pq%d" % pb, "modT2", "x0"], ["x0"])
            DMA("act", XBv[:, :, t0:t0 + 512], x0[:], ["x0"], ["XB"], "x0s")
        ph.end()

    def phase_E(l, Xin, Xout, ranges, TT_):
        ph = Phase()
        c = consts(ph)
        A, B = make_AB(ph, l, n2g[l], 4, 3, "E_")
        m = load_mod(ph, l, (5,))
        cw = ph.sb("cw", [128, 3, FC])
        for j in range(3):
            DMA("sp", cw[:, j, :], fconv[l][j].rearrange("(c p) -> p c", p=128), ["cw"], ["cw"], "cw", slow=True)
        xt = ph.sb("xt", [128, KC, 512])
        P.op("pool", lambda e: e.memset(xt[:], 1.0), [], ["E_xt"])
        sq = [ph.sb("sq", [128, 512], BF16) for _ in range(2)]; rstd = ph.sb("rstd", [128, 512])
        tmp = [ph.sb("tmp", [128, 512]) for _ in range(2)]
        hT = ph.sb("hT", [128, KC, 512], BF16); actT = ph.sb("actT", [128, FC, 512], BF16)
        wa = [ph.sb("wa", [128, KC, 256], BF16) for _ in range(2)]; wb_ = [ph.sb("wb", [128, KC, 256], BF16) for _ in range(2)]
        wo = [ph.sb("wo", [128, FC, 256], BF16) for _ in range(2)]
        ac = [ph.sb("ac", [128, 512]) for _ in range(2)]; sa = [ph.sb("sa", [128, 512]) for _ in range(2)]
        psn = ph.ps("psn"); pa = [ph.ps("pa") for _ in range(2)]; pb = [ph.ps("pb") for _ in range(2)]; pq = [ph.ps("pq") for _ in range(2)]
        Wi = Wfi[l].rearrange("(k p) c -> p k c", p=128); Wo_ = Wfo[l].rearrange("(k p) c -> p k c", p=128)
        Xi = Xin.rearrange("(k p) t -> p k t", p=128); Xo = Xout.rearrange("(k p) t -> p k t", p=128)
        wi = 0; woi = 0; ci_ = 0
        clE = cast_list(["mi", "uq", "ukv", "mo", "fi1", "fo1"]) if l == 0 else []
        ntile = sum(len(split(L, 510)) for (_, L, _, _) in ranges)
        perE = (len(clE) + ntile - 1) // ntile
        for (s0, L, r, ext) in ranges:
            for (s, e_) in split(L, 510):
                n = e_ - s; NW = n + 2
                emit_casts(clE, perE)
                lo = max(s - 1, 0); hi = min(e_ + 1, L + ext)
                off = lo - (s - 1)
                DMA("act", xt[:, :, off:off + hi - lo], Xi[:, :, s0 + lo:s0 + hi], ["Xin"], ["E_xt"], "E_xt")
                fm_norm_mod(ph, c, xt, NW, A, B, r, hT, ["E_xt"], "E_", (sq, psn, rstd), tmp)
                if s == 0:
                    P.op("dve", lambda e: e.memset(hT[:, :, 0:1], 0.0), ["E_hT"], ["E_hT"])
                if e_ == L and not ext:
                    P.op("dve", lambda e, NW=NW: e.memset(hT[:, :, NW - 1:NW], 0.0), ["E_hT"], ["E_hT"])
                for c2 in range(FC // 2):
                    w_ = wi % 2; wi += 1
                    W_ = "%d" % w_
                    DMA("sp", wa[w_][:], Wi[:, :, c2 * 256:(c2 + 1) * 256], ["wcast"], ["wa" + W_], "wa" + W_)
                    DMA("sp", wb_[w_][:], Wi[:, :, DFF + c2 * 256:DFF + (c2 + 1) * 256], ["wcast"], ["wb" + W_], "wb" + W_)
                    for cc in range(2):
                        ch = c2 * 2 + cc
                        p_ = ci_ % 2; ci_ += 1
                        P_ = "%d" % p_
                        for kc in range(KC):
                            MM(pa[p_][:, 0:NW], wa[w_][:, kc, cc * 128:(cc + 1) * 128], hT[:, kc, 0:NW], kc == 0, kc == KC - 1, ["wa" + W_, "E_hT"], ["pa" + P_])
                        for kc in range(KC):
                            MM(pb[p_][:, 0:NW], wb_[w_][:, kc, cc * 128:(cc + 1) * 128], hT[:, kc, 0:NW], kc == 0, kc == KC - 1, ["wb" + W_, "E_hT"], ["pb" + P_])
                        TSC("dve", ac[p_][:, 0:n], pa[p_][:, 1:n + 1], cw[:, 1, ch:ch + 1], None, ALU.mult, None, ["pa" + P_, "cw"], ["ac" + P_])
                        STT("dve", ac[p_][:, 0:n], pa[p_][:, 0:n], cw[:, 0, ch:ch + 1], ac[p_][:, 0:n], ALU.mult, ALU.add, ["pa" + P_, "cw", "ac" + P_], ["ac" + P_])
                        STT("dve", ac[p_][:, 0:n], pa[p_][:, 2:n + 2], cw[:, 2, ch:ch + 1], ac[p_][:, 0:n], ALU.mult, ALU.add, ["pa" + P_, "cw", "ac" + P_], ["ac" + P_])
                        ACT(sa[p_][:, 0:n], ac[p_][:, 0:n], AF.Silu, ["ac" + P_], ["sa" + P_])
                        TT("dve", actT[:, ch, 0:n], sa[p_][:, 0:n], pb[p_][:, 1:n + 1], ALU.mult, ["sa" + P_, "pb" + P_], ["actT"])
                for f2 in range(8):
                    w_ = woi % 2; woi += 1
                    W_ = "%d" % w_
                    for k0 in range(0, FC, 11):
                        DMA("sp", wo[w_][:, k0:k0 + 11, :], Wo_[:, k0:k0 + 11, f2 * 256:(f2 + 1) * 256], ["wcast", "wo" + W_], ["wo" + W_], "wo" + W_)
                    for f1 in range(2):
                        fc = f2 * 2 + f1
                        q_ = fc % 2
                        for kc in range(FC):
                            MM(pq[q_][:, 0:n], wo[w_][:, kc, f1 * 128:(f1 + 1) * 128], actT[:, kc, 0:n], kc == 0, kc == FC - 1, ["wo" + W_, "actT"], ["pq%d" % q_])
                        STT("dve", xt[:, fc, 1:n + 1], pq[q_][:, 0:n], m[5][:, r, fc:fc + 1], xt[:, fc, 1:n + 1], ALU.mult, ALU.add,
                            ["pq%d" % q_, "modT5", "E_xt"], ["E_xt"])
                DMA("act", Xo[:, :, s0 + s:s0 + e_], xt[:, :, 1:n + 1], ["E_xt"], ["Xout"], "E_xs")
        ph.end()


    TK = PAST + T0
    QNT = scr("QNT", [16, 128, T0], BF16); QPT = scr("QPT", [16, 64, T0], BF16)
    KNT = scr("KNT", [16, 128, TK], BF16); KPT = scr("KPT", [64, TK], BF16); VV = scr("VV", [16, TK, 128], BF16)
    AOT = scr("AOT", [D, T0], BF16)
    MLA_SCALE = 192 ** -0.5

    def phase_F():
        ph = Phase()
        c = consts(ph)
        A, B = make_AB(ph, 1, n1g[1], 1, 0, "F_")
        qg_ = load_vecT(ph, qng, 4, "qng"); kg_ = load_vecT(ph, kvng, 4, "kvng")
        NT = 256
        wmi = ph.sb("wmi", [128, KC, 1088], BF16); wmr = ph.sb("wmr", [128, KC, 64], BF16)
        wuq = ph.sb("wuq", [128, 4, 3072], BF16); wur = ph.sb("wur", [128, 4, 16, 64], BF16)
        wkv = ph.sb("wkv", [128, 4, 4096], BF16)
        for k0 in range(0, KC, 4):
            DMA("sp", wmi[:, k0:k0 + 4, :], Wmi.rearrange("(k p) c -> p k c", p=128)[:, k0:k0 + 4, :], ["wcast", "wmi"], ["wmi"], "wmi")
        DMA("sp", wuq[:], Wuq.rearrange("(k p) c -> p k c", p=128), ["wcast"], ["wuq"], "wuq")
        DMA("sp", wkv[:], Wukv.rearrange("(k p) c -> p k c", p=128), ["wcast"], ["wkv"], "wkv")
        wuqv = wuq[:].rearrange("p k (h c) -> p k h c", c=192)
        for (d0, s0_, sign) in [(0, 16, -1.0), (16, 0, 1.0), (32, 48, -1.0), (48, 32, 1.0)]:
            TSC("dve", wmr[:, :, d0:d0 + 16], wmi[:, :, 1024 + s0_:1024 + s0_ + 16], sign, None, ALU.mult, None, ["wmi"], ["wmr"])
            for k in range(4):
                TSC("dve", wur[:, k, :, d0:d0 + 16], wuqv[:, k, :, 128 + s0_:128 + s0_ + 16], sign, None, ALU.mult, None, ["wuq"], ["wur"])
        xt_ = [ph.sb("xt", [128, KC, NT]) for _ in range(2)]; sq = [ph.sb("sq", [128, 512], BF16) for _ in range(2)]; rstd = ph.sb("rstd", [128, 512])
        tmp = [ph.sb("tmp", [128, 512]) for _ in range(2)]; hT_ = [ph.sb("hT", [128, KC, NT], BF16) for _ in range(2)]
        cq = ph.sb("cq", [128, 4, NT]); ckv = ph.sb("ckv", [128, 4, NT])
        cqn = ph.sb("cqn", [128, 4, NT], BF16); ckn = ph.sb("ckn", [128, 4, NT], BF16)
        rope = ph.sb("rope", [64, 2, NT]); t1 = ph.sb("t1", [64, NT]); t2 = ph.sb("t2", [64, NT])
        kraw = ph.sb("kraw", [64, NT])
        stb = [ph.sb("stb", [128, 512], BF16) for _ in range(4)]
        tok = [ph.sb("tok", [128, 512]) for _ in range(2)]
        cin = ph.sb("cin", [128, 512]); kin = ph.sb("kin", [128, 64])
        psn = ph.ps("psn"); pp = [ph.ps("pp") for _ in range(3)]; pr_ = [ph.ps("pr") for _ in range(2)]; ptr = [ph.ps("ptr") for _ in range(2)]
        ppi = [0]; sbi = [0]; pri = [0]

        def nextp():
            i = ppi[0] % 3; ppi[0] += 1
            return pp[i], "pp%d" % i

        def nexts():
            i = sbi[0] % 4; sbi[0] += 1
            return stb[i], "stb%d" % i

        def expand_kv(N, kbase):
            for h in range(16):
                p_, pk = nextp()
                for kc in range(4):
                    MM(p_[:, 0:N], wkv[:, kc, h * 256:h * 256 + 128], ckn[:, kc, 0:N], kc == 0, kc == 3, ["wkv", "ckn"], [pk])
                sb_, sk = nexts()
                if h % 2 == 0:
                    ACT(sb_[:, 0:N], p_[:, 0:N], AF.Copy, [pk], [sk])
                else:
                    CP("dve", sb_[:, 0:N], p_[:, 0:N], [pk], [sk])
                DMA("act", KNT[h][:, kbase:kbase + N], sb_[:, 0:N], [sk], ["KNT"], sk)
            wv = wkv[:].rearrange("p k (h c) -> p k h c", c=256)
            for s in range(N // 128):
                for h4 in range(4):
                    p_, pk = nextp()
                    for kc in range(4):
                        MM(p_[:].rearrange("p (h c) -> p h c", c=128), ckn[:, kc, s * 128:(s + 1) * 128], wv[:, kc, h4 * 4:(h4 + 1) * 4, 128:256],
                           kc == 0, kc == 3, ["wkv", "ckn"], [pk])
                    sb_, sk = nexts()
                    CP("dve", sb_[:], p_[:], [pk], [sk])
                    DMA("act", VV[h4 * 4:(h4 + 1) * 4, kbase + s * 128:kbase + (s + 1) * 128, :].rearrange("h t e -> t h e"),
                        sb_[:].rearrange("p (h e) -> p h e", e=128), [sk], ["VV"], sk)

        for s in range(PAST // 128):
            DMA("sp", cin[:], cckv[s * 128:(s + 1) * 128, :], [], ["cin"], "cin")
            DMA("sp", kin[:], ckpe[s * 128:(s + 1) * 128, :], [], ["kin"], "kin")
            p_, pk = nextp()
            for k4 in range(4):
                TR(p_[:, k4 * 128:(k4 + 1) * 128], cin[:, k4 * 128:(k4 + 1) * 128], c["id"][:], ["cin", "c_id"], [pk])
            CP("dve", ckn[:, :, s * 128:(s + 1) * 128], p_[:].rearrange("p (k t) -> p k t", k=4), [pk], ["ckn"])
            p2, pk2 = nextp()
            TR(p2[0:64, 0:128], kin[:, :], c["id"][:], ["kin", "c_id"], [pk2])
            sb_, sk = nexts()
            CP("dve", sb_[0:64, 0:128], p2[0:64, 0:128], [pk2], [sk])
            DMA("act", KPT[:, s * 128:(s + 1) * 128], sb_[0:64, 0:128], [sk], ["KPT"], sk)
        expand_kv(PAST, 0)
        XAv = XA.rearrange("(k p) t -> p k t", p=128)
        for t0 in range(0, T0, NT):
            N = NT; t1_ = t0 + N
            r = row_of(t0); lat = t0 < TS
            tb_ = (t0 // NT) % 2
            xt = xt_[tb_]; hT = hT_[tb_]; HK = "F_hT%d" % tb_; XK = "F_xt%d" % tb_
            DMA("act", xt[:], XAv[:, :, t0:t1_], ["XA"], [XK], XK)
            if lat:
                DMA("sp", rope[:], c_rope[:, :, t0:t1_].rearrange("a d t -> d a t"), [], ["rope"], "rope")
            fm_norm_mod(ph, c, xt, N, A, B, r, hT, [XK], "F_", (sq, psn, rstd), tmp, hkey=HK)
            for j in range(8):
                p_, pk = nextp()
                for kc in range(KC):
                    MM(p_[:, 0:N], wmi[:, kc, j * 128:(j + 1) * 128], hT[:, kc, 0:N], kc == 0, kc == KC - 1, ["wmi", HK], [pk])
                dst = (cq if j < 4 else ckv)[:, j % 4, 0:N]
                ACT(dst, p_[:, 0:N], AF.Copy, [pk], ["cq" if j < 4 else "ckv"])

            def roped(praw, prot, pkr, pkt, out_bf, outkey):
                if lat:
                    TT("dve", t1[:, 0:N], praw, rope[:, 0, 0:N], ALU.mult, [pkr, "rope"], ["t1"])
                    TT("dve", t2[:, 0:N], prot, rope[:, 1, 0:N], ALU.mult, [pkt, "rope"], ["t2"])
                    TT("pool", out_bf, t1[:, 0:N], t2[:, 0:N], ALU.add, ["t1", "t2"], [outkey])
                else:
                    CP("dve", out_bf, praw, [pkr], [outkey])

            a_ = pri[0] % 2; pri[0] += 1
            for kc in range(KC):
                MM(pr_[a_][0:64, 0:N], wmi[:, kc, 1024:1088], hT[:, kc, 0:N], kc == 0, kc == KC - 1, ["wmi", HK], ["pr%d" % a_])
            for kc in range(KC):
                MM(pr_[a_][0:64, 256:256 + N], wmr[:, kc, :], hT[:, kc, 0:N], kc == 0, kc == KC - 1, ["wmr", HK], ["pr%d" % a_])
            sb_, sk = nexts()
            roped(pr_[a_][0:64, 0:N], pr_[a_][0:64, 256:256 + N], "pr%d" % a_, "pr%d" % a_, sb_[0:64, 0:N], sk)
            DMA("act", KPT[:, PAST + t0:PAST + t1_], sb_[0:64, 0:N], [sk], ["KPT"], sk)
            if not lat:
                CP("dve", kraw[:, 0:N], pr_[a_][0:64, 0:N], ["pr%d" % a_], ["kraw"])
                for s in range(N // 128):
                    b = s % 2
                    TR(ptr[b][:, 0:64], kraw[:, s * 128:(s + 1) * 128], c["id"][0:64, 0:64], ["kraw", "c_id"], ["ptr%d" % b])
                    CP("dve", tok[b][:, 0:64], ptr[b][:, 0:64], ["ptr%d" % b], ["tok%d" % b])
                    DMA("act", o_kpe[t0 - TS + s * 128:t0 - TS + (s + 1) * 128, :], tok[b][:, 0:64], ["tok%d" % b], ["o_kpe"], "tok%d" % b)
            fm_rstd(ph, c, ckv, 4, N, 512, ["ckv"], "F_", (sq, psn, rstd))
            for kc in range(4):
                STT("dve", ckv[:, kc, 0:N], ckv[:, kc, 0:N], kg_[:, kc:kc + 1], rstd[:, 0:N], ALU.mult, ALU.mult, ["ckv", "kvng", "F_rstd"], ["ckv"])
            CP("pool", ckn[:, :, 0:N], ckv[:, :, 0:N], ["ckv"], ["ckn"])
            if not lat:
                for s in range(N // 128):
                    b = s % 2
                    for k4 in range(4):
                        TR(ptr[b][:, k4 * 128:(k4 + 1) * 128], ckv[:, k4, s * 128:(s + 1) * 128], c["id"][:], ["ckv", "c_id"], ["ptr%d" % b])
                    CP("dve", tok[b][:], ptr[b][:], ["ptr%d" % b], ["tok%d" % b])
                    DMA("act", o_ckv[t0 - TS + s * 128:t0 - TS + (s + 1) * 128, :], tok[b][:], ["tok%d" % b], ["o_ckv"], "tok%d" % b)
            expand_kv(N, PAST + t0)
            if lat and t0 >= LQE:
                continue
            fm_rstd(ph, c, cq, 4, N, 512, ["cq"], "F_", (sq, psn, rstd))
            for kc in range(4):
                STT("dve", cqn[:, kc, 0:N], cq[:, kc, 0:N], qg_[:, kc:kc + 1], rstd[:, 0:N], ALU.mult, ALU.mult, ["cq", "qng", "F_rstd"], ["cqn"])
            for h in range(16):
                p_, pk = nextp()
                for kc in range(4):
                    MM(p_[:, 0:N], wuq[:, kc, h * 192:h * 192 + 128], cqn[:, kc, 0:N], kc == 0, kc == 3, ["wuq", "cqn"], [pk])
                sb_, sk = nexts()
                ACT(sb_[:, 0:N], p_[:, 0:N], AF.Copy, [pk], [sk])
                DMA("act", QNT[h][:, t0:t1_], sb_[:, 0:N], [sk], ["QNT"], sk)
                a_ = pri[0] % 2; pri[0] += 1
                for kc in range(4):
                    MM(pr_[a_][0:64, 0:N], wuq[:, kc, h * 192 + 128:h * 192 + 192], cqn[:, kc, 0:N], kc == 0, kc == 3, ["wuq", "cqn"], ["pr%d" % a_])
                for kc in range(4):
                    MM(pr_[a_][0:64, 256:256 + N], wur[:, kc, h, :], cqn[:, kc, 0:N], kc == 0, kc == 3, ["wur", "cqn"], ["pr%d" % a_])
                sb_, sk = nexts()
                roped(pr_[a_][0:64, 0:N], pr_[a_][0:64, 256:256 + N], "pr%d" % a_, "pr%d" % a_, sb_[0:64, 0:N], sk)
                DMA("act", QPT[h][:, t0:t1_], sb_[0:64, 0:N], [sk], ["QPT"], sk)
        ph.end()

    def phase_G():
        ph = Phase()
        c = consts(ph)
        NKM = PAST + TS
        kn = [ph.sb("kn", [128, NKM], BF16) for _ in range(2)]
        vv = [ph.sb("vv", [128, NKM // 128, 128], BF16) for _ in range(2)]
        kp = ph.sb("kp", [64, NKM], BF16)
        qn = [ph.sb("qn", [128, 512], BF16) for _ in range(2)]; qp = [ph.sb("qp", [64, 512], BF16) for _ in range(2)]
        pT = [ph.sb("pT", [128, 512], BF16) for _ in range(3)]
        rec = ph.sb("rec", [128, 512]); ao = [ph.sb("ao", [128, 512], BF16) for _ in range(2)]

        pss = [ph.ps("pss") for _ in range(3)]; po = [ph.ps("po") for _ in range(2)]; pd = [ph.ps("pd") for _ in range(2)]
        steps = []
        hi_ = 0; qi = 0
        for (s0, L, kind) in seqs:
            if kind == "s":
                k0, NK = 0, PAST + TS
            else:
                k0, NK = PAST + s0, L
            NKT = NK // 128
            first_seq = True
            for h in range(16):
                hb = hi_ % 2; hi_ += 1
                first_h = True
                for (s_, e_) in split(LQE if kind == "s" else L, 512):
                    qb = qi % 2; qi += 1
                    for kt in range(NKT):
                        steps.append(dict(s0=s0, k0=k0, NK=NK, NKT=NKT, h=h, hb=hb, qb=qb, s=s_, e=e_, kt=kt,
                                          ld_seq=first_seq and kt == 0, ld_h=first_h and kt == 0, ld_q=kt == 0))
                        first_seq = False; first_h = False

        def emit_S(i, st):
            sb = i % 3
            SB = "%d" % sb; HB = "%d" % st["hb"]; QB = "%d" % st["qb"]
            hb, qb, h, k0, NK, NKT, s0 = st["hb"], st["qb"], st["h"], st["k0"], st["NK"], st["NKT"], st["s0"]
            n = st["e"] - st["s"]
            if st["ld_seq"]:
                DMA("sp", kp[:, 0:NK], KPT[:, k0:k0 + NK], ["KPT"], ["kp"], "kp")
            if st["ld_h"]:
                DMA("sp", kn[hb][:, 0:NK], KNT[h][:, k0:k0 + NK], ["KNT"], ["kn" + HB], "kn" + HB)
                for kt0 in range(0, NKT, 8):
                    kt1 = min(NKT, kt0 + 8)
                    DMA("act", vv[hb][:, kt0:kt1, :], VV[h][k0 + kt0 * 128:k0 + kt1 * 128, :].rearrange("(t p) e -> p t e", p=128),
                        ["VV", "vv" + HB], ["vv" + HB], "vv" + HB)
            if st["ld_q"]:
                DMA("sp", qn[qb][:, 0:n], QNT[h][:, s0 + st["s"]:s0 + st["e"]], ["QNT"], ["qn" + QB], "qn" + QB)
                DMA("sp", qp[qb][:, 0:n], QPT[h][:, s0 + st["s"]:s0 + st["e"]], ["QPT"], ["qp" + QB], "qp" + QB)
            ks = slice(st["kt"] * 128, (st["kt"] + 1) * 128)
            MM(pss[sb][:, 0:n], kn[hb][:, ks], qn[qb][:, 0:n], True, False, ["kn" + HB, "qn" + QB], ["pss" + SB])
            MM(pss[sb][:, 0:n], kp[:, ks], qp[qb][:, 0:n], False, True, ["kp", "qp" + QB], ["pss" + SB])
            ACT(pT[sb][:, 0:n], pss[sb][:, 0:n], AF.Exp, ["pss" + SB], ["pT" + SB], scale=MLA_SCALE)

        def emit_PV(i, st):
            sb = i % 3
            SB = "%d" % sb; HB = "%d" % st["hb"]; QB = "%d" % st["qb"]
            hb, qb, h, NKT, s0, kt = st["hb"], st["qb"], st["h"], st["NKT"], st["s0"], st["kt"]
            n = st["e"] - st["s"]
            MM(po[qb][:, 0:n], vv[hb][:, kt, :], pT[sb][:, 0:n], kt == 0, kt == NKT - 1, ["vv" + HB, "pT" + SB], ["po" + QB])
            MM(pd[qb][:, 0:n], c["ones"][:], pT[sb][:, 0:n], kt == 0, kt == NKT - 1, ["c_ones", "pT" + SB], ["pd" + QB])
            if kt == NKT - 1:
                P.op("dve", lambda e, qb=qb, n=n: e.reciprocal(out=rec[:, 0:n], in_=pd[qb][:, 0:n]), ["pd" + QB], ["rec"])
                TT("dve", ao[qb][:, 0:n], po[qb][:, 0:n], rec[:, 0:n], ALU.mult, ["po" + QB, "rec"], ["ao" + QB])
                DMA("act", AOT[h * 128:(h + 1) * 128, s0 + st["s"]:s0 + st["e"]], ao[qb][:, 0:n], ["ao" + QB], ["AOT"], "ao" + QB)

        emit_S(0, steps[0])
        for i in range(len(steps)):
            if i + 1 < len(steps):
                emit_S(i + 1, steps[i + 1])
            emit_PV(i, steps[i])
        ph.end()

    def phase_H():
        ph = Phase()
        m = load_mod(ph, 1, (2,))
        aT = ph.sb("aT", [128, KC, 512], BF16); x0 = ph.sb("x0", [128, KC, 512])
        wsl = [ph.sb("wsl", [128, KC, 512], BF16) for _ in range(2)]
        pq = [ph.ps("pq") for _ in range(2)]
        Wv = Wmo.rearrange("(k p) c -> p k c", p=128)
        XAv = XA.rearrange("(k p) t -> p k t", p=128); XBv = XB.rearrange("(k p) t -> p k t", p=128)
        wi = 0
        hr = [(a_, b_) for (a_, b_) in split(LQE, 512)] + [(t, min(t + 512, T0)) for t in range(TS, T0, 512)]
        for (t0, t1h) in hr:
            r = row_of(t0); n = t1h - t0
            DMA("act", x0[:, :, 0:n], XAv[:, :, t0:t1h], ["XA"], ["x0"], "x0")
            DMA("sp", aT[:, :, 0:n], AOT.rearrange("(k p) t -> p k t", p=128)[:, :, t0:t1h], ["AOT"], ["aT"], "aT")
            for f4 in range(4):
                wb = wi % 2; wi += 1
                DMA("sp", wsl[wb][:], Wv[:, :, f4 * 512:(f4 + 1) * 512], ["wcast"], ["wsl%d" % wb], "wsl%d" % wb)
                for f1 in range(4):
                    fc = f4 * 4 + f1
                    pb = fc % 2
                    for kc in range(KC):
                        MM(pq[pb][:, 0:n], wsl[wb][:, kc, f1 * 128:(f1 + 1) * 128], aT[:, kc, 0:n], kc == 0, kc == KC - 1, ["wsl%d" % wb, "aT"], ["pq%d" % pb])
                    STT("dve", x0[:, fc, 0:n], pq[pb][:, 0:n], m[2][:, r, fc:fc + 1], x0[:, fc, 0:n], ALU.mult, ALU.add, ["pq%d" % pb, "modT2", "x0"], ["x0"])
            DMA("act", XBv[:, :, t0:t1h], x0[:, :, 0:n], ["x0"], ["XB"], "x0s")
        ph.end()

    def phase_Z():
        ph = Phase()
        c = consts(ph)
        g = load_vecT(ph, fng, KC, "fng")
        xt = ph.sb("xt", [128, KC, 512]); sq = [ph.sb("sq", [128, 512], BF16) for _ in range(2)]; rstd = ph.sb("rstd", [128, 512])
        yt = [ph.sb("yt", [128, 2048]) for _ in range(2)]
        psn = ph.ps("psn"); pt = [ph.ps("pt") for _ in range(4)]
        XAv = XA.rearrange("(k p) t -> p k t", p=128)
        yi = 0
        for t0 in list(range(0, LQ, 512)) + list(range(TS, T0, 512)):
            DMA("act", xt[:], XAv[:, :, t0:t0 + 512], ["XA"], ["Z_xt"], "Z_xt")
            fm_rstd(ph, c, xt, KC, 512, D, ["Z_xt"], "Z_", (sq, psn, rstd))
            for kc in range(KC):
                STT("dve", xt[:, kc, :], xt[:, kc, :], g[:, kc:kc + 1], rstd[:], ALU.mult, ALU.mult, ["Z_xt", "fng", "Z_rstd"], ["Z_xt"])
            for s in range(4):
                b = yi % 2; yi += 1
                for q in range(4):
                    for k4 in range(4):
                        kc = q * 4 + k4
                        TR(pt[q][:, k4 * 128:(k4 + 1) * 128], xt[:, kc, s * 128:(s + 1) * 128], c["id"][:], ["Z_xt", "c_id"], ["pt%d" % q])
                    if q % 2 == 0:
                        CP("dve", yt[b][:, q * 512:(q + 1) * 512], pt[q][:], ["pt%d" % q], ["yt%d" % b])
                    else:
                        ACT(yt[b][:, q * 512:(q + 1) * 512], pt[q][:], AF.Copy, ["pt%d" % q], ["yt%d" % b])
                tt = t0 + s * 128
                dst = y_s[tt:tt + 128, :] if tt < TS else y_p[tt - TS:tt - TS + 128, :]
                DMA("sp", dst, yt[b][:], ["yt%d" % b], ["y"], "yt%d" % b)
        ph.end()

    L0_ranges = [(s0, L, 0 if k == "s" else 1, 0) for (s0, L, k) in seqs]
    L1_ranges = [((s0, LQ, 0, EXT) if k == "s" else (s0, L, 1, 0)) for (s0, L, k) in seqs]
    phases = [("T", phase_WMT), ("A", phase_A), ("B", lambda: phase_scan(0)), ("C", lambda: phase_scan(1)),
              ("D", phase_D), ("E", lambda: phase_E(0, XB, XA, L0_ranges, T0)),
              ("F", phase_F), ("G", phase_G), ("H", phase_H), ("I", lambda: phase_E(1, XB, XA, L1_ranges, T0)), ("Z", phase_Z)]
    for nm, fn in phases:
        fn()
        if nm == stop_after:
            break
    if dump:
        ph = Phase()
        t = ph.sb("dd", [128, KC, 512])
        XD = XA if dump == "XA" else XB
        for t0 in range(0, T0, 512):
            DMA("sp", t[:], XD.rearrange("(k p) t -> p k t", p=128)[:, :, t0:t0 + 512], ["XA"], ["dd"], "dd")
            DMA("sp", dbg.rearrange("(k p) t -> p k t", p=128)[:, :, t0:t0 + 512], t[:], ["dd"], ["dbg"], "dd2")
        ph.end()
    st.close()
    return nc, P


def host_consts(TS, mir=False, CH=128):
    c = {}
    c["c_ident"] = np.eye(128, dtype=np.float32)
    j = np.arange(128)[:, None]; i = np.arange(128)[None, :]
    same = (j // CH) == (i // CH)
    tri = np.zeros((4, 128, 128), np.float32)
    tri[0] = np.where(same & (j <= i), -1 / 16.0, 0.0)
    tri[1] = np.where(same & (j >= i), -1 / 16.0, 0.0)
    tri[2] = np.where(same & (j > i), -1 / 16.0, 0.0)
    tri[3] = np.where(same & (j < i), -1 / 16.0, 0.0)
    c["c_tri"] = tri
    jj = np.arange(CH)[:, None]; ii = np.arange(CH)[None, :]
    m = np.zeros((2, CH, 8 * CH), np.float32)
    m[0] = np.tile((ii >= jj).astype(np.float32), (1, 8))
    m[1] = np.tile((ii <= jj).astype(np.float32), (1, 8))
    c["c_mask"] = m
    p = np.arange(128)
    pos = np.zeros((128, 4), np.float32)
    pos[:, 0] = (p % CH) + 1
    pos[:, 1] = CH - (p % CH)
    c["c_pos"] = pos
    t = np.arange(TS)
    if mir:
        t = TS - 1 - t
    inv = 10000.0 ** (-np.arange(16, dtype=np.float64) * 2.0 / 32)
    d = np.arange(64)
    posd = np.where((d // 32)[:, None] == 0, (t // 64)[None, :], (t % 64)[None, :]).astype(np.float64)
    ang = posd * inv[d % 16][:, None]
    c["c_rope"] = np.stack([np.cos(ang), np.sin(ang)], 0).astype(np.float32)
    return c


def prep_core(inp, cid, cfg):
    TS, TP = cfg["TS"], cfg["TP"]
    mir = bool(cfg.get("mirror", False)) and (cid % 2 == 1)
    b = cid // 2
    p0 = 2 * cid
    f = lambda a: np.ascontiguousarray(np.asarray(a, dtype=np.float32))
    m = {}
    if mir:
        m["xs"] = f(inp["x_sample"][b][::-1])
        m["xp"] = f(inp["x_prompt"][p0:p0 + 2][:, ::-1].reshape(2 * TP, D))
        m["sg"] = f(inp["state_gla"][b, 0][::-1]); m["sr"] = f(inp["state_ret"][b, 0][::-1])
    else:
        m["xs"] = f(inp["x_sample"][b])
        m["xp"] = f(inp["x_prompt"][p0:p0 + 2].reshape(2 * TP, D))
        m["sg"] = f(inp["state_gla"][b, 0]); m["sr"] = f(inp["state_ret"][b, 0])
    m["cckv"] = f(inp["cache_ckv"][b, 0]); m["ckpe"] = f(inp["cache_kpe"][b, 0])
    m["cond"] = f(np.stack([inp["c"][b], inp["c_ctx"]], 0))
    for k in ("mod_w", "mod_b", "norm1_g", "norm2_g", "ffn_w_in", "ffn_conv", "ffn_w_out", "final_norm_g"):
        m[k] = f(inp[k])
    for k in ("ab_w_in", "gla_gate_w2", "gla_gate_b", "gla_norm_g", "ret_norm_g", "ab_w_out", "mla_w_in", "mla_q_norm_g",
              "mla_w_uq", "mla_kv_norm_g", "mla_w_ukv", "mla_w_out"):
        m[k] = f(inp[k][0])
    m["ret_decay"] = f(inp["ret_decay"][0].reshape(8))
    if mir:
        w = np.array(m["ab_w_in"])
        w[:, C_GLR:C_GLR + 16] = m["ab_w_in"][:, C_GLR + 16:C_GLR + 32]
        w[:, C_GLR + 16:C_GLR + 32] = m["ab_w_in"][:, C_GLR:C_GLR + 16]
        m["ab_w_in"] = w
        m["gla_gate_w2"] = f(m["gla_gate_w2"][::-1]); m["gla_gate_b"] = f(m["gla_gate_b"][::-1])
        m["ret_decay"] = f(inp["ret_decay"][0][::-1].reshape(8))
        m["ffn_conv"] = f(m["ffn_conv"][:, ::-1, :])
    m.update(host_consts(TS, mir, cfg.get("CH", 128)))
    return m


_CACHE = {}


def kernel(**inputs):
    inputs = {k: np.asarray(v) for k, v in inputs.items()}
    TS = inputs["x_sample"].shape[1]
    TP = inputs["x_prompt"].shape[1]
    PAST = inputs["cache_ckv"].shape[2]
    NB = inputs["x_sample"].shape[0]
    BP = inputs["x_prompt"].shape[0]
    n = 8
    cfg = dict(TS=TS, TP=TP, PAST=PAST, LQ=TS // 2, mirror=True)
    key = (TS, TP, PAST)
    if key not in _CACHE:
        _CACHE[key] = build(cfg)
    nc, _ = _CACHE[key]
    in_maps = [prep_core(inputs, cid, cfg) for cid in range(n)]
    res = run_bass_kernel_spmd(nc, in_maps, core_ids=list(range(n)))
    R_ = res.results
    y_prompt = np.zeros((BP, TP, D), np.float32)
    y_sample = np.zeros((NB, TS, D), np.float32)
    ns_gla = np.zeros((BP, 1, 2, 4, 128, 256), np.float32)
    ns_ret = np.zeros((BP, 1, 2, 4, 128, 256), np.float32)
    n_ckv = np.zeros((BP, 1, TP, 512), np.float32)
    n_kpe = np.zeros((BP, 1, TP, 64), np.float32)
    for cid in range(n):
        r = R_[cid]
        b = cid // 2
        half = TS // 2
        p0 = 2 * cid
        if cid % 2 == 0:
            y_sample[b, :half] = r["y_s"][:half]
            y_prompt[p0:p0 + 2] = r["y_p"].reshape(2, TP, D)
            ns_gla[p0:p0 + 2, 0] = r["ns_gla"]
            ns_ret[p0:p0 + 2, 0] = r["ns_ret"]
            n_ckv[p0:p0 + 2, 0] = r["n_ckv"].reshape(2, TP, 512)
            n_kpe[p0:p0 + 2, 0] = r["n_kpe"].reshape(2, TP, 64)
        else:
            y_sample[b, half:] = r["y_s"][:half][::-1]
            y_prompt[p0:p0 + 2] = r["y_p"].reshape(2, TP, D)[:, ::-1]
            ns_gla[p0:p0 + 2, 0] = r["ns_gla"][:, ::-1]
            ns_ret[p0:p0 + 2, 0] = r["ns_ret"][:, ::-1]
            n_ckv[p0:p0 + 2, 0] = r["n_ckv"].reshape(2, TP, 512)[:, ::-1]
            n_kpe[p0:p0 + 2, 0] = r["n_kpe"].reshape(2, TP, 64)[:, ::-1]
    return (y_prompt, y_sample, ns_gla, ns_ret, n_ckv, n_kpe)
```

```python
import math
from contextlib import ExitStack
import numpy as np
import ml_dtypes
import concourse.bass as bass
import concourse.mybir as mybir
from concourse.bass_utils import run_bass_kernel_spmd

F32 = mybir.dt.float32
BF16 = mybir.dt.bfloat16
AF = mybir.ActivationFunctionType
ALU = mybir.AluOpType

ENGS = ("pe", "act", "dve", "pool", "sp")
SEM_WRAP = 3500


class Op:
    __slots__ = ("eng", "fn", "reads", "writes", "dma", "idx", "waits", "sig", "extra")

    def __init__(self, eng, fn, reads, writes, dma):
        self.eng = eng
        self.fn = fn
        self.reads = reads
        self.writes = writes
        self.dma = dma
        self.waits = []
        self.sig = None
        self.extra = ()


def _stream(o):
    return ("d", o.dma) if o.dma is not None else ("e", o.eng)


class Prog:
    def __init__(self, nc, stack):
        self.nc = nc
        self.stack = stack
        self.ops = []
        self.sems = {}
        self.sigcount = {}
        self.nops_total = 0

    def op(self, eng, fn, reads=(), writes=(), dma=None):
        if dma is not None:
            km = self.__dict__.setdefault("dmakeys", {})
            if dma not in km:
                pre = "g" if eng == "pool" else "q"
                km[dma] = "%s%d" % (pre, sum(1 for v in km.values() if v[0] == pre))
            dma = km[dma]
        o = Op(eng, fn, tuple(reads), tuple(writes), dma)
        o.idx = len(self.ops)
        self.ops.append(o)
        return o

    def _resolve(self):
        ops = self.ops
        last_w, readers = {}, {}
        deps_of = []
        for o in ops:
            deps = set(o.extra)
            for r in o.reads:
                w = last_w.get(r)
                if w is not None:
                    deps.add(w.idx)
            for w_ in o.writes:
                w = last_w.get(w_)
                if w is not None:
                    deps.add(w.idx)
                for rd in readers.get(w_, ()):
                    deps.add(rd.idx)
            deps.discard(o.idx)
            deps_of.append(deps)
            for r in o.reads:
                readers.setdefault(r, []).append(o)
            for w_ in o.writes:
                last_w[w_] = o
                readers[w_] = []
        pos, cnt = {}, {}
        for o in ops:
            s = _stream(o)
            cnt[s] = cnt.get(s, 0) + 1
            pos[o.idx] = cnt[s]
        waited = {e: {} for e in ENGS}
        need_sig = set()
        for o in ops:
            e = o.eng
            best = {}
            for d in deps_of[o.idx]:
                p = ops[d]
                s = _stream(p)
                if s == ("e", e):
                    if e == "pe":
                        continue
                    raw = any(b in p.writes for b in o.reads) or any(b in p.writes for b in o.writes)
                    if not raw and d not in o.extra:
                        continue
                if pos[d] > best.get(s, 0):
                    best[s] = pos[d]
            for s, v in best.items():
                if waited[e].get(s, 0) >= v:
                    continue
                waited[e][s] = v
                o.waits.append((s, v))
                need_sig.add((s, v))
        sigval = {}
        for o in ops:
            s = _stream(o)
            if (s, pos[o.idx]) in need_sig or o.dma is not None:
                self.sigcount[s] = self.sigcount.get(s, 0) + 1
                sigval[(s, pos[o.idx])] = self.sigcount[s]
                o.sig = (s, self.sigcount[s])
        for o in ops:
            o.waits = [(s, sigval[(s, v)]) for (s, v) in o.waits]

    def _semval(self, s, v):
        k = (v - 1) // SEM_WRAP
        val = (v - 1) % SEM_WRAP + 1
        lst = self.sems.setdefault(s, [])
        while len(lst) <= k:
            nm = ("s_%s_%s_%d" % (s[0], s[1], len(lst))).replace(" ", "")
            lst.append(self.stack.enter_context(self.nc.semaphore(nm)))
        return lst[k], val * (16 if s[0] == "d" else 1)

    def flush(self):
        nc = self.nc
        if not self.ops:
            return
        last = {}
        for o in self.ops:
            last[_stream(o)] = o.idx
        j = self.op("sp", lambda e: e.nop())
        j.extra = tuple(last.values())
        self._resolve()
        by_eng = {e: [o for o in self.ops if o.eng == e] for e in ENGS}
        semval = self._semval

        def run(engobj, ops):
            for o in ops:
                for (s, v) in o.waits:
                    sem, val = semval(s, v)
                    engobj.wait_ge(sem, val)
                ins = o.fn(engobj)
                if o.sig is not None:
                    s, v = o.sig
                    sem, _ = semval(s, v)
                    ins.then_inc(sem, 16 if s[0] == "d" else 1)

        with nc.Block() as block:
            if by_eng["pe"]:
                @block.tensor
                def _(e):
                    run(e, by_eng["pe"])
            if by_eng["act"]:
                @block.scalar
                def _(e):
                    run(e, by_eng["act"])
            if by_eng["dve"]:
                @block.vector
                def _(e):
                    run(e, by_eng["dve"])
            if by_eng["pool"]:
                @block.gpsimd
                def _(e):
                    run(e, by_eng["pool"])
            if by_eng["sp"]:
                @block.sync
                def _(e):
                    run(e, by_eng["sp"])
        self.nops_total += len(self.ops)
        self.ops = []
        self.dmakeys = {}
        nc.all_engine_barrier()


D = 2048
KC = 16
DFF = 5632
FC = 44
EPS = 1e-6
AB_IN = 6176
C_GQ, C_GK, C_GV, C_GG, C_GLR, C_RQ, C_RK, C_RV, C_RG = 0, 512, 1024, 2048, 3072, 3104, 3616, 4128, 5152


def split(L, w):
    n = (L + w - 1) // w
    ww = (L + n - 1) // n
    return [(s, min(L, s + ww)) for s in range(0, L, ww)]


def build(cfg):
    TS, TP, PAST, NPR = cfg["TS"], cfg["TP"], cfg["PAST"], 2
    LQ = cfg.get("LQ", TS)
    dump = cfg.get("dump", None)
    stop_after = cfg.get("stop", "Z")
    T0 = TS + NPR * TP
    CH = cfg.get("CH", 128)
    NCH = T0 // CH
    NCT = 512 // CH
    EXT = 1 if LQ < TS else 0
    LQE = LQ + EXT
    nc = bass.Bass("TRN2", target_bir_lowering=False)
    I = {}

    def inp(name, shape, dt=F32):
        I[name] = nc.dram_tensor(name, list(shape), dt, kind="ExternalInput").ap()
        return I[name]

    def outp(name, shape):
        return nc.dram_tensor(name, list(shape), F32, kind="ExternalOutput").ap()

    def scr(name, shape, dt=F32):
        return nc.dram_tensor(name, list(shape), dt).ap()

    xs = inp("xs", [TS, D]); xp = inp("xp", [NPR * TP, D])
    sg_in = inp("sg", [2, 4, 128, 256]); sr_in = inp("sr", [2, 4, 128, 256])
    cckv = inp("cckv", [PAST, 512]); ckpe = inp("ckpe", [PAST, 64])
    cond = inp("cond", [2, D])
    mod_w = inp("mod_w", [2, D, 6 * D]); mod_b = inp("mod_b", [2, 6 * D])
    n1g = inp("norm1_g", [2, D]); n2g = inp("norm2_g", [2, D])
    w_ab = inp("ab_w_in", [D, AB_IN]); gw2 = inp("gla_gate_w2", [2, 16, 512]); gb = inp("gla_gate_b", [2, 512])
    rdec = inp("ret_decay", [8]); gng = inp("gla_norm_g", [1024]); rng_ = inp("ret_norm_g", [1024])
    w_abo = inp("ab_w_out", [D, D])
    w_mi = inp("mla_w_in", [D, 1088]); qng = inp("mla_q_norm_g", [512]); w_uq = inp("mla_w_uq", [512, 3072])
    kvng = inp("mla_kv_norm_g", [512]); w_ukv = inp("mla_w_ukv", [512, 4096]); w_mo = inp("mla_w_out", [D, D])
    w_fi = inp("ffn_w_in", [2, D, 2 * DFF]); fconv = inp("ffn_conv", [2, 3, DFF]); w_fo = inp("ffn_w_out", [2, DFF, D])
    fng = inp("final_norm_g", [D])
    c_id = inp("c_ident", [128, 128]); c_tri = inp("c_tri", [4, 128, 128]); c_mask = inp("c_mask", [2, CH, 8 * CH])
    c_pos = inp("c_pos", [128, 4]); c_rope = inp("c_rope", [2, 64, TS])

    y_s = outp("y_s", [LQ, D]); y_p = outp("y_p", [NPR * TP, D])
    o_sg = outp("ns_gla", [NPR, 2, 4, 128, 256]); o_sr = outp("ns_ret", [NPR, 2, 4, 128, 256])
    o_ckv = outp("n_ckv", [NPR * TP, 512]); o_kpe = outp("n_kpe", [NPR * TP, 64])
    dbg = outp("dbg", [D, T0]) if dump else None

    XA = scr("XA", [D, T0]); XB = scr("XB", [D, T0])
    Wab = scr("Wab", [D, AB_IN], BF16); Wabo = scr("Wabo", [D, D], BF16)
    Wmi = scr("Wmi", [D, 1088], BF16); Wuq = scr("Wuq", [512, 3072], BF16); Wukv = scr("Wukv", [512, 4096], BF16)
    Wmo = scr("Wmo", [D, D], BF16)
    Wfi = scr("Wfi", [2, D, 2 * DFF], BF16); Wfo = scr("Wfo", [2, DFF, D], BF16)
    modD = scr("modD", [2, 2, 6 * D])
    QDT = scr("QDT", [2, 4, 128, T0], BF16); KIT = scr("KIT", [2, 4, 128, T0], BF16)
    QRT = scr("QRT", [4, 128, T0], BF16); KRT = scr("KRT", [4, 128, T0], BF16)
    KE = scr("KE", [2, T0, 512], BF16); KR = scr("KR", [T0, 512], BF16)
    VG = scr("VG", [T0, 1024], BF16); VR = scr("VR", [2, T0, 1024], BF16); GG = scr("GG", [T0, 2048], BF16)
    SDD = scr("SDD", [2, 128, NCH, 4])
    OF = scr("OF", [T0, 2048]); OT = scr("OT", [T0, 2048])

    st = ExitStack()
    P = Prog(nc, st)
    cnt = [0]

    def uid(p):
        cnt[0] += 1
        return "%s%d" % (p, cnt[0])

    def DMA(eng, out, in_, reads, writes, key, slow=False):
        if slow:
            P.op(eng, lambda e: e.dma_start(out=out, in_=in_, allow_slow_non_contiguous=True), reads, writes, dma=key)
        else:
            P.op(eng, lambda e: e.dma_start(out=out, in_=in_), reads, writes, dma=key)

    def ACT(out, in_, func, reads, writes, scale=1.0, bias=0.0, accum=None):
        if accum is None:
            P.op("act", lambda e: e.activation(out=out, in_=in_, func=func, bias=bias, scale=scale), reads, writes)
        else:
            P.op("act", lambda e: e.activation(out=out, in_=in_, func=func, bias=bias, scale=scale, accum_out=accum), reads, writes)

    def TSC(eng, out, in0, s1, s2, op0, op1, reads, writes):
        if s2 is None:
            P.op(eng, lambda e: e.tensor_scalar(out=out, in0=in0, scalar1=s1, scalar2=None, op0=op0), reads, writes)
        else:
            P.op(eng, lambda e: e.tensor_scalar(out=out, in0=in0, scalar1=s1, scalar2=s2, op0=op0, op1=op1), reads, writes)

    def TT(eng, out, in0, in1, op, reads, writes):
        P.op(eng, lambda e: e.tensor_tensor(out=out, in0=in0, in1=in1, op=op), reads, writes)

    def STT(eng, out, in0, scalar, in1, op0, op1, reads, writes):
        P.op(eng, lambda e: e.scalar_tensor_tensor(out=out, in0=in0, scalar=scalar, in1=in1, op0=op0, op1=op1), reads, writes)

    def CP(eng, out, in_, reads, writes):
        P.op(eng, lambda e: e.tensor_copy(out=out, in_=in_), reads, writes)

    def MM(out, lhsT, rhs, start, stop, reads, writes):
        P.op("pe", lambda e: e.matmul(out, lhsT, rhs, start=start, stop=stop), reads, writes)

    def TR(out, in_, ident, reads, writes):
        P.op("pe", lambda e: e.transpose(out, in_, ident), reads, writes)

    rr = [0]

    def ve():
        rr[0] ^= 1
        return "dve" if rr[0] else "pool"

    def xsrc_rows(t0, n):
        if t0 < TS:
            return xs[t0:t0 + n, :]
        return xp[t0 - TS:t0 - TS + n, :]

    def row_of(t0):
        return 0 if t0 < TS else 1

    seqs = [(0, TS, "s")] + [(TS + j * TP, TP, "p%d" % j) for j in range(NPR)]

    class Phase:
        def __init__(self):
            self.s = ExitStack()
            self.ps_n = 0

        def sb(self, name, shape, dt=F32):
            return self.s.enter_context(nc.sbuf_tensor(uid(name), list(shape), dt))

        def ps(self, name, shape=(128, 512), dt=F32):
            return self.s.enter_context(nc.psum_tensor(uid(name), list(shape), dt))

        def end(self):
            P.flush()
            self.s.close()

    def consts(ph, bf_ident=False):
        c = {}
        c["id"] = ph.sb("id", [128, 128])
        DMA("sp", c["id"][:], c_id[:, :], [], ["c_id"], "c_id")
        c["ones"] = ph.sb("ones", [128, 128], BF16)
        P.op("pool", lambda e: e.memset(c["ones"][:], 1.0), [], ["c_ones"])
        c["eps"] = ph.sb("eps", [128, 1])
        P.op("pool", lambda e: e.memset(c["eps"][:], EPS), [], ["c_eps"])
        if bf_ident:
            c["idb"] = ph.sb("idb", [128, 128], BF16)
            CP("dve", c["idb"][:], c["id"][:], ["c_id"], ["c_idb"])
        return c

    CASTS = {"ab": (Wab, w_ab, D, AB_IN), "abo": (Wabo, w_abo, D, D), "mi": (Wmi, w_mi, D, 1088), "uq": (Wuq, w_uq, 512, 3072),
             "ukv": (Wukv, w_ukv, 512, 4096), "mo": (Wmo, w_mo, D, D),
             "fi0": (Wfi[0], w_fi[0], D, 2 * DFF), "fo0": (Wfo[0], w_fo[0], DFF, D),
             "fi1": (Wfi[1], w_fi[1], D, 2 * DFF), "fo1": (Wfo[1], w_fo[1], DFF, D)}
    kcast = [0]

    def cast_list(names):
        out = []
        for nm in names:
            dst, src, R, C_ = CASTS[nm]
            rs = max(128, (1 << 21) // C_ // 128 * 128)
            for r0 in range(0, R, rs):
                out.append((dst[r0:min(R, r0 + rs), :], src[r0:min(R, r0 + rs), :]))
        return out

    def emit_casts(lst, n):
        for _ in range(min(n, len(lst))):
            dst, src = lst.pop(0)
            k = kcast[0]; kcast[0] += 1
            DMA("pool", dst, src, [], ["wcast%d" % k], "wc%d" % (k % 8))

    def gen_M(ph, c):
        cT = ph.sb("cT", [128, 2, KC]); csT = ph.sb("csT", [128, 2, KC])
        for r in range(2):
            DMA("sp", cT[:, r, :], cond[r].rearrange("(c p) -> p c", p=128), ["cT"], ["cT"], "cT", slow=True)
        ACT(csT[:], cT[:], AF.Silu, ["cT"], ["csT"])
        wm = [ph.sb("wm", [128, KC, 512]) for _ in range(2)]
        mb = [ph.sb("mb", [2, 512]) for _ in range(2)]
        mrow = [ph.sb("mrow", [2, 512]) for _ in range(2)]
        pm = [ph.ps("pm") for _ in range(2)]
        for l in range(2):
            for j in range(24):
                b = j % 2
                DMA("sp", wm[b][:], mod_w[l][:, j * 512:(j + 1) * 512].rearrange("(k p) c -> p k c", p=128), [], ["wm%d" % b], "wm%d" % b)
                DMA("sp", mb[b][:], mod_b[l, j * 512:(j + 1) * 512].partition_broadcast(2), [], ["mb%d" % b], "mb%d" % b)
                for kc in range(KC):
                    MM(pm[b][0:2, :], csT[:, :, kc], wm[b][:, kc, :], kc == 0, kc == KC - 1, ["csT", "wm%d" % b], ["pm%d" % b])
                TT("dve", mrow[b][:], pm[b][0:2, :], mb[b][:], ALU.add, ["pm%d" % b, "mb%d" % b], ["mrow%d" % b])
                DMA("act", modD[l][:, j * 512:(j + 1) * 512], mrow[b][:], ["mrow%d" % b], ["modD"], "mrow%d" % b)
                yield

    def load_mod(ph, l, which):
        m = {}
        for j in which:
            t = ph.sb("modT", [128, 2, KC])
            for r in range(2):
                DMA("sp", t[:, r, :], modD[l][r, j * D:(j + 1) * D].rearrange("(c p) -> p c", p=128), ["modD", "modT%d" % j], ["modT%d" % j], "modT%d" % j, slow=True)
            m[j] = t
        return m

    def load_vecT(ph, src1d, n, key):
        t = ph.sb(key, [128, n])
        DMA("sp", t[:], src1d.rearrange("(c p) -> p c", p=128), [], [key], key, slow=True)
        return t

    def make_AB(ph, l, gsrc, js, jb, key):
        m = load_mod(ph, l, (js, jb))
        g = load_vecT(ph, gsrc, KC, key + "g")
        A = ph.sb("A", [128, 2, KC])
        for r in range(2):
            STT("dve", A[:, r, :], m[js][:, r, :], 1.0, g[:], ALU.add, ALU.mult, ["modT%d" % js, key + "g"], [key + "A"])
        return A, m[jb]

    def fm_rstd(ph, c, xt, kcn, N, dn, rk, keyp, bufs):
        sq, psn, rstd = bufs
        for kc in range(kcn):
            b = kc % 2
            ACT(sq[b][:, 0:N], xt[:, kc, 0:N], AF.Square, rk, [keyp + "sq%d" % b])
            MM(psn[:, 0:N], c["ones"][:], sq[b][:, 0:N], kc == 0, kc == kcn - 1, [keyp + "sq%d" % b, "c_ones"], [keyp + "psn"])
        ACT(rstd[:, 0:N], psn[:, 0:N], AF.Sqrt, [keyp + "psn", "c_eps"], [keyp + "rstd"], scale=1.0 / dn, bias=c["eps"][:, 0:1])
        P.op("dve", lambda e: e.reciprocal(out=rstd[:, 0:N], in_=rstd[:, 0:N]), [keyp + "rstd"], [keyp + "rstd"])

    def fm_norm_mod(ph, c, xt, N, A, B, r, hT, rk, keyp, bufs, tmp, hkey=None):
        fm_rstd(ph, c, xt, KC, N, D, rk, keyp, bufs)
        rstd = bufs[2]
        for kc in range(KC):
            b = kc % 2
            TT(ve() if False else "dve", tmp[b][:, 0:N], xt[:, kc, 0:N], rstd[:, 0:N], ALU.mult, rk + [keyp + "rstd"], [keyp + "tmp%d" % b])
            ACT(hT[:, kc, 0:N], tmp[b][:, 0:N], AF.Identity, [keyp + "tmp%d" % b, keyp + "A"], [hkey or (keyp + "hT")],
                scale=A[:, r, kc:kc + 1], bias=B[:, r, kc:kc + 1])

    def gen_T(ph, c):
        xt = [ph.sb("xt", [128, D]) for _ in range(2)]
        xT = [ph.sb("xT", [128, KC, 128]) for _ in range(2)]
        pt = [ph.ps("pt") for _ in range(4)]
        for i in range(T0 // 128):
            b = i % 2
            DMA("sp", xt[b][:], xsrc_rows(i * 128, 128), [], ["xt%d" % b], "xt%d" % b)
            for q in range(4):
                for k4 in range(4):
                    kc = q * 4 + k4
                    TR(pt[q][:, k4 * 128:(k4 + 1) * 128], xt[b][:, kc * 128:(kc + 1) * 128], c["id"][:], ["xt%d" % b, "c_id"], ["pt%d" % q])
                if q % 2 == 0:
                    CP("dve", xT[b][:, q * 4:(q + 1) * 4, :], pt[q][:].rearrange("p (k t) -> p k t", k=4), ["pt%d" % q], ["xT%d" % b])
                else:
                    ACT(xT[b][:, q * 4:(q + 1) * 4, :], pt[q][:].rearrange("p (k t) -> p k t", k=4), AF.Copy, ["pt%d" % q], ["xT%d" % b])
            DMA("act", XA.rearrange("(k p) t -> p k t", p=128)[:, :, i * 128:(i + 1) * 128], xT[b][:], ["xT%d" % b], ["XA"], "stx%d" % b)
            yield

    def phase_WMT():
        ph = Phase()
        c = consts(ph)
        cl = cast_list(["ab"])
        emit_casts(cl, len(cl))
        gens = [gen_M(ph, c), gen_T(ph, c)]
        while gens:
            for g_ in list(gens):
                try:
                    next(g_)
                except StopIteration:
                    gens.remove(g_)
        ph.end()

    def phase_A():
        ph = Phase()
        c = consts(ph)
        A, B = make_AB(ph, 0, n1g[0], 1, 0, "A_")
        tri = ph.sb("tri", [128, 4, 128])
        DMA("sp", tri[:], c_tri.rearrange("k p t -> p k t"), [], ["tri"], "tri")
        pos = ph.sb("pos", [128, 4])
        DMA("sp", pos[:], c_pos[:, :], [], ["pos"], "pos")
        rd = ph.sb("rd", [128, 8]); lg = ph.sb("lg", [128, 8]); nlg = ph.sb("nlg", [128, 8])
        DMA("sp", rd[:], rdec.partition_broadcast(128), [], ["rd"], "rd")
        ACT(lg[:], rd[:], AF.Exp, ["rd"], ["lg"], scale=-1.0)
        ACT(lg[:], lg[:], AF.Ln, ["lg"], ["lg"], bias=1.0)
        TSC("dve", nlg[:], lg[:], -1.0, None, ALU.mult, None, ["lg"], ["nlg"])
        vsc = ph.sb("vsc", [128, 8])
        for d_ in range(2):
            for h in range(4):
                ACT(vsc[:, d_ * 4 + h:d_ * 4 + h + 1], pos[:, d_:d_ + 1], AF.Exp, ["pos", "lg"], ["vsc"], scale=lg[:, d_ * 4 + h:d_ * 4 + h + 1])
        w2f = ph.sb("w2f", [17, 2, 512]); w2b = ph.sb("w2b", [17, 2, 512], BF16)
        DMA("sp", w2f[0:16, :, :], gw2.rearrange("d k c -> k d c"), [], ["w2f"], "w2f")
        DMA("sp", w2f[16:17, :, :], gb.rearrange("(o d) c -> o d c", o=1), ["w2f"], ["w2f"], "w2f")
        CP("dve", w2b[:], w2f[:], ["w2f"], ["w2b"])
        xt = [ph.sb("xt", [128, KC, 512]) for _ in range(1)]
        sq = [ph.sb("sq", [128, 512], BF16) for _ in range(2)]
        rstd = ph.sb("rstd", [128, 512])
        tmp = [ph.sb("tmp", [128, 512]) for _ in range(2)]
        hT = ph.sb("hT", [128, KC, 512], BF16)
        wsl = [ph.sb("wsl", [128, KC, 512], BF16) for _ in range(4)]
        glrT = [ph.sb("glrT", [17, 512], BF16) for _ in range(2)]
        for d_ in range(2):
            P.op("pool", lambda e, d_=d_: e.memset(glrT[d_][:], 1.0), [], ["glrT%d" % d_])
        ex = ph.sb("ex", [128, 512]); la = [ph.sb("la", [128, 512]) for _ in range(2)]
        EB = [ph.sb("EB", [128, 4, 512]) for _ in range(2)]; EI = [ph.sb("EI", [128, 4, 512]) for _ in range(2)]
        EE = [ph.sb("EE", [128, 4, 512]) for _ in range(2)]
        sd = ph.sb("sd", [128, 2, NCT, 4])
        stf = [ph.sb("stf", [128, 512], BF16) for _ in range(4)]
        stt = [ph.sb("stt", [128, 2048], BF16) for _ in range(2)]
        psn = ph.ps("psn"); pg = ph.ps("pg"); pa = [ph.ps("pa") for _ in range(3)]; pb_ = [ph.ps("pb") for _ in range(2)]
        Wv = Wab.rearrange("(k p) c -> p k c", p=128)
        nslab = [0]

        def wslab(c0, cw):
            b = nslab[0] % 4
            nslab[0] += 1
            DMA("sp", wsl[b][:, :, 0:cw], Wv[:, :, c0:c0 + cw], ["wcast"], ["wsl%d" % b], "wsl%d" % b)
            return wsl[b], "wsl%d" % b

        sfi = [0]; sti = [0]; pai = [0]
        clA = cast_list(["abo", "fi0", "fo0"])
        perA = (len(clA) + (T0 // 512) - 1) // (T0 // 512)

        for (t0, t1) in [(t, t + 512) for t in range(0, T0, 512)]:
            N = 512
            r = row_of(t0)
            emit_casts(clA, perA)
            DMA("act", xt[0][:], XA.rearrange("(k p) t -> p k t", p=128)[:, :, t0:t1], ["XA"], ["A_xt"], "A_xt")
            fm_norm_mod(ph, c, xt[0], N, A, B, r, hT, ["A_xt"], "A_", (sq, psn, rstd), tmp)
            w, wk = wslab(C_GLR, 32)
            for d_ in range(2):
                for kc in range(KC):
                    MM(pg[0:16, :], w[:, kc, d_ * 16:(d_ + 1) * 16], hT[:, kc, :], kc == 0, kc == KC - 1, [wk, "A_hT"], ["pg"])
                CP("dve", glrT[d_][0:16, :], pg[0:16, :], ["pg"], ["glrT%d" % d_])
            for s in range(4):
                for d_ in range(2):
                    p_ = pa[pai[0] % 3]; pk = "pa%d" % (pai[0] % 3); pai[0] += 1
                    MM(p_[:], glrT[d_][:, s * 128:(s + 1) * 128], w2b[:, d_, :], True, True, ["glrT%d" % d_, "w2b"], [pk])
                    ACT(ex[:], p_[:], AF.Exp, [pk], ["ex"], scale=-1.0)
                    ACT(la[d_][:], ex[:], AF.Ln, ["ex"], ["la%d" % d_], bias=1.0)
                    p2 = pa[pai[0] % 3]; pk2 = "pa%d" % (pai[0] % 3); pai[0] += 1
                    for h in range(4):
                        MM(p2[:, h * 128:(h + 1) * 128], la[d_][:, h * 128:(h + 1) * 128], tri[:, d_, :], True, True, ["la%d" % d_, "tri"], [pk2])
                    ACT(EB[d_][:, :, s * 128:(s + 1) * 128], p2[:].rearrange("p (h t) -> p h t", h=4), AF.Exp, [pk2], ["EB%d" % d_])
                    ACT(EI[d_][:, :, s * 128:(s + 1) * 128], p2[:].rearrange("p (h t) -> p h t", h=4), AF.Exp, [pk2], ["EI%d" % d_], scale=-1.0)
                    p3 = pa[pai[0] % 3]; pk3 = "pa%d" % (pai[0] % 3); pai[0] += 1
                    MM(p3[:], tri[:, 2 + d_, :], la[d_][:], True, True, ["la%d" % d_, "tri"], [pk3])
                    ACT(EE[d_][:, s, :], p3[:], AF.Exp, [pk3], ["EE%d" % d_])
            for d_ in range(2):
                off = CH - 1 if d_ == 0 else 0
                CP("pool", sd[:, d_, :, :], EB[d_][:].rearrange("p h (c j) -> p c h j", j=CH)[:, :, :, off], ["EB%d" % d_], ["sd"])
                DMA("pool", SDD[d_][:, t0 // CH:t0 // CH + NCT, :], sd[:, d_, :, :], ["sd"], ["SDD"], "sd%d" % d_)
            for (c0, isq, gla) in [(C_GQ, True, True), (C_GK, False, True), (C_RQ, True, False), (C_RK, False, False)]:
                w, wk = wslab(c0, 512)
                for h in range(4):
                    p_ = pb_[h % 2]; pk = "pb%d" % (h % 2)
                    for kc in range(KC):
                        MM(p_[:], w[:, kc, h * 128:(h + 1) * 128], hT[:, kc, :], kc == 0, kc == KC - 1, [wk, "A_hT"], [pk])
                    if gla:
                        for d_ in range(2):
                            sb_ = stf[sfi[0] % 4]; sk = "stf%d" % (sfi[0] % 4); sfi[0] += 1
                            if isq:
                                STT("dve", sb_[:], p_[:], 128 ** -0.5, EB[d_][:, h, :], ALU.mult, ALU.mult, [pk, "EB%d" % d_], [sk])
                                DMA("pool", QDT[d_, h][:, t0:t1], sb_[:], [sk], ["QDT"], sk)
                            else:
                                TT("dve", sb_[:], p_[:], EI[d_][:, h, :], ALU.mult, [pk, "EI%d" % d_], [sk])
                                DMA("pool", KIT[d_, h][:, t0:t1], sb_[:], [sk], ["KIT"], sk)
                    else:
                        sb_ = stf[sfi[0] % 4]; sk = "stf%d" % (sfi[0] % 4); sfi[0] += 1
                        if isq:
                            ACT(sb_[:], p_[:], AF.Copy, [pk], [sk])
                            DMA("act", QRT[h][:, t0:t1], sb_[:], [sk], ["QRT"], sk)
                        else:
                            ACT(sb_[:], p_[:], AF.Copy, [pk], [sk], scale=128 ** -0.5)
                            DMA("act", KRT[h][:, t0:t1], sb_[:], [sk], ["KRT"], sk)
            for (c0, cw, kind) in [(C_GK, 512, "ke"), (C_RK, 512, "kr"), (C_GV, 512, "gv0"), (C_GV + 512, 512, "gv1"),
                                   (C_RV, 512, "rv0"), (C_RV + 512, 512, "rv1"),
                                   (C_GG, 512, "g0"), (C_GG + 512, 512, "g1"), (C_RG, 512, "g2"), (C_RG + 512, 512, "g3")]:
                w, wk = wslab(c0, cw)
                for s in range(4):
                    p_ = pa[pai[0] % 3]; pk = "pa%d" % (pai[0] % 3); pai[0] += 1
                    for kc in range(KC):
                        MM(p_[:], hT[:, kc, s * 128:(s + 1) * 128], w[:, kc, :], kc == 0, kc == KC - 1, [wk, "A_hT"], [pk])
                    rows = slice(t0 + s * 128, t0 + (s + 1) * 128)
                    sb_ = stt[sti[0] % 2]; sk = "stt%d" % (sti[0] % 2); sti[0] += 1
                    if kind == "ke":
                        for d_ in range(2):
                            TT("dve", sb_[:, d_ * 512:(d_ + 1) * 512], p_[:], EE[d_][:, s, :], ALU.mult, [pk, "EE%d" % d_], [sk])
                        DMA("pool", KE[0][rows, :], sb_[:, 0:512], [sk], ["KE"], sk)
                        DMA("pool", KE[1][rows, :], sb_[:, 512:1024], [sk], ["KE"], sk + "b")
                    elif kind == "kr":
                        ACT(sb_[:, 0:512], p_[:], AF.Copy, [pk], [sk], scale=128 ** -0.5)
                        DMA("act", KR[rows, :], sb_[:, 0:512], [sk], ["KR"], sk)
                    elif kind[:2] == "gv":
                        j = int(kind[2])
                        ACT(sb_[:, 0:512], p_[:], AF.Copy, [pk], [sk])
                        DMA("act", VG[rows, j * 512:(j + 1) * 512], sb_[:, 0:512], [sk], ["VG"], sk)
                    elif kind[:2] == "rv":
                        j = int(kind[2])
                        for d_ in range(2):
                            for hh in range(2):
                                h = j * 2 + hh
                                TSC("dve", sb_[:, d_ * 512 + hh * 256:d_ * 512 + (hh + 1) * 256], p_[:, hh * 256:(hh + 1) * 256],
                                    vsc[:, d_ * 4 + h:d_ * 4 + h + 1], None, ALU.mult, None, [pk, "vsc"], [sk])
                        DMA("pool", VR[0][rows, j * 512:(j + 1) * 512], sb_[:, 0:512], [sk], ["VR"], sk)
                        DMA("pool", VR[1][rows, j * 512:(j + 1) * 512], sb_[:, 512:1024], [sk], ["VR"], sk + "b")
                    else:
                        j = int(kind[1])
                        ACT(sb_[:, 0:512], p_[:], AF.Copy, [pk], [sk])
                        DMA("act", GG[rows, j * 512:(j + 1) * 512], sb_[:, 0:512], [sk], ["GG"], sk)
        ph.end()


    def phase_scan(d_):
        ph = Phase()
        msk = ph.sb("msk", [CH, 8 * CH])
        DMA("sp", msk[:], c_mask[d_], [], ["msk"], "msk")
        pos = ph.sb("pos", [128, 4])
        DMA("sp", pos[:], c_pos[:, :], [], ["pos"], "pos")
        rd = ph.sb("rd", [128, 8]); lg = ph.sb("lg", [128, 8])
        DMA("sp", rd[:], rdec.partition_broadcast(128), [], ["rd"], "rd")
        ACT(lg[:], rd[:], AF.Exp, ["rd"], ["lg"], scale=-1.0)
        ACT(lg[:], lg[:], AF.Ln, ["lg"], ["lg"], bias=1.0)
        nlg = ph.sb("nlg", [128, 8])
        TSC("dve", nlg[:], lg[:], -1.0, None, ALU.mult, None, ["lg"], ["nlg"])
        pcol = ph.sb("pcol", [128, 4]); eret = ph.sb("eret", [128, 4]); c64 = ph.sb("c64", [128, 1])
        P.op("pool", lambda e: e.memset(c64[:], float(CH)), [], ["c64"])
        for h in range(4):
            ACT(pcol[:, h:h + 1], pos[:, d_:d_ + 1], AF.Exp, ["pos", "nlg"], ["pcol"], scale=nlg[:, d_ * 4 + h:d_ * 4 + h + 1])
            ACT(eret[:, h:h + 1], c64[:], AF.Exp, ["c64", "nlg"], ["eret"], scale=nlg[:, d_ * 4 + h:d_ * 4 + h + 1])
        S = ph.sb("S", [128, 8, 256]); Sb = ph.sb("Sb", [128, 8, 256], BF16)
        tS = [ph.sb("tS", [128, 256]) for _ in range(2)]
        qg = [ph.sb("qg", [128, 4, 512], BF16) for _ in range(2)]; kg = [ph.sb("kg", [128, 4, 512], BF16) for _ in range(2)]
        qr = [ph.sb("qr", [128, 4, 512], BF16) for _ in range(2)]; kr = [ph.sb("kr", [128, 4, 512], BF16) for _ in range(2)]
        ke = [ph.sb("ke", [CH, NCT, 512], BF16) for _ in range(2)]; krr = [ph.sb("krr", [CH, NCT, 512], BF16) for _ in range(2)]
        vg = [ph.sb("vg", [CH, NCT, 1024], BF16) for _ in range(2)]; vr = [ph.sb("vr", [CH, NCT, 1024], BF16) for _ in range(2)]
        sdt = [ph.sb("sdt", [128, NCT, 4]) for _ in range(2)]
        ofc = [ph.sb("ofc", [CH, 2048]) for _ in range(2)]; oc = [ph.sb("oc", [CH, 2048]) for _ in range(2)]
        att = [ph.sb("att", [CH, 8 * CH], BF16) for _ in range(2)]
        pat = ph.ps("pat", (128, 8 * CH)); po = [ph.ps("po", (128, 1024)) for _ in range(2)]; pkv = [ph.ps("pkv") for _ in range(2)]
        ti = 0; cidx = 0
        SK = ["S%d" % h for h in range(8)]
        for (s0, L, kind) in seqs:
            if kind == "s":
                DMA("sp", S[:, 0:4, :], sg_in[d_].rearrange("h p e -> p h e"), [], SK[0:4], "S0")
                DMA("sp", S[:, 4:8, :], sr_in[d_].rearrange("h p e -> p h e"), [], SK[4:8], "S0b")
            else:
                P.op("pool", lambda e: e.memset(S[:], 0.0), [], SK)
            CP("pool", Sb[:], S[:], SK, ["Sb0", "Sb1"])
            tl = [(t, min(t + 512, s0 + L)) for t in range(s0, s0 + L, 512)]
            if d_ == 1:
                tl = tl[::-1]
            for (t0, t1) in tl:
                b = ti % 2; ti += 1
                n = t1 - t0; ncw = n // CH
                B_ = "%d" % b
                DMA("sp", qg[b][:, :, 0:n], QDT[d_].rearrange("h p t -> p h t")[:, :, t0:t1], ["QDT"], ["qg" + B_], "qg" + B_)
                DMA("sp", kg[b][:, :, 0:n], KIT[d_].rearrange("h p t -> p h t")[:, :, t0:t1], ["KIT"], ["kg" + B_], "kg" + B_)
                DMA("sp", qr[b][:, :, 0:n], QRT.rearrange("h p t -> p h t")[:, :, t0:t1], ["QRT"], ["qr" + B_], "qr" + B_)
                DMA("sp", kr[b][:, :, 0:n], KRT.rearrange("h p t -> p h t")[:, :, t0:t1], ["KRT"], ["kr" + B_], "kr" + B_)
                DMA("act", ke[b][:, 0:ncw, :], KE[d_][t0:t1, :].rearrange("(c j) f -> j c f", j=CH), ["KE"], ["ke" + B_], "ke" + B_)
                DMA("act", krr[b][:, 0:ncw, :], KR[t0:t1, :].rearrange("(c j) f -> j c f", j=CH), ["KR"], ["krr" + B_], "krr" + B_)
                DMA("act", vg[b][:, 0:ncw, :], VG[t0:t1, :].rearrange("(c j) f -> j c f", j=CH), ["VG"], ["vg" + B_], "vg" + B_)
                DMA("act", vr[b][:, 0:ncw, :], VR[d_][t0:t1, :].rearrange("(c j) f -> j c f", j=CH), ["VR"], ["vr" + B_], "vr" + B_)
                DMA("sp", sdt[b][:, 0:ncw, :], SDD[d_][:, t0 // CH:t0 // CH + ncw, :], ["SDD"], ["sdt" + B_], "sdt" + B_)
                cl = list(range(ncw))
                if d_ == 1:
                    cl = cl[::-1]
                for ci in cl:
                    cb = cidx % 2; cidx += 1
                    CB = "%d" % cb
                    cs = slice(ci * CH, (ci + 1) * CH)
                    rows = slice(t0 + ci * CH, t0 + (ci + 1) * CH)
                    if d_ == 1:
                        DMA("sp", ofc[cb][:], OF[rows, :], ["OF"], ["ofc" + CB], "ofc" + CB)
                    for h in range(8):
                        K_ = kg[b][:, h, cs] if h < 4 else kr[b][:, h - 4, cs]
                        Q_ = qg[b][:, h, cs] if h < 4 else qr[b][:, h - 4, cs]
                        MM(pat[0:CH, h * CH:(h + 1) * CH], K_, Q_, True, True, ["kg" + B_, "kr" + B_, "qg" + B_, "qr" + B_], ["pat"])
                    TT("dve", att[cb][:], pat[0:CH, :], msk[:], ALU.mult, ["pat", "msk"], ["att" + CB])
                    for g in range(2):
                        for hh in range(4):
                            h = g * 4 + hh
                            V_ = (vg[b] if g == 0 else vr[b])[:, ci, hh * 256:(hh + 1) * 256]
                            Q_ = qg[b][:, hh, cs] if g == 0 else qr[b][:, hh, cs]
                            MM(po[g][0:CH, hh * 256:(hh + 1) * 256], att[cb][:, h * CH:(h + 1) * CH], V_, True, False,
                               ["att" + CB, "vg" + B_, "vr" + B_], ["po%d" % g])
                            MM(po[g][0:CH, hh * 256:(hh + 1) * 256], Q_, Sb[:, h, :], False, True, ["qg" + B_, "qr" + B_, "Sb%d" % g], ["po%d" % g])
                    if d_ == 0:
                        ACT(oc[cb][:, 0:1024], po[0][0:CH, :], AF.Copy, ["po0"], ["oc" + CB])
                    else:
                        TT("dve", oc[cb][:, 0:1024], po[0][0:CH, :], ofc[cb][:, 0:1024], ALU.add, ["po0", "ofc" + CB], ["oc" + CB])
                    for hh in range(4):
                        o_ = oc[cb][:, 1024 + hh * 256:1024 + (hh + 1) * 256]
                        if d_ == 0:
                            TSC("dve", o_, po[1][0:CH, hh * 256:(hh + 1) * 256], pcol[0:CH, hh:hh + 1], None, ALU.mult, None, ["po1", "pcol"], ["oc" + CB])
                        else:
                            STT("dve", o_, po[1][0:CH, hh * 256:(hh + 1) * 256], pcol[0:CH, hh:hh + 1], ofc[cb][:, 1024 + hh * 256:1024 + (hh + 1) * 256],
                                ALU.mult, ALU.add, ["po1", "pcol", "ofc" + CB], ["oc" + CB])
                    DMA("sp", (OF if d_ == 0 else OT)[rows, :], oc[cb][:], ["oc" + CB], ["OF" if d_ == 0 else "OT"], "oc" + CB)
                    for g in range(2):
                        for hh in range(4):
                            h = g * 4 + hh
                            Ke_ = (ke[b] if g == 0 else krr[b])[:, ci, hh * 128:(hh + 1) * 128]
                            V_ = (vg[b] if g == 0 else vr[b])[:, ci, hh * 256:(hh + 1) * 256]
                            MM(pkv[hh // 2][:, (hh % 2) * 256:(hh % 2 + 1) * 256], Ke_, V_, True, True, ["ke" + B_, "krr" + B_, "vg" + B_, "vr" + B_], ["pkv%d" % (hh // 2)])
                        for hh in range(4):
                            h = g * 4 + hh
                            tb = h % 2
                            e1 = sdt[b][:, ci, hh:hh + 1] if g == 0 else eret[:, hh:hh + 1]
                            e2 = 1.0 if g == 0 else eret[:, hh:hh + 1]
                            ACT(tS[tb][:], S[:, h, :], AF.Copy, ["S%d" % h, "sdt" + B_, "eret"], ["tS%d" % tb], scale=e1)
                            STT("dve", S[:, h, :], pkv[hh // 2][:, (hh % 2) * 256:(hh % 2 + 1) * 256], e2, tS[tb][:], ALU.mult, ALU.add,
                                ["pkv%d" % (hh // 2), "tS%d" % tb, "eret"], ["S%d" % h])
                        ACT(Sb[:, g * 4:(g + 1) * 4, :], S[:, g * 4:(g + 1) * 4, :], AF.Copy, ["S%d" % (g * 4 + q_) for q_ in range(4)], ["Sb%d" % g])
            if kind != "s":
                j = int(kind[1])
                DMA("sp", o_sg[j, d_].rearrange("h p e -> p h e"), S[:, 0:4, :], SK[0:4], ["o_sg"], "So")
                DMA("sp", o_sr[j, d_].rearrange("h p e -> p h e"), S[:, 4:8, :], SK[4:8], ["o_sr"], "So2")
        ph.end()

    def phase_D():
        ph = Phase()
        c = consts(ph, bf_ident=True)
        m = load_mod(ph, 0, (2,))
        ngb = ph.sb("ngb", [128, 2048])
        DMA("sp", ngb[:, 0:1024], gng.partition_broadcast(128), [], ["ngb"], "ngb")
        DMA("sp", ngb[:, 1024:2048], rng_.partition_broadcast(128), ["ngb"], ["ngb"], "ngb")
        ot = [ph.sb("ot", [128, 2048]) for _ in range(2)]; gt = [ph.sb("gt", [128, 2048], BF16) for _ in range(2)]
        on_ = [ph.sb("on", [128, 2048]) for _ in range(2)]; sgm_ = [ph.sb("sgm", [128, 2048]) for _ in range(2)]
        y_ = [ph.sb("y", [128, 2048], BF16) for _ in range(2)]
        st8_ = [ph.sb("st8", [128, 4, 8]) for _ in range(2)]
        yT = ph.sb("yT", [128, KC, 512], BF16); x0 = ph.sb("x0", [128, KC, 512])
        wsl = [ph.sb("wsl", [128, KC, 512], BF16) for _ in range(2)]
        ptb = [ph.ps("ptb", (128, 512), BF16) for _ in range(2)]; pq = [ph.ps("pq") for _ in range(2)]
        Wv = Wabo.rearrange("(k p) c -> p k c", p=128)
        XAv = XA.rearrange("(k p) t -> p k t", p=128); XBv = XB.rearrange("(k p) t -> p k t", p=128)
        si = 0; wi = 0
        for t0 in range(0, T0, 512):
            r = row_of(t0)
            DMA("act", x0[:], XAv[:, :, t0:t0 + 512], ["XA"], ["x0"], "x0")
            for s in range(4):
                b = si % 2; si += 1
                B_ = "%d" % b
                on = on_[b]; sgm = sgm_[b]; y = y_[b]; st8 = st8_[b]
                rows = slice(t0 + s * 128, t0 + (s + 1) * 128)
                DMA("sp", ot[b][:], OT[rows, :], ["OT"], ["ot" + B_], "ot" + B_)
                DMA("sp", gt[b][:], GG[rows, :], ["GG"], ["gt" + B_], "gt" + B_)
                otv = ot[b][:].rearrange("p (h e) -> p h e", h=8)
                P.op("dve", lambda e, otv=otv, st8=st8: e.tensor_reduce(out=st8[:, 0, :], in_=otv, axis=mybir.AxisListType.X, op=ALU.add), ["ot" + B_], ["st8" + B_])
                TT("pool", on[:], ot[b][:], ot[b][:], ALU.mult, ["ot" + B_], ["on" + B_])
                onv = on[:].rearrange("p (h e) -> p h e", h=8)
                P.op("dve", lambda e, onv=onv, st8=st8: e.tensor_reduce(out=st8[:, 1, :], in_=onv, axis=mybir.AxisListType.X, op=ALU.add), ["on" + B_], ["st8" + B_])
                TSC("dve", st8[:, 0, :], st8[:, 0, :], 1.0 / 256, None, ALU.mult, None, ["st8" + B_], ["st8" + B_])
                TT("dve", st8[:, 2, :], st8[:, 0, :], st8[:, 0, :], ALU.mult, ["st8" + B_], ["st8" + B_])
                STT("dve", st8[:, 1, :], st8[:, 1, :], 1.0 / 256, st8[:, 2, :], ALU.mult, ALU.subtract, ["st8" + B_], ["st8" + B_])
                ACT(st8[:, 1, :], st8[:, 1, :], AF.Sqrt, ["st8" + B_, "c_eps"], ["st8" + B_], bias=c["eps"][:, 0:1])
                P.op("dve", lambda e, st8=st8: e.reciprocal(out=st8[:, 1, :], in_=st8[:, 1, :]), ["st8" + B_], ["st8" + B_])
                STT("dve", st8[:, 2, :], st8[:, 0, :], -1.0, st8[:, 1, :], ALU.mult, ALU.mult, ["st8" + B_], ["st8" + B_])
                for h in range(8):
                    ACT(on[:, h * 256:(h + 1) * 256], ot[b][:, h * 256:(h + 1) * 256], AF.Identity, ["ot" + B_, "st8" + B_], ["on" + B_],
                        scale=st8[:, 1, h:h + 1], bias=st8[:, 2, h:h + 1])
                ACT(sgm[:], gt[b][:], AF.Silu, ["gt" + B_], ["sgm" + B_])
                TT("pool", sgm[:], sgm[:], ngb[:], ALU.mult, ["sgm" + B_, "ngb"], ["sgm" + B_])
                TT("dve", y[:], on[:], sgm[:], ALU.mult, ["on" + B_, "sgm" + B_], ["y" + B_])
                for q in range(4):
                    pb = q % 2
                    for k4 in range(4):
                        kc = q * 4 + k4
                        TR(ptb[pb][:, k4 * 128:(k4 + 1) * 128], y[:, kc * 128:(kc + 1) * 128], c["idb"][:], ["y" + B_, "c_idb"], ["ptb%d" % pb])
                    CP("dve", yT[:, q * 4:(q + 1) * 4, s * 128:(s + 1) * 128], ptb[pb][:].rearrange("p (k t) -> p k t", k=4), ["ptb%d" % pb], ["yT"])
            for f4 in range(4):
                wb = wi % 2; wi += 1
                DMA("sp", wsl[wb][:], Wv[:, :, f4 * 512:(f4 + 1) * 512], ["wcast"], ["wsl%d" % wb], "wsl%d" % wb)
                for f1 in range(4):
                    fc = f4 * 4 + f1
                    pb = fc % 2
                    for kc in range(KC):
                        MM(pq[pb][:], wsl[wb][:, kc, f1 * 128:(f1 + 1) * 128], yT[:, kc, :], kc == 0, kc == KC - 1, ["wsl%d" % wb, "yT"], ["pq%d" % pb])
                    STT("dve", x0[:, fc, :], pq[pb][:], m[2][:, r, fc:fc + 1], x0[:, fc, :], ALU.mult, ALU.add, ["pq%d" % pb, "modT2", "x0"], ["x0"])
            DMA("act", XBv[:, :, t0:t0 + 512], x0[:], ["x0"], ["XB"], "x0s")
        ph.end()

    def phase_E(l, Xin, Xout, ranges, TT_):
        ph = Phase()
        c = consts(ph)
        A, B = make_AB(ph, l, n2g[l], 4, 3, "E_")
        m = load_mod(ph, l, (5,))
        cw = ph.sb("cw", [128, 3, FC])
        for j in range(3):
            DMA("sp", cw[:, j, :], fconv[l][j].rearrange("(c p) -> p c", p=128), ["cw"], ["cw"], "cw", slow=True)
        xt = ph.sb("xt", [128, KC, 512])
        P.op("pool", lambda e: e.memset(xt[:], 1.0), [], ["E_xt"])
        sq = [ph.sb("sq", [128, 512], BF16) for _ in range(2)]; rstd = ph.sb("rstd", [128, 512])
        tmp = [ph.sb("tmp", [128, 512]) for _ in range(2)]
        hT = ph.sb("hT", [128, KC, 512], BF16); actT = ph.sb("actT", [128, FC, 512], BF16)
        wa = [ph.sb("wa", [128, KC, 256], BF16) for _ in range(2)]; wb_ = [ph.sb("wb", [128, KC, 256], BF16) for _ in range(2)]
        wo = [ph.sb("wo", [128, FC, 256], BF16) for _ in range(2)]
        ac = [ph.sb("ac", [128, 512]) for _ in range(2)]; sa = [ph.sb("sa", [128, 512]) for _ in range(2)]
        psn = ph.ps("psn"); pa = [ph.ps("pa") for _ in range(2)]; pb = [ph.ps("pb") for _ in range(2)]; pq = [ph.ps("pq") for _ in range(2)]
        Wi = Wfi[l].rearrange("(k p) c -> p k c", p=128); Wo_ = Wfo[l].rearrange("(k p) c -> p k c", p=128)
        Xi = Xin.rearrange("(k p) t -> p k t", p=128); Xo = Xout.rearrange("(k p) t -> p k t", p=128)
        wi = 0; woi = 0; ci_ = 0
        clE = cast_list(["mi", "uq", "ukv", "mo", "fi1", "fo1"]) if l == 0 else []
        ntile = sum(len(split(L, 510)) for (_, L, _, _) in ranges)
        perE = (len(clE) + ntile - 1) // ntile
        for (s0, L, r, ext) in ranges:
            for (s, e_) in split(L, 510):
                n = e_ - s; NW = n + 2
                emit_casts(clE, perE)
                lo = max(s - 1, 0); hi = min(e_ + 1, L + ext)
                off = lo - (s - 1)
                DMA("act", xt[:, :, off:off + hi - lo], Xi[:, :, s0 + lo:s0 + hi], ["Xin"], ["E_xt"], "E_xt")
                fm_norm_mod(ph, c, xt, NW, A, B, r, hT, ["E_xt"], "E_", (sq, psn, rstd), tmp)
                if s == 0:
                    P.op("dve", lambda e: e.memset(hT[:, :, 0:1], 0.0), ["E_hT"], ["E_hT"])
                if e_ == L and not ext:
                    P.op("dve", lambda e, NW=NW: e.memset(hT[:, :, NW - 1:NW], 0.0), ["E_hT"], ["E_hT"])
                for c2 in range(FC // 2):
                    w_ = wi % 2; wi += 1
                    W_ = "%d" % w_
                    DMA("sp", wa[w_][:], Wi[:, :, c2 * 256:(c2 + 1) * 256], ["wcast"], ["wa" + W_], "wa" + W_)
                    DMA("sp", wb_[w_][:], Wi[:, :, DFF + c2 * 256:DFF + (c2 + 1) * 256], ["wcast"], ["wb" + W_], "wb" + W_)
                    for cc in range(2):
                        ch = c2 * 2 + cc
                        p_ = ci_ % 2; ci_ += 1
                        P_ = "%d" % p_
                        for kc in range(KC):
                            MM(pa[p_][:, 0:NW], wa[w_][:, kc, cc * 128:(cc + 1) * 128], hT[:, kc, 0:NW], kc == 0, kc == KC - 1, ["wa" + W_, "E_hT"], ["pa" + P_])
                        for kc in range(KC):
                            MM(pb[p_][:, 0:NW], wb_[w_][:, kc, cc * 128:(cc + 1) * 128], hT[:, kc, 0:NW], kc == 0, kc == KC - 1, ["wb" + W_, "E_hT"], ["pb" + P_])
                        TSC("dve", ac[p_][:, 0:n], pa[p_][:, 1:n + 1], cw[:, 1, ch:ch + 1], None, ALU.mult, None, ["pa" + P_, "cw"], ["ac" + P_])
                        STT("dve", ac[p_][:, 0:n], pa[p_][:, 0:n], cw[:, 0, ch:ch + 1], ac[p_][:, 0:n], ALU.mult, ALU.add, ["pa" + P_, "cw", "ac" + P_], ["ac" + P_])
                        STT("dve", ac[p_][:, 0:n], pa[p_][:, 2:n + 2], cw[:, 2, ch:ch + 1], ac[p_][:, 0:n], ALU.mult, ALU.add, ["pa" + P_, "cw", "ac" + P_], ["ac" + P_])
                        ACT(sa[p_][:, 0:n], ac[p_][:, 0:n], AF.Silu, ["ac" + P_], ["sa" + P_])
                        TT("dve", actT[:, ch, 0:n], sa[p_][:, 0:n], pb[p_][:, 1:n + 1], ALU.mult, ["sa" + P_, "pb" + P_], ["actT"])
                for f2 in range(8):
                    w_ = woi % 2; woi += 1
                    W_ = "%d" % w_
                    for k0 in range(0, FC, 11):
                        DMA("sp", wo[w_][:, k0:k0 + 11, :], Wo_[:, k0:k0 + 11, f2 * 256:(f2 + 1) * 256], ["wcast", "wo" + W_], ["wo" + W_], "wo" + W_)
                    for f1 in range(2):
                        fc = f2 * 2 + f1
                        q_ = fc % 2
                        for kc in range(FC):
                            MM(pq[q_][:, 0:n], wo[w_][:, kc, f1 * 128:(f1 + 1) * 128], actT[:, kc, 0:n], kc == 0, kc == FC - 1, ["wo" + W_, "actT"], ["pq%d" % q_])
                        STT("dve", xt[:, fc, 1:n + 1], pq[q_][:, 0:n], m[5][:, r, fc:fc + 1], xt[:, fc, 1:n + 1], ALU.mult, ALU.add,
                            ["pq%d" % q_, "modT5", "E_xt"], ["E_xt"])
                DMA("act", Xo[:, :, s0 + s:s0 + e_], xt[:, :, 1:n + 1], ["E_xt"], ["Xout"], "E_xs")
        ph.end()


    TK = PAST + T0
    QNT = scr("QNT", [16, 128, T0], BF16); QPT = scr("QPT", [16, 64, T0], BF16)
    KNT = scr("KNT", [16, 128, TK], BF16); KPT = scr("KPT", [64, TK], BF16); VV = scr("VV", [16, TK, 128], BF16)
    AOT = scr("AOT", [D, T0], BF16)
    MLA_SCALE = 192 ** -0.5

    def phase_F():
        ph = Phase()
        c = consts(ph)
        A, B = make_AB(ph, 1, n1g[1], 1, 0, "F_")
        qg_ = load_vecT(ph, qng, 4, "qng"); kg_ = load_vecT(ph, kvng, 4, "kvng")
        NT = 256
        wmi = ph.sb("wmi", [128, KC, 1088], BF16); wmr = ph.sb("wmr", [128, KC, 64], BF16)
        wuq = ph.sb("wuq", [128, 4, 3072], BF16); wur = ph.sb("wur", [128, 4, 16, 64], BF16)
        wkv = ph.sb("wkv", [128, 4, 4096], BF16)
        for k0 in range(0, KC, 4):
            DMA("sp", wmi[:, k0:k0 + 4, :], Wmi.rearrange("(k p) c -> p k c", p=128)[:, k0:k0 + 4, :], ["wcast", "wmi"], ["wmi"], "wmi")
        DMA("sp", wuq[:], Wuq.rearrange("(k p) c -> p k c", p=128), ["wcast"], ["wuq"], "wuq")
        DMA("sp", wkv[:], Wukv.rearrange("(k p) c -> p k c", p=128), ["wcast"], ["wkv"], "wkv")
        wuqv = wuq[:].rearrange("p k (h c) -> p k h c", c=192)
        for (d0, s0_, sign) in [(0, 16, -1.0), (16, 0, 1.0), (32, 48, -1.0), (48, 32, 1.0)]:
            TSC("dve", wmr[:, :, d0:d0 + 16], wmi[:, :, 1024 + s0_:1024 + s0_ + 16], sign, None, ALU.mult, None, ["wmi"], ["wmr"])
            for k in range(4):
                TSC("dve", wur[:, k, :, d0:d0 + 16], wuqv[:, k, :, 128 + s0_:128 + s0_ + 16], sign, None, ALU.mult, None, ["wuq"], ["wur"])
        xt_ = [ph.sb("xt", [128, KC, NT]) for _ in range(2)]; sq = [ph.sb("sq", [128, 512], BF16) for _ in range(2)]; rstd = ph.sb("rstd", [128, 512])
        tmp = [ph.sb("tmp", [128, 512]) for _ in range(2)]; hT_ = [ph.sb("hT", [128, KC, NT], BF16) for _ in range(2)]
        cq = ph.sb("cq", [128, 4, NT]); ckv = ph.sb("ckv", [128, 4, NT])
        cqn = ph.sb("cqn", [128, 4, NT], BF16); ckn = ph.sb("ckn", [128, 4, NT], BF16)
        rope = ph.sb("rope", [64, 2, NT]); t1 = ph.sb("t1", [64, NT]); t2 = ph.sb("t2", [64, NT])
        kraw = ph.sb("kraw", [64, NT])
        stb = [ph.sb("stb", [128, 512], BF16) for _ in range(4)]
        tok = [ph.sb("tok", [128, 512]) for _ in range(2)]
        cin = ph.sb("cin", [128, 512]); kin = ph.sb("kin", [128, 64])
        psn = ph.ps("psn"); pp = [ph.ps("pp") for _ in range(3)]; pr_ = [ph.ps("pr") for _ in range(2)]; ptr = [ph.ps("ptr") for _ in range(2)]
        ppi = [0]; sbi = [0]; pri = [0]

        def nextp():
            i = ppi[0] % 3; ppi[0] += 1
            return pp[i], "pp%d" % i

        def nexts():
            i = sbi[0] % 4; sbi[0] += 1
            return stb[i], "stb%d" % i

        def expand_kv(N, kbase):
            for h in range(16):
                p_, pk = nextp()
                for kc in range(4):
                    MM(p_[:, 0:N], wkv[:, kc, h * 256:h * 256 + 128], ckn[:, kc, 0:N], kc == 0, kc == 3, ["wkv", "ckn"], [pk])
                sb_, sk = nexts()
                if h % 2 == 0:
                    ACT(sb_[:, 0:N], p_[:, 0:N], AF.Copy, [pk], [sk])
                else:
                    CP("dve", sb_[:, 0:N], p_[:, 0:N], [pk], [sk])
                DMA("sp", KNT[h][:, kbase:kbase + N], sb_[:, 0:N], [sk], ["KNT"], sk)
            wv = wkv[:].rearrange("p k (h c) -> p k h c", c=256)
            for s in range(N // 128):
                for h4 in range(4):
                    p_, pk = nextp()
                    for kc in range(4):
                        MM(p_[:].rearrange("p (h c) -> p h c", c=128), ckn[:, kc, s * 128:(s + 1) * 128], wv[:, kc, h4 * 4:(h4 + 1) * 4, 128:256],
                           kc == 0, kc == 3, ["wkv", "ckn"], [pk])
                    sb_, sk = nexts()
                    CP("dve", sb_[:], p_[:], [pk], [sk])
                    DMA("sp", VV[h4 * 4:(h4 + 1) * 4, kbase + s * 128:kbase + (s + 1) * 128, :].rearrange("h t e -> t h e"),
                        sb_[:].rearrange("p (h e) -> p h e", e=128), [sk], ["VV"], sk)

        for s in range(PAST // 128):
            DMA("sp", cin[:], cckv[s * 128:(s + 1) * 128, :], [], ["cin"], "cin")
            DMA("sp", kin[:], ckpe[s * 128:(s + 1) * 128, :], [], ["kin"], "kin")
            p_, pk = nextp()
            for k4 in range(4):
                TR(p_[:, k4 * 128:(k4 + 1) * 128], cin[:, k4 * 128:(k4 + 1) * 128], c["id"][:], ["cin", "c_id"], [pk])
            CP("dve", ckn[:, :, s * 128:(s + 1) * 128], p_[:].rearrange("p (k t) -> p k t", k=4), [pk], ["ckn"])
            p2, pk2 = nextp()
            TR(p2[0:64, 0:128], kin[:, :], c["id"][:], ["kin", "c_id"], [pk2])
            sb_, sk = nexts()
            CP("dve", sb_[0:64, 0:128], p2[0:64, 0:128], [pk2], [sk])
            DMA("sp", KPT[:, s * 128:(s + 1) * 128], sb_[0:64, 0:128], [sk], ["KPT"], sk)
        expand_kv(PAST, 0)
        XAv = XA.rearrange("(k p) t -> p k t", p=128)
        for t0 in range(0, T0, NT):
            N = NT; t1_ = t0 + N
            r = row_of(t0); lat = t0 < TS
            tb_ = (t0 // NT) % 2
            xt = xt_[tb_]; hT = hT_[tb_]; HK = "F_hT%d" % tb_; XK = "F_xt%d" % tb_
            DMA("act", xt[:], XAv[:, :, t0:t1_], ["XA"], [XK], XK)
            if lat:
                DMA("sp", rope[:], c_rope[:, :, t0:t1_].rearrange("a d t -> d a t"), [], ["rope"], "rope")
            fm_norm_mod(ph, c, xt, N, A, B, r, hT, [XK], "F_", (sq, psn, rstd), tmp, hkey=HK)
            for j in range(8):
                p_, pk = nextp()
                for kc in range(KC):
                    MM(p_[:, 0:N], wmi[:, kc, j * 128:(j + 1) * 128], hT[:, kc, 0:N], kc == 0, kc == KC - 1, ["wmi", HK], [pk])
                dst = (cq if j < 4 else ckv)[:, j % 4, 0:N]
                ACT(dst, p_[:, 0:N], AF.Copy, [pk], ["cq" if j < 4 else "ckv"])

            def roped(praw, prot, pkr, pkt, out_bf, outkey):
                if lat:
                    TT("dve", t1[:, 0:N], praw, rope[:, 0, 0:N], ALU.mult, [pkr, "rope"], ["t1"])
                    TT("dve", t2[:, 0:N], prot, rope[:, 1, 0:N], ALU.mult, [pkt, "rope"], ["t2"])
                    TT("pool", out_bf, t1[:, 0:N], t2[:, 0:N], ALU.add, ["t1", "t2"], [outkey])
                else:
                    CP("dve", out_bf, praw, [pkr], [outkey])

            a_ = pri[0] % 2; pri[0] += 1
            for kc in range(KC):
                MM(pr_[a_][0:64, 0:N], wmi[:, kc, 1024:1088], hT[:, kc, 0:N], kc == 0, kc == KC - 1, ["wmi", HK], ["pr%d" % a_])
            for kc in range(KC):
                MM(pr_[a_][0:64, 256:256 + N], wmr[:, kc, :], hT[:, kc, 0:N], kc == 0, kc == KC - 1, ["wmr", HK], ["pr%d" % a_])
            sb_, sk = nexts()
            roped(pr_[a_][0:64, 0:N], pr_[a_][0:64, 256:256 + N], "pr%d" % a_, "pr%d" % a_, sb_[0:64, 0:N], sk)
            DMA("sp", KPT[:, PAST + t0:PAST + t1_], sb_[0:64, 0:N], [sk], ["KPT"], sk)
            if not lat:
                CP("dve", kraw[:, 0:N], pr_[a_][0:64, 0:N], ["pr%d" % a_], ["kraw"])
                for s in range(N // 128):
                    b = s % 2
                    TR(ptr[b][:, 0:64], kraw[:, s * 128:(s + 1) * 128], c["id"][0:64, 0:64], ["kraw", "c_id"], ["ptr%d" % b])
                    CP("dve", tok[b][:, 0:64], ptr[b][:, 0:64], ["ptr%d" % b], ["tok%d" % b])
                    DMA("sp", o_kpe[t0 - TS + s * 128:t0 - TS + (s + 1) * 128, :], tok[b][:, 0:64], ["tok%d" % b], ["o_kpe"], "tok%d" % b)
            fm_rstd(ph, c, ckv, 4, N, 512, ["ckv"], "F_", (sq, psn, rstd))
            for kc in range(4):
                STT("dve", ckv[:, kc, 0:N], ckv[:, kc, 0:N], kg_[:, kc:kc + 1], rstd[:, 0:N], ALU.mult, ALU.mult, ["ckv", "kvng", "F_rstd"], ["ckv"])
            CP("pool", ckn[:, :, 0:N], ckv[:, :, 0:N], ["ckv"], ["ckn"])
            if not lat:
                for s in range(N // 128):
                    b = s % 2
                    for k4 in range(4):
                        TR(ptr[b][:, k4 * 128:(k4 + 1) * 128], ckv[:, k4, s * 128:(s + 1) * 128], c["id"][:], ["ckv", "c_id"], ["ptr%d" % b])
                    CP("dve", tok[b][:], ptr[b][:], ["ptr%d" % b], ["tok%d" % b])
                    DMA("sp", o_ckv[t0 - TS + s * 128:t0 - TS + (s + 1) * 128, :], tok[b][:], ["tok%d" % b], ["o_ckv"], "tok%d" % b)
            expand_kv(N, PAST + t0)
            if lat and t0 >= LQE:
                continue
            fm_rstd(ph, c, cq, 4, N, 512, ["cq"], "F_", (sq, psn, rstd))
            for kc in range(4):
                STT("dve", cqn[:, kc, 0:N], cq[:, kc, 0:N], qg_[:, kc:kc + 1], rstd[:, 0:N], ALU.mult, ALU.mult, ["cq", "qng", "F_rstd"], ["cqn"])
            for h in range(16):
                p_, pk = nextp()
                for kc in range(4):
                    MM(p_[:, 0:N], wuq[:, kc, h * 192:h * 192 + 128], cqn[:, kc, 0:N], kc == 0, kc == 3, ["wuq", "cqn"], [pk])
                sb_, sk = nexts()
                ACT(sb_[:, 0:N], p_[:, 0:N], AF.Copy, [pk], [sk])
                DMA("sp", QNT[h][:, t0:t1_], sb_[:, 0:N], [sk], ["QNT"], sk)
                a_ = pri[0] % 2; pri[0] += 1
                for kc in range(4):
                    MM(pr_[a_][0:64, 0:N], wuq[:, kc, h * 192 + 128:h * 192 + 192], cqn[:, kc, 0:N], kc == 0, kc == 3, ["wuq", "cqn"], ["pr%d" % a_])
                for kc in range(4):
                    MM(pr_[a_][0:64, 256:256 + N], wur[:, kc, h, :], cqn[:, kc, 0:N], kc == 0, kc == 3, ["wur", "cqn"], ["pr%d" % a_])
                sb_, sk = nexts()
                roped(pr_[a_][0:64, 0:N], pr_[a_][0:64, 256:256 + N], "pr%d" % a_, "pr%d" % a_, sb_[0:64, 0:N], sk)
                DMA("sp", QPT[h][:, t0:t1_], sb_[0:64, 0:N], [sk], ["QPT"], sk)
        ph.end()

    def phase_G():
        ph = Phase()
        c = consts(ph)
        NKM = PAST + TS
        kn = [ph.sb("kn", [128, NKM], BF16) for _ in range(2)]
        vv = [ph.sb("vv", [128, NKM // 128, 128], BF16) for _ in range(2)]
        kp = ph.sb("kp", [64, NKM], BF16)
        qn = [ph.sb("qn", [128, 512], BF16) for _ in range(2)]; qp = [ph.sb("qp", [64, 512], BF16) for _ in range(2)]
        pT = [ph.sb("pT", [128, 512], BF16) for _ in range(3)]
        rec = ph.sb("rec", [128, 512]); ao = [ph.sb("ao", [128, 512], BF16) for _ in range(2)]

        pss = [ph.ps("pss") for _ in range(3)]; po = [ph.ps("po") for _ in range(2)]; pd = [ph.ps("pd") for _ in range(2)]
        steps = []
        hi_ = 0; qi = 0
        for (s0, L, kind) in seqs:
            if kind == "s":
                k0, NK = 0, PAST + TS
            else:
                k0, NK = PAST + s0, L
            NKT = NK // 128
            first_seq = True
            for h in range(16):
                hb = hi_ % 2; hi_ += 1
                first_h = True
                for (s_, e_) in split(LQE if kind == "s" else L, 512):
                    qb = qi % 2; qi += 1
                    for kt in range(NKT):
                        steps.append(dict(s0=s0, k0=k0, NK=NK, NKT=NKT, h=h, hb=hb, qb=qb, s=s_, e=e_, kt=kt,
                                          ld_seq=first_seq and kt == 0, ld_h=first_h and kt == 0, ld_q=kt == 0))
                        first_seq = False; first_h = False

        def emit_S(i, st):
            sb = i % 3
            SB = "%d" % sb; HB = "%d" % st["hb"]; QB = "%d" % st["qb"]
            hb, qb, h, k0, NK, NKT, s0 = st["hb"], st["qb"], st["h"], st["k0"], st["NK"], st["NKT"], st["s0"]
            n = st["e"] - st["s"]
            if st["ld_seq"]:
                DMA("sp", kp[:, 0:NK], KPT[:, k0:k0 + NK], ["KPT"], ["kp"], "kp")
            if st["ld_h"]:
                DMA("sp", kn[hb][:, 0:NK], KNT[h][:, k0:k0 + NK], ["KNT"], ["kn" + HB], "kn" + HB)
                for kt0 in range(0, NKT, 8):
                    kt1 = min(NKT, kt0 + 8)
                    DMA("act", vv[hb][:, kt0:kt1, :], VV[h][k0 + kt0 * 128:k0 + kt1 * 128, :].rearrange("(t p) e -> p t e", p=128),
                        ["VV", "vv" + HB], ["vv" + HB], "vv" + HB)
            if st["ld_q"]:
                DMA("sp", qn[qb][:, 0:n], QNT[h][:, s0 + st["s"]:s0 + st["e"]], ["QNT"], ["qn" + QB], "qn" + QB)
                DMA("sp", qp[qb][:, 0:n], QPT[h][:, s0 + st["s"]:s0 + st["e"]], ["QPT"], ["qp" + QB], "qp" + QB)
            ks = slice(st["kt"] * 128, (st["kt"] + 1) * 128)
            MM(pss[sb][:, 0:n], kn[hb][:, ks], qn[qb][:, 0:n], True, False, ["kn" + HB, "qn" + QB], ["pss" + SB])
            MM(pss[sb][:, 0:n], kp[:, ks], qp[qb][:, 0:n], False, True, ["kp", "qp" + QB], ["pss" + SB])
            ACT(pT[sb][:, 0:n], pss[sb][:, 0:n], AF.Exp, ["pss" + SB], ["pT" + SB], scale=MLA_SCALE)

        def emit_PV(i, st):
            sb = i % 3
            SB = "%d" % sb; HB = "%d" % st["hb"]; QB = "%d" % st["qb"]
            hb, qb, h, NKT, s0, kt = st["hb"], st["qb"], st["h"], st["NKT"], st["s0"], st["kt"]
            n = st["e"] - st["s"]
            MM(po[qb][:, 0:n], vv[hb][:, kt, :], pT[sb][:, 0:n], kt == 0, kt == NKT - 1, ["vv" + HB, "pT" + SB], ["po" + QB])
            MM(pd[qb][:, 0:n], c["ones"][:], pT[sb][:, 0:n], kt == 0, kt == NKT - 1, ["c_ones", "pT" + SB], ["pd" + QB])
            if kt == NKT - 1:
                P.op("dve", lambda e, qb=qb, n=n: e.reciprocal(out=rec[:, 0:n], in_=pd[qb][:, 0:n]), ["pd" + QB], ["rec"])
                TT("dve", ao[qb][:, 0:n], po[qb][:, 0:n], rec[:, 0:n], ALU.mult, ["po" + QB, "rec"], ["ao" + QB])
                DMA("sp", AOT[h * 128:(h + 1) * 128, s0 + st["s"]:s0 + st["e"]], ao[qb][:, 0:n], ["ao" + QB], ["AOT"], "ao" + QB)

        emit_S(0, steps[0])
        for i in range(len(steps)):
            if i + 1 < len(steps):
                emit_S(i + 1, steps[i + 1])
            emit_PV(i, steps[i])
        ph.end()

    def phase_H():
        ph = Phase()
        m = load_mod(ph, 1, (2,))
        aT = ph.sb("aT", [128, KC, 512], BF16); x0 = ph.sb("x0", [128, KC, 512])
        wsl = [ph.sb("wsl", [128, KC, 512], BF16) for _ in range(2)]
        pq = [ph.ps("pq") for _ in range(2)]
        Wv = Wmo.rearrange("(k p) c -> p k c", p=128)
        XAv = XA.rearrange("(k p) t -> p k t", p=128); XBv = XB.rearrange("(k p) t -> p k t", p=128)
        wi = 0
        hr = [(a_, b_) for (a_, b_) in split(LQE, 512)] + [(t, min(t + 512, T0)) for t in range(TS, T0, 512)]
        for (t0, t1h) in hr:
            r = row_of(t0); n = t1h - t0
            DMA("act", x0[:, :, 0:n], XAv[:, :, t0:t1h], ["XA"], ["x0"], "x0")
            DMA("sp", aT[:, :, 0:n], AOT.rearrange("(k p) t -> p k t", p=128)[:, :, t0:t1h], ["AOT"], ["aT"], "aT")
            for f4 in range(4):
                wb = wi % 2; wi += 1
                DMA("sp", wsl[wb][:], Wv[:, :, f4 * 512:(f4 + 1) * 512], ["wcast"], ["wsl%d" % wb], "wsl%d" % wb)
                for f1 in range(4):
                    fc = f4 * 4 + f1
                    pb = fc % 2
                    for kc in range(KC):
                        MM(pq[pb][:, 0:n], wsl[wb][:, kc, f1 * 128:(f1 + 1) * 128], aT[:, kc, 0:n], kc == 0, kc == KC - 1, ["wsl%d" % wb, "aT"], ["pq%d" % pb])
                    STT("dve", x0[:, fc, 0:n], pq[pb][:, 0:n], m[2][:, r, fc:fc + 1], x0[:, fc, 0:n], ALU.mult, ALU.add, ["pq%d" % pb, "modT2", "x0"], ["x0"])
            DMA("act", XBv[:, :, t0:t1h], x0[:, :, 0:n], ["x0"], ["XB"], "x0s")
        ph.end()

    def phase_Z():
        ph = Phase()
        c = consts(ph)
        g = load_vecT(ph, fng, KC, "fng")
        xt = ph.sb("xt", [128, KC, 512]); sq = [ph.sb("sq", [128, 512], BF16) for _ in range(2)]; rstd = ph.sb("rstd", [128, 512])
        yt = [ph.sb("yt", [128, 2048]) for _ in range(2)]
        psn = ph.ps("psn"); pt = [ph.ps("pt") for _ in range(4)]
        XAv = XA.rearrange("(k p) t -> p k t", p=128)
        yi = 0
        for t0 in list(range(0, LQ, 512)) + list(range(TS, T0, 512)):
            DMA("act", xt[:], XAv[:, :, t0:t0 + 512], ["XA"], ["Z_xt"], "Z_xt")
            fm_rstd(ph, c, xt, KC, 512, D, ["Z_xt"], "Z_", (sq, psn, rstd))
            for kc in range(KC):
                STT("dve", xt[:, kc, :], xt[:, kc, :], g[:, kc:kc + 1], rstd[:], ALU.mult, ALU.mult, ["Z_xt", "fng", "Z_rstd"], ["Z_xt"])
            for s in range(4):
                b = yi % 2; yi += 1
                for q in range(4):
                    for k4 in range(4):
                        kc = q * 4 + k4
                        TR(pt[q][:, k4 * 128:(k4 + 1) * 128], xt[:, kc, s * 128:(s + 1) * 128], c["id"][:], ["Z_xt", "c_id"], ["pt%d" % q])
                    if q % 2 == 0:
                        CP("dve", yt[b][:, q * 512:(q + 1) * 512], pt[q][:], ["pt%d" % q], ["yt%d" % b])
                    else:
                        ACT(yt[b][:, q * 512:(q + 1) * 512], pt[q][:], AF.Copy, ["pt%d" % q], ["yt%d" % b])
                tt = t0 + s * 128
                dst = y_s[tt:tt + 128, :] if tt < TS else y_p[tt - TS:tt - TS + 128, :]
                DMA("sp", dst, yt[b][:], ["yt%d" % b], ["y"], "yt%d" % b)
        ph.end()

    L0_ranges = [(s0, L, 0 if k == "s" else 1, 0) for (s0, L, k) in seqs]
    L1_ranges = [((s0, LQ, 0, EXT) if k == "s" else (s0, L, 1, 0)) for (s0, L, k) in seqs]
    phases = [("T", phase_WMT), ("A", phase_A), ("B", lambda: phase_scan(0)), ("C", lambda: phase_scan(1)),
              ("D", phase_D), ("E", lambda: phase_E(0, XB, XA, L0_ranges, T0)),
              ("F", phase_F), ("G", phase_G), ("H", phase_H), ("I", lambda: phase_E(1, XB, XA, L1_ranges, T0)), ("Z", phase_Z)]
    for nm, fn in phases:
        fn()
        if nm == stop_after:
            break
    if dump:
        ph = Phase()
        t = ph.sb("dd", [128, KC, 512])
        XD = XA if dump == "XA" else XB
        for t0 in range(0, T0, 512):
            DMA("sp", t[:], XD.rearrange("(k p) t -> p k t", p=128)[:, :, t0:t0 + 512], ["XA"], ["dd"], "dd")
            DMA("sp", dbg.rearrange("(k p) t -> p k t", p=128)[:, :, t0:t0 + 512], t[:], ["dd"], ["dbg"], "dd2")
        ph.end()
    st.close()
    return nc, P


def host_consts(TS, mir=False, CH=128):
    c = {}
    c["c_ident"] = np.eye(128, dtype=np.float32)
    j = np.arange(128)[:, None]; i = np.arange(128)[None, :]
    same = (j // CH) == (i // CH)
    tri = np.zeros((4, 128, 128), np.float32)
    tri[0] = np.where(same & (j <= i), -1 / 16.0, 0.0)
    tri[1] = np.where(same & (j >= i), -1 / 16.0, 0.0)
    tri[2] = np.where(same & (j > i), -1 / 16.0, 0.0)
    tri[3] = np.where(same & (j < i), -1 / 16.0, 0.0)
    c["c_tri"] = tri
    jj = np.arange(CH)[:, None]; ii = np.arange(CH)[None, :]
    m = np.zeros((2, CH, 8 * CH), np.float32)
    m[0] = np.tile((ii >= jj).astype(np.float32), (1, 8))
    m[1] = np.tile((ii <= jj).astype(np.float32), (1, 8))
    c["c_mask"] = m
    p = np.arange(128)
    pos = np.zeros((128, 4), np.float32)
    pos[:, 0] = (p % CH) + 1
    pos[:, 1] = CH - (p % CH)
    c["c_pos"] = pos
    t = np.arange(TS)
    if mir:
        t = TS - 1 - t
    inv = 10000.0 ** (-np.arange(16, dtype=np.float64) * 2.0 / 32)
    d = np.arange(64)
    posd = np.where((d // 32)[:, None] == 0, (t // 64)[None, :], (t % 64)[None, :]).astype(np.float64)
    ang = posd * inv[d % 16][:, None]
    c["c_rope"] = np.stack([np.cos(ang), np.sin(ang)], 0).astype(np.float32)
    return c


def prep_core(inp, cid, cfg):
    TS, TP = cfg["TS"], cfg["TP"]
    mir = bool(cfg.get("mirror", False)) and (cid % 2 == 1)
    b = cid // 2
    p0 = 2 * cid
    f = lambda a: np.ascontiguousarray(np.asarray(a, dtype=np.float32))
    m = {}
    if mir:
        m["xs"] = f(inp["x_sample"][b][::-1])
        m["xp"] = f(inp["x_prompt"][p0:p0 + 2][:, ::-1].reshape(2 * TP, D))
        m["sg"] = f(inp["state_gla"][b, 0][::-1]); m["sr"] = f(inp["state_ret"][b, 0][::-1])
    else:
        m["xs"] = f(inp["x_sample"][b])
        m["xp"] = f(inp["x_prompt"][p0:p0 + 2].reshape(2 * TP, D))
        m["sg"] = f(inp["state_gla"][b, 0]); m["sr"] = f(inp["state_ret"][b, 0])
    m["cckv"] = f(inp["cache_ckv"][b, 0]); m["ckpe"] = f(inp["cache_kpe"][b, 0])
    m["cond"] = f(np.stack([inp["c"][b], inp["c_ctx"]], 0))
    for k in ("mod_w", "mod_b", "norm1_g", "norm2_g", "ffn_w_in", "ffn_conv", "ffn_w_out", "final_norm_g"):
        m[k] = f(inp[k])
    for k in ("ab_w_in", "gla_gate_w2", "gla_gate_b", "gla_norm_g", "ret_norm_g", "ab_w_out", "mla_w_in", "mla_q_norm_g",
              "mla_w_uq", "mla_kv_norm_g", "mla_w_ukv", "mla_w_out"):
        m[k] = f(inp[k][0])
    m["ret_decay"] = f(inp["ret_decay"][0].reshape(8))
    if mir:
        w = np.array(m["ab_w_in"])
        w[:, C_GLR:C_GLR + 16] = m["ab_w_in"][:, C_GLR + 16:C_GLR + 32]
        w[:, C_GLR + 16:C_GLR + 32] = m["ab_w_in"][:, C_GLR:C_GLR + 16]
        m["ab_w_in"] = w
        m["gla_gate_w2"] = f(m["gla_gate_w2"][::-1]); m["gla_gate_b"] = f(m["gla_gate_b"][::-1])
        m["ret_decay"] = f(inp["ret_decay"][0][::-1].reshape(8))
        m["ffn_conv"] = f(m["ffn_conv"][:, ::-1, :])
    m.update(host_consts(TS, mir, cfg.get("CH", 128)))
    return m


_CACHE = {}


def kernel(**inputs):
    inputs = {k: np.asarray(v) for k, v in inputs.items()}
    TS = inputs["x_sample"].shape[1]
    TP = inputs["x_prompt"].shape[1]
    PAST = inputs["cache_ckv"].shape[2]
    NB = inputs["x_sample"].shape[0]
    BP = inputs["x_prompt"].shape[0]
    n = 8
    cfg = dict(TS=TS, TP=TP, PAST=PAST, LQ=TS // 2, mirror=True)
    key = (TS, TP, PAST)
    if key not in _CACHE:
        _CACHE[key] = build(cfg)
    nc, _ = _CACHE[key]
    in_maps = [prep_core(inputs, cid, cfg) for cid in range(n)]
    res = run_bass_kernel_spmd(nc, in_maps, core_ids=list(range(n)))
    R_ = res.results
    y_prompt = np.zeros((BP, TP, D), np.float32)
    y_sample = np.zeros((NB, TS, D), np.float32)
    ns_gla = np.zeros((BP, 1, 2, 4, 128, 256), np.float32)
    ns_ret = np.zeros((BP, 1, 2, 4, 128, 256), np.float32)
    n_ckv = np.zeros((BP, 1, TP, 512), np.float32)
    n_kpe = np.zeros((BP, 1, TP, 64), np.float32)
    for cid in range(n):
        r = R_[cid]
        b = cid // 2
        half = TS // 2
        p0 = 2 * cid
        if cid % 2 == 0:
            y_sample[b, :half] = r["y_s"][:half]
            y_prompt[p0:p0 + 2] = r["y_p"].reshape(2, TP, D)
            ns_gla[p0:p0 + 2, 0] = r["ns_gla"]
            ns_ret[p0:p0 + 2, 0] = r["ns_ret"]
            n_ckv[p0:p0 + 2, 0] = r["n_ckv"].reshape(2, TP, 512)
            n_kpe[p0:p0 + 2, 0] = r["n_kpe"].reshape(2, TP, 64)
        else:
            y_sample[b, half:] = r["y_s"][:half][::-1]
            y_prompt[p0:p0 + 2] = r["y_p"].reshape(2, TP, D)[:, ::-1]
            ns_gla[p0:p0 + 2, 0] = r["ns_gla"][:, ::-1]
            ns_ret[p0:p0 + 2, 0] = r["ns_ret"][:, ::-1]
            n_ckv[p0:p0 + 2, 0] = r["n_ckv"].reshape(2, TP, 512)[:, ::-1]
            n_kpe[p0:p0 + 2, 0] = r["n_kpe"].reshape(2, TP, 64)[:, ::-1]
    return (y_prompt, y_sample, ns_gla, ns_ret, n_ckv, n_kpe)
```
